# Optimizing a Trainium2 kernel written in Bass

```python
import math
import jax, jax.numpy as jnp
from jax import lax
import numpy as np

D_MODEL = 1024
BATCH = 32
SEQ = 2048
DEPTH = 1

HEAD_DIM = 64
DIFF_HEADS = 4
DIFF_WIDTH = DIFF_HEADS * 2 * HEAD_DIM
NSA_HEADS = 8
NSA_KV_HEADS = 2
NSA_GROUP = NSA_HEADS // NSA_KV_HEADS
NSA_WIDTH = NSA_HEADS * HEAD_DIM
NSA_KV_WIDTH = NSA_KV_HEADS * HEAD_DIM
CMP_BLOCK = 32
CMP_STRIDE = 16
CMP_HIDDEN = 128
SEL_BLOCK = 64
SEL_TOPN = 16
WINDOW = 512
N_GATES = 3 * NSA_HEADS
Q_BLOCK = 128
SEL_Q_BLOCK = 32
MIX_WIDTH = DIFF_WIDTH + NSA_WIDTH
D_FF = -(-8 * D_MODEL // (3 * 256)) * 256
N_ALIBI = DIFF_HEADS + NSA_HEADS
LN_EPS = 1e-5
RMS_EPS = 1e-5
NEG_INF = -1e30
DEEPNORM_ALPHA = (2.0 * DEPTH) ** 0.25
DEEPNORM_BETA = (8.0 * DEPTH) ** -0.25

OFF_DQ = 0
OFF_DK = OFF_DQ + DIFF_WIDTH
OFF_DV = OFF_DK + DIFF_WIDTH
OFF_NQ = OFF_DV + DIFF_WIDTH
OFF_CK = OFF_NQ + NSA_WIDTH
OFF_CV = OFF_CK + NSA_KV_WIDTH
OFF_SK = OFF_CV + NSA_KV_WIDTH
OFF_SV = OFF_SK + NSA_KV_WIDTH
OFF_WK = OFF_SV + NSA_KV_WIDTH
OFF_WV = OFF_WK + NSA_KV_WIDTH
OFF_G = OFF_WV + NSA_KV_WIDTH
N_IN = OFF_G + N_GATES

kernel_name = "hybrid_diffattn_nsa_alibi_deepnorm"


def lambda_init(layer_idx):
    return 0.8 - 0.6 * math.exp(-0.3 * layer_idx)


def alibi_slopes():
    return jnp.asarray(2.0 ** (-8.0 * (np.arange(N_ALIBI) + 1) / N_ALIBI), dtype=jnp.float32)


def layer_norm(x, g, b):
    xf = x.astype(jnp.float32)
    mu = xf.mean(-1, keepdims=True)
    var = jnp.square(xf - mu).mean(-1, keepdims=True)
    return ((xf - mu) * lax.rsqrt(var + LN_EPS) * g.astype(jnp.float32) + b.astype(jnp.float32)).astype(x.dtype)


def rms_norm(x, g):
    xf = x.astype(jnp.float32)
    return (xf * lax.rsqrt(jnp.mean(xf * xf, -1, keepdims=True) + RMS_EPS) * g.astype(jnp.float32)).astype(x.dtype)


def masked_softmax(s, mask):
    s = jnp.where(mask, s.astype(jnp.float32), NEG_INF)
    return jax.nn.softmax(s, axis=-1) * mask


def diff_attention(q, k, v, lq1, lk1, lq2, lk2, subln_g, slopes, lam_init):
    B, T = q.shape[0], q.shape[1]
    dt = v.dtype
    f32 = jnp.float32
    lam = (jnp.exp(jnp.sum(lq1.astype(f32) * lk1.astype(f32)))
           - jnp.exp(jnp.sum(lq2.astype(f32) * lk2.astype(f32))) + lam_init)
    scale = HEAD_DIM ** -0.5
    nq = T // Q_BLOCK
    qb = q.reshape(B, nq, Q_BLOCK, DIFF_HEADS, 2, HEAD_DIM).swapaxes(0, 1)
    kpos = jnp.arange(T)

    def block(args):
        qi, q_blk = args
        qpos = qi * Q_BLOCK + jnp.arange(Q_BLOCK)
        dist = (qpos[:, None] - kpos[None, :]).astype(f32)
        s = jnp.einsum('bqhcd,bkhcd->bhcqk', q_blk, k).astype(f32) * scale
        s = s - slopes[None, :, None, None, None] * dist
        p = masked_softmax(s, dist >= 0)
        a = p[:, :, 0] - lam * p[:, :, 1]
        return jnp.einsum('bhqk,bkhe->bqhe', a.astype(dt), v)

    o = lax.map(block, (jnp.arange(nq), qb))
    o = o.swapaxes(0, 1).reshape(B, T, DIFF_HEADS, 2 * HEAD_DIM)
    o = (rms_norm(o, subln_g).astype(jnp.float32) * (1.0 - lam_init)).astype(dt)
    return o.reshape(B, T, DIFF_WIDTH)


def compress_kv(kv, pe, w1, w2):
    B, T = kv.shape[0], kv.shape[1]
    n_cmp = (T - CMP_BLOCK) // CMP_STRIDE + 1
    idx = jnp.arange(n_cmp)[:, None] * CMP_STRIDE + jnp.arange(CMP_BLOCK)[None, :]
    blk = kv[:, idx] + pe[None, None, :, None, :]
    blk = jnp.moveaxis(blk, 3, 2).reshape(B, n_cmp, NSA_KV_HEADS, CMP_BLOCK * HEAD_DIM)
    return jax.nn.gelu(blk @ w1) @ w2


def nsa_attention(q, ck, cv, sk, sv, wk, wv, gates, pe_k, w1_k, w2_k, pe_v, w1_v, w2_v, slopes):
    B, T = q.shape[0], q.shape[1]
    dt = q.dtype
    f32 = jnp.float32
    G, Hg = NSA_KV_HEADS, NSA_GROUP
    scale = HEAD_DIM ** -0.5
    t = jnp.arange(T)

    kc = compress_kv(ck, pe_k, w1_k, w2_k)
    vc = compress_kv(cv, pe_v, w1_v, w2_v)
    n_cmp = kc.shape[1]
    cmp_start = jnp.arange(n_cmp) * CMP_STRIDE
    mask_c = (cmp_start + CMP_BLOCK - 1)[None, :] <= t[:, None]
    s_c = jnp.einsum('btghd,bngd->btghn', q, kc) * scale
    p_cmp = masked_softmax(s_c, mask_c[None, :, None, None, :])
    o_cmp = jnp.einsum('btghn,bngd->btghd', p_cmp.astype(dt), vc)

    n_sel = T // SEL_BLOCK
    sel_start = jnp.arange(n_sel) * SEL_BLOCK
    overlap = ((cmp_start[:, None] <= sel_start[None, :] + SEL_BLOCK - 1)
               & (cmp_start[:, None] + CMP_BLOCK - 1 >= sel_start[None, :])).astype(f32)
    imp = jnp.einsum('btghn,ns->btgs', p_cmp, overlap)
    cur = t // SEL_BLOCK
    blk = jnp.arange(n_sel)
    forced = (blk[None, :] == 0) | (blk[None, :] == cur[:, None]) | (blk[None, :] == cur[:, None] - 1)
    future = blk[None, :] > cur[:, None]
    imp = jnp.where(forced[None, :, None, :], jnp.inf,
                    jnp.where(future[None, :, None, :], -jnp.inf, imp))
    top_n = min(SEL_TOPN, n_sel)
    _, sel_idx = lax.top_k(imp, top_n)

    ks_blk = sk.reshape(B, n_sel, SEL_BLOCK, G, HEAD_DIM).transpose(0, 3, 1, 2, 4)
    vs_blk = sv.reshape(B, n_sel, SEL_BLOCK, G, HEAD_DIM).transpose(0, 3, 1, 2, 4)
    nqc = T // SEL_Q_BLOCK
    qc = q.reshape(B, nqc, SEL_Q_BLOCK, G, Hg, HEAD_DIM).swapaxes(0, 1)
    ic = sel_idx.reshape(B, nqc, SEL_Q_BLOCK, G, top_n).swapaxes(0, 1)
    b_ix = jnp.arange(B)[:, None, None, None]
    g_ix = jnp.arange(G)[None, None, :, None]
    n_keys = top_n * SEL_BLOCK

    def sel_block(args):
        ci, q_blk, i_blk = args
        kg = ks_blk[b_ix, g_ix, i_blk].reshape(B, SEL_Q_BLOCK, G, n_keys, HEAD_DIM)
        vg = vs_blk[b_ix, g_ix, i_blk].reshape(B, SEL_Q_BLOCK, G, n_keys, HEAD_DIM)
        qpos = ci * SEL_Q_BLOCK + jnp.arange(SEL_Q_BLOCK)
        kpos = (i_blk[..., None] * SEL_BLOCK + jnp.arange(SEL_BLOCK)).reshape(B, SEL_Q_BLOCK, G, n_keys)
        dist = (qpos[None, :, None, None] - kpos).astype(f32)
        s = jnp.einsum('bqghd,bqgkd->bqghk', q_blk, kg).astype(f32) * scale
        s = s - slopes[None, None, :, :, None] * dist[:, :, :, None, :]
        p = masked_softmax(s, (dist >= 0)[:, :, :, None, :])
        return jnp.einsum('bqghk,bqgkd->bqghd', p.astype(dt), vg)

    o_sel = lax.map(sel_block, (jnp.arange(nqc), qc, ic))
    o_sel = o_sel.swapaxes(0, 1).reshape(B, T, G, Hg, HEAD_DIM)

    kwp = jnp.pad(wk, ((0, 0), (WINDOW, 0), (0, 0), (0, 0)))
    vwp = jnp.pad(wv, ((0, 0), (WINDOW, 0), (0, 0), (0, 0)))
    nq = T // Q_BLOCK
    qb = q.reshape(B, nq, Q_BLOCK, G, Hg, HEAD_DIM).swapaxes(0, 1)
    span = WINDOW + Q_BLOCK

    def win_block(args):
        qi, q_blk = args
        start = qi * Q_BLOCK
        kblk = lax.dynamic_slice_in_dim(kwp, start, span, axis=1)
        vblk = lax.dynamic_slice_in_dim(vwp, start, span, axis=1)
        qpos = start + jnp.arange(Q_BLOCK)
        kpos = start - WINDOW + jnp.arange(span)
        dist_i = qpos[:, None] - kpos[None, :]
        mask = (dist_i >= 0) & (dist_i < WINDOW) & (kpos >= 0)[None, :]
        dist = dist_i.astype(f32)
        s = jnp.einsum('bqghd,bkgd->bqghk', q_blk, kblk).astype(f32) * scale
        s = s - slopes[None, None, :, :, None] * dist[None, :, None, None, :]
        p = masked_softmax(s, mask[None, :, None, None, :])
        return jnp.einsum('bqghk,bkgd->bqghd', p.astype(dt), vblk)

    o_win = lax.map(win_block, (jnp.arange(nq), qb))
    o_win = o_win.swapaxes(0, 1).reshape(B, T, G, Hg, HEAD_DIM)

    o = gates[..., 0:1] * o_cmp + gates[..., 1:2] * o_sel + gates[..., 2:3] * o_win
    return o.reshape(B, T, NSA_WIDTH)


def setup_inputs(seed: int = 0) -> dict:
    key = jax.random.key(seed)
    ks = jax.random.split(key, 24)
    f32 = jnp.float32
    nrm = lambda k, shape, s: jax.random.normal(k, shape, f32) * s
    L = DEPTH
    col_scale = np.ones((N_IN,), np.float32)
    col_scale[OFF_DV:OFF_DV + DIFF_WIDTH] = DEEPNORM_BETA
    for off in (OFF_CV, OFF_SV, OFF_WV):
        col_scale[off:off + NSA_KV_WIDTH] = DEEPNORM_BETA
    w_in = nrm(ks[1], (L, D_MODEL, N_IN), D_MODEL ** -0.5) * jnp.asarray(col_scale)
    return {
        "x": nrm(ks[0], (BATCH, SEQ, D_MODEL), 1.0),
        "w_in": w_in,
        "diff_lq1": nrm(ks[2], (L, HEAD_DIM), 0.1),
        "diff_lk1": nrm(ks[3], (L, HEAD_DIM), 0.1),
        "diff_lq2": nrm(ks[4], (L, HEAD_DIM), 0.1),
        "diff_lk2": nrm(ks[5], (L, HEAD_DIM), 0.1),
        "diff_subln_g": 1.0 + nrm(ks[6], (L, 2 * HEAD_DIM), 0.02),
        "cmp_pe_k": nrm(ks[7], (L, CMP_BLOCK, HEAD_DIM), 0.02),
        "cmp_w1_k": nrm(ks[8], (L, CMP_BLOCK * HEAD_DIM, CMP_HIDDEN), (CMP_BLOCK * HEAD_DIM) ** -0.5),
        "cmp_w2_k": nrm(ks[9], (L, CMP_HIDDEN, HEAD_DIM), CMP_HIDDEN ** -0.5),
        "cmp_pe_v": nrm(ks[10], (L, CMP_BLOCK, HEAD_DIM), 0.02),
        "cmp_w1_v": nrm(ks[11], (L, CMP_BLOCK * HEAD_DIM, CMP_HIDDEN), (CMP_BLOCK * HEAD_DIM) ** -0.5),
        "cmp_w2_v": nrm(ks[12], (L, CMP_HIDDEN, HEAD_DIM), CMP_HIDDEN ** -0.5),
        "w_out": nrm(ks[13], (L, MIX_WIDTH, D_MODEL), MIX_WIDTH ** -0.5 * DEEPNORM_BETA),
        "ln1_g": 1.0 + nrm(ks[14], (L, D_MODEL), 0.02),
        "ln1_b": nrm(ks[15], (L, D_MODEL), 0.02),
        "w_gate": nrm(ks[16], (L, D_MODEL, D_FF), D_MODEL ** -0.5),
        "w_up": nrm(ks[17], (L, D_MODEL, D_FF), D_MODEL ** -0.5),
        "w_down": nrm(ks[18], (L, D_FF, D_MODEL), D_FF ** -0.5 * DEEPNORM_BETA),
        "ln2_g": 1.0 + nrm(ks[19], (L, D_MODEL), 0.02),
        "ln2_b": nrm(ks[20], (L, D_MODEL), 0.02),
    }


def reference(x, w_in, diff_lq1, diff_lk1, diff_lq2, diff_lk2, diff_subln_g,
              cmp_pe_k, cmp_w1_k, cmp_w2_k, cmp_pe_v, cmp_w1_v, cmp_w2_v,
              w_out, ln1_g, ln1_b, w_gate, w_up, w_down, ln2_g, ln2_b):
    B, T, _ = x.shape
    slopes = alibi_slopes()
    diff_slopes = slopes[:DIFF_HEADS]
    nsa_slopes = slopes[DIFF_HEADS:].reshape(NSA_KV_HEADS, NSA_GROUP)
    for l in range(DEPTH):
        h = x @ w_in[l]
        dq = h[..., OFF_DQ:OFF_DK].reshape(B, T, DIFF_HEADS, 2, HEAD_DIM)
        dk = h[..., OFF_DK:OFF_DV].reshape(B, T, DIFF_HEADS, 2, HEAD_DIM)
        dv = h[..., OFF_DV:OFF_NQ].reshape(B, T, DIFF_HEADS, 2 * HEAD_DIM)
        nq = h[..., OFF_NQ:OFF_CK].reshape(B, T, NSA_KV_HEADS, NSA_GROUP, HEAD_DIM)
        kv = lambda o: h[..., o:o + NSA_KV_WIDTH].reshape(B, T, NSA_KV_HEADS, HEAD_DIM)
        gates = jax.nn.sigmoid(h[..., OFF_G:N_IN].reshape(B, T, NSA_KV_HEADS, NSA_GROUP, 3))
        o_diff = diff_attention(dq, dk, dv, diff_lq1[l], diff_lk1[l], diff_lq2[l], diff_lk2[l],
                                diff_subln_g[l], diff_slopes, lambda_init(l))
        o_nsa = nsa_attention(nq, kv(OFF_CK), kv(OFF_CV), kv(OFF_SK), kv(OFF_SV), kv(OFF_WK), kv(OFF_WV),
                              gates, cmp_pe_k[l], cmp_w1_k[l], cmp_w2_k[l],
                              cmp_pe_v[l], cmp_w1_v[l], cmp_w2_v[l], nsa_slopes)
        mix = jnp.concatenate([o_diff, o_nsa], axis=-1) @ w_out[l]
        x = layer_norm(DEEPNORM_ALPHA * x + mix, ln1_g[l], ln1_b[l])
        ffn = (jax.nn.silu(x @ w_gate[l]) * (x @ w_up[l])) @ w_down[l]
        x = layer_norm(DEEPNORM_ALPHA * x + ffn, ln2_g[l], ln2_b[l])
    return x
```

```python
import numpy as np
import ml_dtypes
from contextlib import ExitStack
import concourse.bass as bass
import concourse.mybir as mybir
from concourse.bass_utils import run_bass_kernel_spmd

F32 = mybir.dt.float32
BF16 = mybir.dt.bfloat16
AF = mybir.ActivationFunctionType
ALU = mybir.AluOpType
AX = mybir.AxisListType

T = 2048
DM = 1024
NT = 16
DFF = 2816
NJ = 22
NEG = -30000.0
ALPHA = 2.0 ** 0.25
LAM_INIT = 0.2
DBG_BR = [1, 1, 1]
SLOPES = (2.0 ** (-8.0 * (np.arange(12) + 1) / 12)).astype(np.float32)

OFF_DQ, OFF_DK, OFF_DV, OFF_NQ = 0, 512, 1024, 1536
OFF_CK, OFF_CV, OFF_SK, OFF_SV, OFF_WK, OFF_WV, OFF_G = 2048, 2176, 2304, 2432, 2560, 2688, 2816
CH = []
for h in range(4):
    CH.append(([(OFF_DQ + h * 128, 128)], 'F', 0.125))
for h in range(4):
    CH.append(([(OFF_DK + h * 128, 128)], 'F', 1.0))
for j in range(4):
    CH.append(([(OFF_NQ + j * 64, 64), (OFF_NQ + (j + 4) * 64, 64)], 'F', 0.125))
CH.append(([(OFF_CK, 128)], 'F', 1.0))
CH.append(([(OFF_CV, 128)], 'F', 1.0))
CH.append(([(OFF_SK, 128)], 'F', 1.0))
CH.append(([(OFF_WK, 128)], 'F', 1.0))
for h in range(4):
    CH.append(([(OFF_DV + h * 128, 128)], 'T', 1.0))
CH.append(([(OFF_SV, 128)], 'T', 1.0))
CH.append(([(OFF_WV, 128)], 'T', 1.0))
CH.append(([(OFF_G, 24)], 'T', 1.0))
NCH = len(CH)
F_DQ, F_DK, F_NQ, F_CK, F_CV, F_SK, F_WK = 0, 4, 8, 12, 13, 14, 15

C_ID = 0
C_TLO = 128
C_THI = 256
C_MCMP = 384
C_E = C_MCMP + 2048
C_ALA = C_E + 2048
C_ALB = C_ALA + 12 * 128
C_OVL = C_ALB + 512
NC16 = C_OVL + 32
C_FB = 0
C_BT = 512
NC32 = C_BT + 192
P_LQ = 0
P_SUBG = 256
P_PEK = 384
P_PEV = 416
NPV = 448


def _split3(v):
    v = np.asarray(v, np.float64)
    a = v.astype(np.float32).astype(ml_dtypes.bfloat16)
    r = v - a.astype(np.float64)
    b = r.astype(np.float32).astype(ml_dtypes.bfloat16)
    r = r - b.astype(np.float64)
    c = r.astype(np.float32).astype(ml_dtypes.bfloat16)
    return a, b, c


def make_consts():
    cb = np.zeros((128, NC16), np.float32)
    cb[:, C_ID:C_ID + 128] = np.eye(128)
    r = np.arange(128)[:, None]
    j = np.arange(128)[None, :]
    cb[:, C_TLO:C_TLO + 128] = np.where(j < r, NEG, 0.0)
    cb[:, C_THI:C_THI + 128] = np.where(j >= r, NEG, 0.0)
    n = np.arange(127)[:, None]
    t = np.arange(T)[None, :]
    cb[:127, C_MCMP:C_MCMP + T] = np.where(t < 16 * n + 31, NEG, 0.0)
    s = np.arange(32)[:, None]
    m = np.arange(T)[None, :]
    cb[:32, C_E:C_E + T] = (m // 64 == s).astype(np.float32)
    cb16 = cb.astype(ml_dtypes.bfloat16)
    jj = np.arange(512)
    for si in range(12):
        sv = float(SLOPES[si])
        p = _split3(np.full(128, sv))
        q = _split3(sv * np.arange(128, dtype=np.float64))
        for k in range(3):
            cb16[k, C_ALA + si * 128:C_ALA + (si + 1) * 128] = p[k]
            cb16[3 + k, C_ALA + si * 128:C_ALA + (si + 1) * 128] = p[k]
            cb16[6 + k, C_ALA + si * 128:C_ALA + (si + 1) * 128] = q[k]
    for k in range(3):
        cb16[k, C_ALB:C_ALB + 512] = (-(jj // 2) * 2).astype(np.float32).astype(ml_dtypes.bfloat16)
        cb16[3 + k, C_ALB:C_ALB + 512] = (-(jj % 2)).astype(np.float32).astype(ml_dtypes.bfloat16)
        cb16[6 + k, C_ALB:C_ALB + 512] = np.ones(512, np.float32).astype(ml_dtypes.bfloat16)
    cs = np.arange(127)[:, None] * 16
    ss = np.arange(32)[None, :] * 64
    ovl = ((cs <= ss + 63) & (cs + 31 >= ss)).astype(np.float32)
    cb16[:127, C_OVL:C_OVL + 32] = ovl.astype(ml_dtypes.bfloat16)

    cf = np.zeros((128, NC32), np.float32)
    p = np.arange(128)
    for qt in range(16):
        cur = 2 * qt + p // 64
        for sb in range(32):
            v = np.zeros(128, np.float32)
            if sb == 0:
                v[:] = 1000.0
            v[sb == cur - 1] = 3000.0
            v[sb == cur] = 2000.0
            v[sb > cur] = -1000.0 - sb
            cf[:, C_FB + qt * 32 + sb] = v
    for si in range(12):
        for mm in range(16):
            cf[:, C_BT + si * 16 + mm] = -np.float32(SLOPES[si]) * 128.0 * mm
    return cb16, cf


class _Op:
    __slots__ = ('fn', 'waits', 'dma_key', 'needed', 'ms')

    def __init__(self, fn, waits, dma_key):
        self.fn = fn
        self.waits = waits
        self.dma_key = dma_key
        self.needed = False
        self.ms = 0


class Sched:
    COMPUTE = ('pe', 'act', 'dve', 'pool')
    ALL = ('pe', 'act', 'dve', 'pool', 'sp')

    def __init__(self, nc, es):
        self.nc = nc
        self.es = es
        self.ops = {e: [] for e in self.ALL}
        self.res = {}
        self.waited = {e: {} for e in self.ALL}
        self.dcount = {}
        self.dsem = {}
        self.esem = {e: es.enter_context(nc.semaphore('es_' + e)) for e in self.COMPUTE}

    def _need(self, eng, tok, waits):
        if tok is None:
            return
        key = (tok[0], tok[1])
        w = self.waited[eng]
        if w.get(key, 0) >= tok[2]:
            return
        w[key] = tok[2]
        waits.append(tok)
        if tok[0] == 'e':
            self.ops[tok[1]][tok[2] - 1].needed = True

    def add(self, eng, fn, reads=(), writes=(), dma_key=None, ndma=1):
        waits = []
        is_dma = dma_key is not None
        for r in reads:
            st = self.res.get(r)
            if st is not None:
                self._need(eng, st[0], waits)
        for w in writes:
            st = self.res.get(w)
            if st is not None:
                lw = st[0]
                if lw is not None and (is_dma or not (lw[0] == 'e' and lw[1] == eng)):
                    self._need(eng, lw, waits)
                for tk in st[1].values():
                    if is_dma or not (tk[0] == 'e' and tk[1] == eng):
                        self._need(eng, tk, waits)
        op = _Op(fn, waits, dma_key)
        self.ops[eng].append(op)
        if is_dma:
            if dma_key not in self.dsem:
                self.dsem[dma_key] = self.es.enter_context(self.nc.semaphore('ds_' + dma_key))
                self.dcount[dma_key] = 0
            self.dcount[dma_key] += 16 * ndma
            tok = ('d', dma_key, self.dcount[dma_key])
        else:
            tok = ('e', eng, len(self.ops[eng]))
        for r in reads:
            st = self.res.setdefault(r, [None, {}])
            st[1][(tok[0], tok[1])] = tok
        for w in writes:
            self.res[w] = [tok, {}]
        return tok

    def barrier(self):
        toks = []
        for e in self.COMPUTE:
            n = len(self.ops[e])
            while n > 0 and self.ops[e][n - 1].fn is None:
                n -= 1
            if n > 0:
                toks.append(('e', e, n))
        for k, c in self.dcount.items():
            toks.append(('d', k, c))
        for e in self.ALL:
            waits = []
            for tk in toks:
                if tk[0] == 'e' and tk[1] == e:
                    continue
                self._need(e, tk, waits)
            self.ops[e].append(_Op(None, waits, None))

    def emit(self):
        nc = self.nc
        for e in self.COMPUTE:
            m = 0
            for op in self.ops[e]:
                if op.needed:
                    assert op.fn is not None
                    m += 1
                    op.ms = m

        def run(name):
            def f(eng):
                for op in self.ops[name]:
                    for tk in op.waits:
                        if tk[0] == 'e':
                            eng.wait_ge(self.esem[tk[1]], self.ops[tk[1]][tk[2] - 1].ms)
                        else:
                            eng.wait_ge(self.dsem[tk[1]], tk[2])
                    if op.fn is None:
                        continue
                    r = op.fn(eng)
                    if op.dma_key is not None:
                        for ins in r:
                            ins.then_inc(self.dsem[op.dma_key], 16)
                    elif op.needed:
                        r.then_inc(self.esem[name], 1)
            return f

        with nc.Block() as blk:
            blk.sync(run('sp'))
            blk.tensor(run('pe'))
            blk.vector(run('dve'))
            blk.scalar(run('act'))
            blk.gpsimd(run('pool'))


def build(nseq=4, dbg=False, stop_after=None):
    nc = bass.Bass("TRN2", target_bir_lowering=False)

    def DIN(name, shape, dt=F32):
        return nc.dram_tensor(name, shape, dt, kind="ExternalInput").ap()

    x = DIN("x", [nseq, T, DM])
    winc = DIN("winc", [NCH, 128, 1024])
    wgc = DIN("wgc", [NJ, 128, 1024])
    wuc = DIN("wuc", [NJ, 128, 1024])
    wdc = DIN("wdc", [NJ, 128, 1024])
    woc = DIN("woc", [128, 8192])
    w1c = DIN("w1c", [128, 8192])
    w2c = DIN("w2c", [128, 192])
    lnc = DIN("lnc", [4, DM])
    pvc = DIN("pvc", [128, NPV])
    cb16d = DIN("cb16", [128, NC16], BF16)
    cf32d = DIN("cf32", [128, NC32])
    out = nc.dram_tensor("out", [nseq, T, DM], F32, kind="ExternalOutput").ap()
    if dbg:
        dbg_o = nc.dram_tensor("dbg", [128, 35000], F32, kind="ExternalOutput").ap()

    es = ExitStack()
    S = Sched(nc, es)

    def SB(name, cols, dt=F32):
        return es.enter_context(nc.sbuf_tensor(name, [128, cols], dt))

    A1 = SB("arena1", 8192)
    A2 = SB("arena2", 23040)
    xT = A1[:, :].bitcast(BF16).rearrange("p (c t) -> p c t", c=8)
    w1b = A1[:, 0:4096].bitcast(BF16).rearrange("p (k l h) -> p k l h", k=2, l=32)
    cat = A1[:, :].bitcast(BF16).rearrange("p (t f) -> p t f", t=16)
    featT = [A2[:, i * 1024:(i + 1) * 1024].bitcast(BF16) for i in range(16)]
    o2 = 16384
    dvaug = A2[:, o2:o2 + 4160].bitcast(BF16).rearrange("p (t h e) -> p t h e", t=16, h=4)
    o2 += 4160
    svaug = A2[:, o2:o2 + 1056].bitcast(BF16).rearrange("p (t g e) -> p t g e", t=16, g=2)
    o2 += 1056
    wvaug = A2[:, o2:o2 + 1056].bitcast(BF16).rearrange("p (t g e) -> p t g e", t=16, g=2)
    o2 += 1056
    gates = A2[:, o2:o2 + 384].rearrange("p (t c) -> p t c", t=16)
    woutb = A2[:, 0:4096].bitcast(BF16).rearrange("p (c n) -> p c n", c=8)
    lnp = A2[:, 4096:8192].rearrange("p (a n) -> p a n", a=4)
    aT = A2[:, 8192:8192 + 5632].bitcast(BF16).rearrange("p (j t) -> p j t", j=NJ)
    x1 = A2[:, 13824:13824 + 4096].rearrange("p (t n) -> p t n", t=4)
    x1T = A2[:, 17920:17920 + 2048].bitcast(BF16).rearrange("p (c t) -> p c t", c=8)
    catT = A2[:, 19968:19968 + 512].bitcast(BF16).rearrange("p (c t) -> p c t", c=8)
    x1b = A2[:, 20480:20480 + 512].bitcast(BF16)

    acc = SB("acc", 2048)
    accv = acc[:, :].rearrange("p (t h d) -> p t h d", t=4, h=8)
    impacc = SB("impacc", 256)
    impv = impacc[:, :].rearrange("p (t g s) -> p t g s", t=4, g=2)
    wst = [SB("wst%d" % i, 1024) for i in range(3)]
    wbf = [SB("wbf%d" % i, 1024, BF16) for i in range(3)]
    xs = [SB("xs%d" % i, 1024) for i in range(2)]
    xb = SB("xb", 1024, BF16)
    pt = [SB("pt%d" % i, 512, BF16) for i in range(3)]
    cb = SB("cb", NC16, BF16)
    cf = SB("cf", NC32)
    pv = SB("pv", NPV)
    w2b = SB("w2b", 192, BF16)
    tmp0 = SB("tmp0", 512)
    tmpa = SB("tmpa", 512)
    tmpb = SB("tmpb", 512)
    sm = SB("sm", 256)
    selw = SB("selw", 256)
    selb = SB("selb", 256, BF16)
    selbT = SB("selbT", 1024, BF16)
    kcT = SB("kcT", 128, BF16)
    vcaug = SB("vcaug", 200, BF16)
    hid = SB("hid", 512, BF16)
    cmpt = [SB("cmpt%d" % i, 128, BF16) for i in range(2)]
    hpre = SB("hpre", 512)
    PS = [es.enter_context(nc.psum_tensor("ps%d" % i, [128, 512], F32)) for i in range(8)]

    selbv = selb[:, :].rearrange("p (t g s) -> p t g s", t=4, g=2)
    selbTv = selbT[:, :].rearrange("p (g q) -> p g q", g=2)
    vcv = vcaug[:, :].rearrange("p (g e) -> p g e", g=2)
    hidv = hid[:, :].rearrange("p (k n) -> p k n", k=4)

    ident = cb[:, C_ID:C_ID + 128]

    def dma(key, outs_ins, reads, writes):
        def fn(eng):
            return [eng.dma_start(out=o, in_=i) for (o, i) in outs_ins]
        return S.add('sp', fn, reads=reads, writes=writes, dma_key=key, ndma=len(outs_ins))

    def mm(o, lhsT, rhs, start, stop, reads, writes):
        S.add('pe', lambda e: e.matmul(o, lhsT=lhsT, rhs=rhs, start=start, stop=stop), reads=reads, writes=writes)

    def tr(o, i, reads, writes):
        S.add('pe', lambda e: e.transpose(out=o, in_=i, identity=ident), reads=reads + ['cb'], writes=writes)

    def act(o, i, func, reads, writes, bias=None, scale=None, accum=None):
        kw = {}
        if bias is not None:
            kw['bias'] = bias
        if scale is not None:
            kw['scale'] = scale
        if accum is not None:
            kw['accum_out'] = accum
        S.add('act', lambda e: e.activation(out=o, in_=i, func=func, **kw), reads=reads, writes=writes)

    def cp(eng, o, i, reads, writes):
        if eng == 'act':
            S.add(eng, lambda e: e.activation(out=o, in_=i, func=AF.Copy), reads=reads, writes=writes)
        else:
            S.add(eng, lambda e: e.tensor_copy(out=o, in_=i), reads=reads, writes=writes)

    def tt(eng, o, a, b, op, reads, writes):
        S.add(eng, lambda e: e.tensor_tensor(out=o, in0=a, in1=b, op=op), reads=reads, writes=writes)

    def ts(eng, o, a, s1, s2, op0, op1, reads, writes):
        if op1 is None:
            S.add(eng, lambda e: e.tensor_scalar(out=o, in0=a, scalar1=s1, scalar2=None, op0=op0), reads=reads, writes=writes)
        else:
            S.add(eng, lambda e: e.tensor_scalar(out=o, in0=a, scalar1=s1, scalar2=s2, op0=op0, op1=op1), reads=reads, writes=writes)

    def stt(eng, o, a, sc, b, op0, op1, reads, writes):
        S.add(eng, lambda e: e.scalar_tensor_tensor(out=o, in0=a, scalar=sc, in1=b, op0=op0, op1=op1), reads=reads, writes=writes)

    dma('c0', [(cb[:, :], cb16d)], [], ['cb'])
    dma('c1', [(cf[:, :], cf32d), (pv[:, :], pvc)], [], ['cf', 'pv'])
    dma('c2', [(wst[0][:, 0:192], w2c)], [], ['wst0'])
    cp('pool', w2b[:, :], wst[0][:, 0:192], ['wst0'], ['w2b'])
    tt('dve', sm[:, 0:64], pv[:, 0:64], pv[:, 64:128], ALU.mult, ['pv'], ['sm_a'])
    tt('dve', sm[:, 64:128], pv[:, 128:192], pv[:, 192:256], ALU.mult, ['pv'], ['sm_b'])
    S.add('dve', lambda e: e.tensor_reduce(out=sm[:, 128:129], in_=sm[:, 0:64], axis=AX.X, op=ALU.add), reads=['sm_a'], writes=['sm_c'])
    S.add('dve', lambda e: e.tensor_reduce(out=sm[:, 129:130], in_=sm[:, 64:128], axis=AX.X, op=ALU.add), reads=['sm_b'], writes=['sm_d'])
    act(sm[:, 130:131], sm[:, 128:129], AF.Exp, ['sm_c'], ['sm_e'])
    act(sm[:, 131:132], sm[:, 129:130], AF.Exp, ['sm_d'], ['sm_f'])
    tt('dve', sm[:, 132:133], sm[:, 131:132], sm[:, 130:131], ALU.subtract, ['sm_e', 'sm_f'], ['sm_g'])
    ts('dve', sm[:, 133:134], sm[:, 132:133], -LAM_INIT, None, ALU.add, None, ['sm_g'], ['neglam'])
    neglam = sm[:, 133:134]
    ts('dve', sm[:, 0:128], pv[:, P_SUBG:P_SUBG + 128], 1.0 - LAM_INIT, None, ALU.mult, None, ['pv', 'sm_a', 'sm_b', 'sm_c', 'sm_d'], ['gsub'])
    gsub = sm[:, 0:128]

    rr = {'ps': 0, 'w': 0, 'pt': 0, 'ev': 0}

    def evac_eng():
        rr['ev'] += 1
        return 'act' if rr['ev'] % 2 else 'dve'

    for s in range(nseq):
        for t in range(NT):
            b = t % 2
            dma('xs%d' % b, [(xs[b][:, :], x[s, t * 128:(t + 1) * 128, :])], [], ['xs%d' % b])
            cp('pool', xb[:, :], xs[b][:, :], ['xs%d' % b], ['xb'])
            pb = PS[t % 2]
            pbb = pb[:, :].bitcast(BF16)
            for c in range(8):
                tr(pbb[:, c * 128:(c + 1) * 128], xb[:, c * 128:(c + 1) * 128], ['xb'], ['ps%d' % (t % 2)])
            cp(evac_eng(), xT[:, :, t * 128:(t + 1) * 128], pbb.rearrange("p (c t) -> p c t", c=8),
               ['ps%d' % (t % 2)], ['xT%d' % (t // 4)])
        S.add('pool', lambda e: e.memset(dvaug[:, :, :, 128:130], 1.0), reads=[], writes=['dvaug'])
        S.add('pool', lambda e: e.memset(svaug[:, :, :, 64:66], 1.0), reads=[], writes=['svaug'])
        S.add('pool', lambda e: e.memset(wvaug[:, :, :, 64:66], 1.0), reads=[], writes=['wvaug'])
        for ci in range(NCH):
            pieces, kind, scale = CH[ci]
            wb = rr['w'] % 3
            rr['w'] += 1
            dma('wst%d' % wb, [(wst[wb][:, :], winc[ci])], [], ['wst%d' % wb])
            cp('pool', wbf[wb][:, :], wst[wb][:, :], ['wst%d' % wb], ['wbf%d' % wb])
            wv = wbf[wb][:, :].rearrange("p (c n) -> p c n", c=8)
            if kind == 'F':
                for tg in range(4):
                    pi = rr['ps'] % 4
                    rr['ps'] += 1
                    for k in range(8):
                        mm(PS[pi][:, :], wv[:, k, :], xT[:, k, tg * 512:(tg + 1) * 512], k == 0, k == 7,
                           ['wbf%d' % wb, 'xT%d' % tg], ['ps%d' % pi])
                    dst = featT[ci][:, tg * 512:(tg + 1) * 512]
                    if evac_eng() == 'act':
                        act(dst, PS[pi][:, :], AF.Copy, ['ps%d' % pi], ['featT%d' % ci], scale=float(scale))
                    else:
                        ts('dve', dst, PS[pi][:, :], float(scale), None, ALU.mult, None, ['ps%d' % pi], ['featT%d' % ci])
            else:
                for t4 in range(4):
                    pi = rr['ps'] % 4
                    rr['ps'] += 1
                    for ti in range(4):
                        t = t4 * 4 + ti
                        for k in range(8):
                            mm(PS[pi][:, ti * 128:(ti + 1) * 128], xT[:, k, t * 128:(t + 1) * 128], wv[:, k, :], k == 0, k == 7,
                               ['wbf%d' % wb, 'xT%d' % t4], ['ps%d' % pi])
                    src = PS[pi][:, :].rearrange("p (t n) -> p t n", t=4)
                    tsl = slice(t4 * 4, t4 * 4 + 4)
                    if ci < 20:
                        h = ci - 16
                        cp(evac_eng(), dvaug[:, tsl, h, 0:128], src, ['ps%d' % pi], ['dvaug'])
                    elif ci == 20:
                        cp(evac_eng(), svaug[:, tsl, :, 0:64], src.rearrange("p t (g e) -> p t g e", g=2), ['ps%d' % pi], ['svaug'])
                    elif ci == 21:
                        cp(evac_eng(), wvaug[:, tsl, :, 0:64], src.rearrange("p t (g e) -> p t g e", g=2), ['ps%d' % pi], ['wvaug'])
                    else:
                        act(gates[:, tsl, :], src[:, :, 0:24], AF.Sigmoid, ['ps%d' % pi], ['gates'])
        S.barrier()
        if stop_after == 'P':
            break

        for q8 in range(8):
            wb = rr['w'] % 3
            rr['w'] += 1
            dma('wst%d' % wb, [(wst[wb][:, :], w1c[:, q8 * 1024:(q8 + 1) * 1024])], [], ['wst%d' % wb])
            cp('pool', A1[:, q8 * 512:(q8 + 1) * 512].bitcast(BF16), wst[wb][:, :], ['wst%d' % wb], ['w1b'])
        for kv in range(2):
            src = featT[F_CK + kv]
            pcol = P_PEK if kv == 0 else P_PEV
            for l in range(32):
                cb_i = l % 2
                ts('dve' if l % 2 else 'pool', cmpt[cb_i][:, 0:127], src[:, l:l + 16 * 126 + 1:16], pv[:, pcol + l:pcol + l + 1], None,
                   ALU.add, None, ['featT%d' % (F_CK + kv), 'pv'], ['cmpt%d' % cb_i])
                for g in range(2):
                    mm(PS[4 + g][:, 0:127], w1b[g * 64:(g + 1) * 64, kv, l, :], cmpt[cb_i][g * 64:(g + 1) * 64, 0:127],
                       l == 0, l == 31, ['w1b', 'cmpt%d' % cb_i], ['ps%d' % (4 + g)])
            for g in range(2):
                hp = hpre[:, g * 128:g * 128 + 127]
                h2 = hpre[:, 256 + g * 128:256 + g * 128 + 127]
                cp('dve', hp, PS[4 + g][:, 0:127], ['ps%d' % (4 + g)], ['hp%d' % g])
                tt('dve', h2, hp, hp, ALU.mult, ['hp%d' % g], ['h2%d' % g])
                ts('dve', h2, h2, 0.044715, 1.0, ALU.mult, ALU.add, ['h2%d' % g], ['h2b%d' % g])
                tt('dve', h2, h2, hp, ALU.mult, ['h2b%d' % g, 'hp%d' % g], ['h2c%d' % g])
                act(h2, h2, AF.Sigmoid, ['h2c%d' % g], ['h2d%d' % g], scale=1.5957691216057308)
                tt('dve', hidv[:, kv * 2 + g, 0:127], h2, hp, ALU.mult, ['h2d%d' % g, 'hp%d' % g], ['hid'])
        for g in range(2):
            mm(PS[6][:, 0:127], w2b[:, 0:128], hidv[:, g, 0:127], True, True, ['w2b', 'hid'], ['ps6'])
            cp('dve', kcT[g * 64:(g + 1) * 64, 0:127], PS[6][g * 64:(g + 1) * 64, 0:127], ['ps6'], ['kcT'])
            mm(PS[7][0:127, 0:64], hidv[:, 2 + g, 0:127], w2b[:, 128:192], True, True, ['w2b', 'hid'], ['ps7'])
            cp('dve', vcv[0:127, g, 0:64], PS[7][0:127, 0:64], ['ps7'], ['vcaug'])
            S.add('pool', lambda e, g=g: e.memset(vcv[0:127, g, 64:65], 1.0), reads=[], writes=['vcaug'])
            cp('pool', vcv[0:127, g, 65:97], cb[0:127, C_OVL:C_OVL + 32], ['cb'], ['vcaug'])
        S.barrier()
        if stop_after == 'C':
            break

        def attn_qg(qg, kT_of, qT_of, v_of, dv, slope_i, window, selg, o_slot, kres, qres, vres, ores):
            kts = range(max(0, 4 * qg - 4), 4 * qg + 4) if window else range(0, 4 * qg + 4)
            started = set()
            for kt in kts:
                qlo = max(4 * qg, kt)
                qhi = min(4 * qg + 3, kt + 4) if window else 4 * qg + 3
                if qhi < qlo:
                    continue
                q0 = qlo * 128
                n = (qhi - qlo + 1) * 128
                pi = rr['ps'] % 3
                rr['ps'] += 1
                st = PS[pi]
                sres = 'ps%d' % pi
                mm(st[:, 0:n], kT_of(kt), qT_of(q0, n), True, False, [kres, qres], [sres])
                if kt == qlo:
                    mm(st[:, 0:128], ident, cb[:, C_TLO:C_TLO + 128], False, False, ['cb'], [sres])
                if window and qhi == kt + 4:
                    mm(st[:, n - 128:n], ident, cb[:, C_THI:C_THI + 128], False, False, ['cb'], [sres])
                if selg is not None:
                    mm(st[:, 0:n], cb[0:32, C_E + kt * 128:C_E + (kt + 1) * 128], selbTv[0:32, selg, q0 - qg * 512:q0 - qg * 512 + n],
                       False, False, ['cb', 'selbT'], [sres])
                mm(st[:, 0:n], cb[0:9, C_ALA + slope_i * 128:C_ALA + (slope_i + 1) * 128], cb[0:9, C_ALB:C_ALB + n],
                   False, True, ['cb'], [sres])
                pj = rr['pt'] % 3
                rr['pt'] += 1
                m_off = qlo - kt
                act(pt[pj][:, 0:n], st[:, 0:n], AF.Exp, [sres, 'cf'], ['pt%d' % pj],
                    bias=cf[:, C_BT + slope_i * 16 + m_off:C_BT + slope_i * 16 + m_off + 1])
                for qt in range(qlo, qhi + 1):
                    oap, obank = o_slot(qt - 4 * qg)
                    first = obank not in started
                    started.add(obank)
                    last = (kt == qt)
                    mm(oap, pt[pj][:, (qt - qlo) * 128:(qt - qlo + 1) * 128], v_of(kt), first, last,
                       ['pt%d' % pj, vres], ores)

        for h in range(4):
            for qg in range(4):
                tsl = slice(qg * 4, qg * 4 + 4)
                for c in range(2):
                    ob = 4 + 2 * ((h * 8 + qg * 2 + c) % 2)
                    ores = ['o%d' % ob, 'o%d' % (ob + 1)]

                    def o_slot(qi, ob=ob):
                        return PS[ob + qi // 2][:, (qi % 2) * 256:(qi % 2) * 256 + 129], ob + qi // 2
                    attn_qg(qg,
                            lambda kt, h=h, c=c: featT[F_DK + h][c * 64:(c + 1) * 64, kt * 128:(kt + 1) * 128],
                            lambda q0, n, h=h, c=c: featT[F_DQ + h][c * 64:(c + 1) * 64, q0:q0 + n],
                            lambda kt, h=h: dvaug[:, kt, h, 0:129],
                            128, h, False, None, o_slot,
                            'featT%d' % (F_DK + h), 'featT%d' % (F_DQ + h), 'dvaug', ores)
                    for half in range(2):
                        ov = PS[ob + half][:, :].rearrange("p (a e) -> p a e", a=2)
                        rs = sm[:, 140 + c * 4 + half * 2:140 + c * 4 + half * 2 + 2]
                        S.add('dve', lambda e, rs=rs, ov=ov: e.reciprocal(out=rs.unsqueeze(2), in_=ov[:, :, 128:129]), reads=ores, writes=['rs%d%d' % (c, half)])
                        dst = (tmp0 if c == 0 else tmpa)[:, half * 256:(half + 1) * 256].rearrange("p (a e) -> p a e", a=2)
                        tt('dve', dst, ov[:, :, 0:128], rs.unsqueeze(2).to_broadcast([128, 2, 128]), ALU.mult,
                           ores + ['rs%d%d' % (c, half)], ['tmp%d' % c])
                stt('dve', tmpb[:, :], tmpa[:, :], neglam, tmp0[:, :], ALU.mult, ALU.add, ['tmp0', 'tmp1', 'neglam'], ['tmpb'])
                for qi in range(4):
                    act(tmpa[:, qi * 128:(qi + 1) * 128], tmpb[:, qi * 128:(qi + 1) * 128], AF.Square, ['tmpb'], ['tmp1'],
                        accum=sm[:, 150 + qi:151 + qi])
                ts('dve', sm[:, 156:160], sm[:, 150:154], 1.0 / 128.0, 1e-5, ALU.mult, ALU.add, ['tmp1'], ['rms_a'])
                act(sm[:, 156:160], sm[:, 156:160], AF.Sqrt, ['rms_a'], ['rms_b'])
                S.add('dve', lambda e: e.reciprocal(out=sm[:, 160:164], in_=sm[:, 156:160]), reads=['rms_b'], writes=['rms_c'])
                tb3 = tmpb[:, :].rearrange("p (a e) -> p a e", a=4)
                tt('dve', tb3, tb3, sm[:, 160:164].unsqueeze(2).to_broadcast([128, 4, 128]), ALU.mult, ['tmpb', 'rms_c'], ['tmpb', 'tmpb2'])
                tt('pool', cat[:, tsl, h * 128:(h + 1) * 128], tb3, gsub.unsqueeze(1).to_broadcast([128, 4, 128]), ALU.mult,
                   ['tmpb2', 'gsub'], ['cat'])
        if stop_after == 'D':
            break

        for qg in range(4):
            tsl = slice(qg * 4, qg * 4 + 4)
            for h in range(8):
                g = h // 4
                base = (h // 4) * 64
                j = h % 4
                pi = rr['ps'] % 3
                rr['ps'] += 1
                st = PS[pi]
                sres = 'ps%d' % pi
                mm(st[0:127, :], kcT[base:base + 64, 0:127], featT[F_NQ + j][base:base + 64, qg * 512:(qg + 1) * 512], True, False,
                   ['kcT', 'featT%d' % (F_NQ + j)], [sres])
                mm(st[0:127, :], cb[0:127, C_ID:C_ID + 127], cb[0:127, C_MCMP + qg * 512:C_MCMP + (qg + 1) * 512], False, True, ['cb'], [sres])
                pj = rr['pt'] % 3
                rr['pt'] += 1
                act(pt[pj][0:127, :], st[0:127, :], AF.Exp, [sres], ['pt%d' % pj])
                ob = 4 + (h % 2)
                ores = 'o%d' % ob
                ov = PS[ob][:, 0:400].rearrange("p (a e) -> p a e", a=4)
                for qi in range(4):
                    mm(ov[:, qi, 0:97], pt[pj][0:127, qi * 128:(qi + 1) * 128], vcv[0:127, g, 0:97], True, True, ['pt%d' % pj, 'vcaug'], [ores])
                rs = sm[:, 170:174]
                ts('dve', rs.unsqueeze(2), ov[:, :, 64:65], 1e-30, None, ALU.max, None, [ores], ['rsA'])
                S.add('dve', lambda e, rs=rs: e.reciprocal(out=rs, in_=rs), reads=['rsA'], writes=['rsB'])
                sg = sm[:, 174:178]
                tt('dve', sg.unsqueeze(2), rs.unsqueeze(2), gates[:, tsl, h * 3:h * 3 + 1], ALU.mult, ['rsB', 'gates'], ['sgA'])
                tt('dve', accv[:, :, h, :], ov[:, :, 0:64], sg.unsqueeze(2).to_broadcast([128, 4, 64]), ALU.mult, [ores, 'sgA'], ['acc%d' % h])
                if not DBG_BR[0]:
                    S.add('dve', lambda e, h=h: e.memset(accv[:, :, h, :], 0.0), reads=[], writes=['acc%d' % h])
                if j == 0:
                    tt('dve', impv[:, :, g, :], ov[:, :, 65:97], rs.unsqueeze(2).to_broadcast([128, 4, 32]), ALU.mult, [ores, 'rsB'], ['imp%d' % g])
                else:
                    tt('dve', tmp0[:, 0:128].rearrange("p (a e) -> p a e", a=4), ov[:, :, 65:97], rs.unsqueeze(2).to_broadcast([128, 4, 32]),
                       ALU.mult, [ores, 'rsB'], ['tmp0'])
                    tt('dve', impv[:, :, g, :], impv[:, :, g, :], tmp0[:, 0:128].rearrange("p (a e) -> p a e", a=4), ALU.add,
                       ['tmp0', 'imp%d' % g], ['imp%d' % g])
            for g in range(2):
                for qi in range(4):
                    qt = qg * 4 + qi
                    val = selw[:, 0:32]
                    tt('dve', val, impv[:, qi, g, :], cf[:, C_FB + qt * 32:C_FB + (qt + 1) * 32], ALU.add, ['imp%d' % g, 'cf'], ['sw_a'])
                    S.add('dve', lambda e: e.max(out=selw[:, 32:40], in_=selw[:, 0:32]), reads=['sw_a'], writes=['sw_b'])
                    S.add('dve', lambda e: e.match_replace(out=selw[:, 64:96], in_to_replace=selw[:, 32:40], in_values=selw[:, 0:32], imm_value=-1e9),
                          reads=['sw_a', 'sw_b'], writes=['sw_c'])
                    S.add('dve', lambda e: e.max(out=selw[:, 40:48], in_=selw[:, 64:96]), reads=['sw_c'], writes=['sw_d'])
                    S.add('dve', lambda e: e.tensor_reduce(out=selw[:, 48:49], in_=selw[:, 40:48], axis=AX.X, op=ALU.min), reads=['sw_d'], writes=['sw_e'])
                    ts('dve', selbv[:, qi, g, :], val, selw[:, 48:49], NEG, ALU.is_lt, ALU.mult, ['sw_a', 'sw_e'], ['selb'])
                    pbb = PS[3][:, :].bitcast(BF16)
                    tr(pbb[0:32, (g * 4 + qi) * 128:(g * 4 + qi + 1) * 128], selbv[:, qi, g, :], ['selb'], ['ps3'])
                cp('act', selbTv[0:32, g, :], PS[3][:, :].bitcast(BF16)[0:32, g * 512:(g + 1) * 512], ['ps3'], ['selbT'])
            for br in range(2):
                for h in range(8):
                    g = h // 4
                    base = g * 64
                    j = h % 4
                    ob = 4 + (h % 4)
                    ores = ['o%d' % ob]
                    ovv = PS[ob][:, 0:264].rearrange("p (a e) -> p a e", a=4)

                    def o_slot(qi, ovv=ovv, ob=ob):
                        return ovv[:, qi, 0:65], ob
                    kf = F_SK if br == 0 else F_WK
                    va = svaug if br == 0 else wvaug
                    attn_qg(qg,
                            lambda kt, kf=kf, base=base: featT[kf][base:base + 64, kt * 128:(kt + 1) * 128],
                            lambda q0, n, j=j, base=base: featT[F_NQ + j][base:base + 64, q0:q0 + n],
                            lambda kt, va=va, g=g: va[:, kt, g, 0:65],
                            64, 4 + h, br == 1, (g if br == 0 else None), o_slot,
                            'featT%d' % kf, 'featT%d' % (F_NQ + j), 'svaug' if br == 0 else 'wvaug', ores)
                    rs = sm[:, 180 + (h % 2) * 8:184 + (h % 2) * 8]
                    sg = sm[:, 184 + (h % 2) * 8:188 + (h % 2) * 8]
                    rk = 'rs%d' % (h % 2)
                    S.add('dve', lambda e, rs=rs, ovv=ovv: e.reciprocal(out=rs.unsqueeze(2), in_=ovv[:, :, 64:65]), reads=ores, writes=[rk + 'A'])
                    tt('dve', sg.unsqueeze(2), rs.unsqueeze(2), gates[:, tsl, h * 3 + 1 + br:h * 3 + 2 + br], ALU.mult, [rk + 'A', 'gates'], [rk + 'B'])
                    tv = (tmpa if h % 2 else tmpb)[:, 0:256].rearrange("p (a e) -> p a e", a=4)
                    tk = 'tv%d' % (h % 2)
                    tt('dve', tv, ovv[:, :, 0:64], sg.unsqueeze(2).to_broadcast([128, 4, 64]), ALU.mult, ores + [rk + 'B'], [tk])
                    if not DBG_BR[1 + br]:
                        S.add('dve', lambda e, tv=tv: e.memset(tv, 0.0), reads=[], writes=[tk])
                    if br == 0:
                        tt('pool', accv[:, :, h, :], accv[:, :, h, :], tv, ALU.add, [tk, 'acc%d' % h], ['acc%d' % h])
                    else:
                        tt('pool', cat[:, tsl, 512 + h * 64:512 + (h + 1) * 64], accv[:, :, h, :], tv, ALU.add, [tk, 'acc%d' % h], ['cat'])
        S.barrier()
        if stop_after == 'N':
            break

        for q8 in range(8):
            wb = rr['w'] % 3
            rr['w'] += 1
            dma('wst%d' % wb, [(wst[wb][:, :], woc[:, q8 * 1024:(q8 + 1) * 1024])], [], ['wst%d' % wb])
            cp('pool', woutb[:, q8, :], wst[wb][:, :], ['wst%d' % wb], ['woutb'])
        dma('lnp', [(lnp[:, a, :], lnc[a].partition_broadcast(128)) for a in range(4)], [], ['lnp'])

        def layer_norm(src, srck, dst, dstk, gi, eng_aff):
            S.add('dve', lambda e: e.bn_stats(out=sm[:, 200:206], in_=src[:, 0:512]), reads=[srck], writes=['bn_a'])
            S.add('dve', lambda e: e.bn_stats(out=sm[:, 206:212], in_=src[:, 512:1024]), reads=[srck], writes=['bn_b'])
            S.add('dve', lambda e: e.bn_aggr(out=sm[:, 212:214], in_=sm[:, 200:212]), reads=['bn_a', 'bn_b'], writes=['bn_c'])
            ts('dve', sm[:, 214:215], sm[:, 213:214], 1e-5, None, ALU.add, None, ['bn_c'], ['bn_d'])
            act(sm[:, 214:215], sm[:, 214:215], AF.Sqrt, ['bn_d'], ['bn_e'])
            S.add('dve', lambda e: e.reciprocal(out=sm[:, 215:216], in_=sm[:, 214:215]), reads=['bn_e'], writes=['bn_f'])
            ts('dve', src, src, sm[:, 212:213], sm[:, 215:216], ALU.subtract, ALU.mult, [srck, 'bn_c', 'bn_f'], [srck])
            tt(eng_aff, src, src, lnp[:, gi, :], ALU.mult, [srck, 'lnp'], [srck])
            tt(eng_aff, dst, src, lnp[:, gi + 1, :], ALU.add, [srck, 'lnp'], [dstk])

        for tg in range(4):
            for ti in range(4):
                t = tg * 4 + ti
                b = t % 2
                dma('xs%d' % b, [(xs[b][:, :], x[s, t * 128:(t + 1) * 128, :])], [], ['xs%d' % b])
                pbb = PS[3][:, :].bitcast(BF16)
                for c in range(8):
                    tr(pbb[:, c * 128:(c + 1) * 128], cat[:, t, c * 128:(c + 1) * 128], ['cat'], ['ps3'])
                cp('act', catT[:, :, :], pbb.rearrange("p (c t) -> p c t", c=8), ['ps3'], ['catT'])
                for hf in range(2):
                    for k in range(8):
                        mm(PS[hf][:, :], catT[:, k, :], woutb[:, k, hf * 512:(hf + 1) * 512], k == 0, k == 7, ['catT', 'woutb'], ['ps%d' % hf])
                    stt('dve', xs[b][:, hf * 512:(hf + 1) * 512], xs[b][:, hf * 512:(hf + 1) * 512], float(ALPHA), PS[hf][:, :],
                        ALU.mult, ALU.add, ['xs%d' % b, 'ps%d' % hf], ['xs%d' % b])
                layer_norm(xs[b][:, :], 'xs%d' % b, x1[:, ti, :], 'x1_%d' % ti, 0, 'pool')
                cp('act', x1b[:, :], x1[:, ti, :], ['x1_%d' % ti], ['x1b'])
                pbb2 = PS[2][:, :].bitcast(BF16)
                for c in range(8):
                    tr(pbb2[:, c * 128:(c + 1) * 128], x1b[:, c * 128:(c + 1) * 128], ['x1b'], ['ps2'])
                cp('act', x1T[:, :, ti * 128:(ti + 1) * 128], pbb2.rearrange("p (c t) -> p c t", c=8), ['ps2'], ['x1T'])
            for j in range(NJ):
                wb1 = rr['w'] % 3
                rr['w'] += 1
                dma('wst%d' % wb1, [(wst[wb1][:, :], wgc[j])], [], ['wst%d' % wb1])
                cp('pool', wbf[wb1][:, :], wst[wb1][:, :], ['wst%d' % wb1], ['wbf%d' % wb1])
                wb2 = rr['w'] % 3
                rr['w'] += 1
                dma('wst%d' % wb2, [(wst[wb2][:, :], wuc[j])], [], ['wst%d' % wb2])
                cp('pool', wbf[wb2][:, :], wst[wb2][:, :], ['wst%d' % wb2], ['wbf%d' % wb2])
                wgv = wbf[wb1][:, :].rearrange("p (c n) -> p c n", c=8)
                wuv = wbf[wb2][:, :].rearrange("p (c n) -> p c n", c=8)
                pg = (j % 2) * 2
                for k in range(8):
                    mm(PS[pg][:, :], wgv[:, k, :], x1T[:, k, :], k == 0, k == 7, ['wbf%d' % wb1, 'x1T'], ['ps%d' % pg])
                for k in range(8):
                    mm(PS[pg + 1][:, :], wuv[:, k, :], x1T[:, k, :], k == 0, k == 7, ['wbf%d' % wb2, 'x1T'], ['ps%d' % (pg + 1)])
                sgb = tmp0 if j % 2 == 0 else tmpa
                sgk = 'tmp0' if j % 2 == 0 else 'tmp1'
                act(sgb[:, :], PS[pg][:, :], AF.Silu, ['ps%d' % pg], [sgk])
                tt('dve', aT[:, j, :], sgb[:, :], PS[pg + 1][:, :], ALU.mult, [sgk, 'ps%d' % (pg + 1)], ['aT'])
            for hf in range(2):
                for j in range(NJ):
                    wb = rr['w'] % 3
                    rr['w'] += 1
                    dma('wst%d' % wb, [(wst[wb][:, 0:512], wdc[j, :, hf * 512:(hf + 1) * 512])], [], ['wst%d' % wb])
                    cp('pool', wbf[wb][:, 0:512], wst[wb][:, 0:512], ['wst%d' % wb], ['wbf%d' % wb])
                    for ti in range(4):
                        mm(PS[4 + ti][:, :], aT[:, j, ti * 128:(ti + 1) * 128], wbf[wb][:, 0:512], j == 0, j == NJ - 1,
                           ['aT', 'wbf%d' % wb], ['o%d' % (4 + ti)])
                for ti in range(4):
                    stt('dve', x1[:, ti, hf * 512:(hf + 1) * 512], x1[:, ti, hf * 512:(hf + 1) * 512], float(ALPHA), PS[4 + ti][:, :],
                        ALU.mult, ALU.add, ['x1_%d' % ti, 'o%d' % (4 + ti)], ['x1_%d' % ti])
            for ti in range(4):
                t = tg * 4 + ti
                ob = ti % 2
                layer_norm(x1[:, ti, :], 'x1_%d' % ti, x1[:, ti, :], 'x1_%d' % ti, 2, 'pool')
                dma('out%d' % ti, [(out[s, t * 128:(t + 1) * 128, :], x1[:, ti, :])], ['x1_%d' % ti], ['outd'])
        S.barrier()

    if dbg:
        S.barrier()
        lst = []
        for c0 in range(0, 8192, 2048):
            lst.append((dbg_o[:, c0:c0 + 2048], A1[:, c0:c0 + 2048]))
        for c0 in range(0, 23040, 2048):
            c1 = min(23040, c0 + 2048)
            lst.append((dbg_o[:, 8192 + c0:8192 + c1], A2[:, c0:c1]))
        lst += [(dbg_o[:, 31232:31232 + 2048], acc[:, :]), (dbg_o[:, 33280:33280 + 256], sm[:, :]), (dbg_o[:, 33536:33536 + 256], impacc[:, :])]
        lst += [(dbg_o[:, 33792:33856], kcT[:, :].bitcast(F32)), (dbg_o[:, 33856:33956], vcaug[:, :].bitcast(F32)),
                (dbg_o[:, 33956:34212], hid[:, :].bitcast(F32)), (dbg_o[:, 34212:34340], selb[:, :].bitcast(F32)),
                (dbg_o[:, 34340:34340 + 512], selbT[:, :].bitcast(F32))]
        dma('dbg', lst, [], ['dbgd'])
    S.barrier()
    S.emit()
    es.close()
    return nc


def host_prep(inputs):
    f = lambda a: np.ascontiguousarray(a, dtype=np.float32)
    w_in = f(inputs["w_in"][0])
    winc = np.zeros((NCH, 128, 8, 128), np.float32)
    for ci, (pieces, kind, scale) in enumerate(CH):
        o = 0
        for (c0, ncol) in pieces:
            blk = w_in[:, c0:c0 + ncol].reshape(8, 128, ncol).transpose(1, 0, 2)
            winc[ci, :, :, o:o + ncol] = blk
            o += ncol
    winc = winc.reshape(NCH, 128, 1024)

    def chunk_cols(w):
        return np.ascontiguousarray(w.reshape(8, 128, NJ, 128).transpose(2, 1, 0, 3)).reshape(NJ, 128, 1024)
    wgc = chunk_cols(f(inputs["w_gate"][0]))
    wuc = chunk_cols(f(inputs["w_up"][0]))
    wdc = np.ascontiguousarray(f(inputs["w_down"][0]).reshape(NJ, 128, 1024))
    woc = np.ascontiguousarray(f(inputs["w_out"][0]).reshape(8, 128, 1024).transpose(1, 0, 2)).reshape(128, 8192)
    w1 = np.stack([f(inputs["cmp_w1_k"][0]), f(inputs["cmp_w1_v"][0])], 0)
    w1 = w1.reshape(2, 32, 64, 128).transpose(2, 0, 1, 3).reshape(64, 8192)
    w1c = np.ascontiguousarray(np.concatenate([w1, w1], 0))
    w2k = f(inputs["cmp_w2_k"][0])
    w2v = f(inputs["cmp_w2_v"][0])
    w2c = np.ascontiguousarray(np.concatenate([w2k, w2k, w2v], 1))
    lnc = np.ascontiguousarray(np.stack([f(inputs["ln1_g"][0]), f(inputs["ln1_b"][0]), f(inputs["ln2_g"][0]), f(inputs["ln2_b"][0])], 0))
    pvc = np.zeros((128, NPV), np.float32)
    pvc[:, 0:64] = f(inputs["diff_lq1"][0])[None, :]
    pvc[:, 64:128] = f(inputs["diff_lk1"][0])[None, :]
    pvc[:, 128:192] = f(inputs["diff_lq2"][0])[None, :]
    pvc[:, 192:256] = f(inputs["diff_lk2"][0])[None, :]
    pvc[:, P_SUBG:P_SUBG + 128] = f(inputs["diff_subln_g"][0])[None, :]
    pek = f(inputs["cmp_pe_k"][0]).T
    pev = f(inputs["cmp_pe_v"][0]).T
    pvc[:, P_PEK:P_PEK + 32] = np.concatenate([pek, pek], 0)
    pvc[:, P_PEV:P_PEV + 32] = np.concatenate([pev, pev], 0)
    cb16, cf32 = make_consts()
    return dict(winc=winc, wgc=wgc, wuc=wuc, wdc=wdc, woc=woc, w1c=w1c, w2c=w2c, lnc=lnc, pvc=pvc, cb16=cb16, cf32=cf32)


def kernel(**inputs):
    x = np.ascontiguousarray(inputs["x"], dtype=np.float32)
    shared = host_prep(inputs)
    ncores = 8
    nseq = x.shape[0] // ncores
    nc = build(nseq)
    in_maps = []
    for c in range(ncores):
        m = dict(shared)
        m["x"] = np.ascontiguousarray(x[c * nseq:(c + 1) * nseq])
        in_maps.append(m)
    res = run_bass_kernel_spmd(nc, in_maps, core_ids=list(range(ncores)))
    return np.concatenate([r["out"] for r in res.results], axis=0).astype(np.float32)
```

```python
import numpy as np
import ml_dtypes
from contextlib import ExitStack
import concourse.bass as bass
import concourse.mybir as mybir
from concourse.bass_utils import run_bass_kernel_spmd

F32 = mybir.dt.float32
BF16 = mybir.dt.bfloat16
AF = mybir.ActivationFunctionType
ALU = mybir.AluOpType
AX = mybir.AxisListType

T = 2048
DM = 1024
NT = 16
DFF = 2816
NJ = 22
NEG = -30000.0
ALPHA = 2.0 ** 0.25
LAM_INIT = 0.2
DBG_BR = [1, 1, 1]
SLOPES = (2.0 ** (-8.0 * (np.arange(12) + 1) / 12)).astype(np.float32)
MMAX = []
for _s in SLOPES:
    _m = 15
    while _m > 0 and float(_s) * (128 * (_m - 1) + 1) > 144.0:
        _m -= 1
    MMAX.append(_m)

OFF_DQ, OFF_DK, OFF_DV, OFF_NQ = 0, 512, 1024, 1536
OFF_CK, OFF_CV, OFF_SK, OFF_SV, OFF_WK, OFF_WV, OFF_G = 2048, 2176, 2304, 2432, 2560, 2688, 2816
CH = []
for h in range(4):
    CH.append(([(OFF_DQ + h * 128, 128)], 'F', 0.125))
for h in range(4):
    CH.append(([(OFF_DK + h * 128, 128)], 'F', 1.0))
for j in range(4):
    CH.append(([(OFF_NQ + j * 64, 64), (OFF_NQ + (j + 4) * 64, 64)], 'F', 0.125))
CH.append(([(OFF_CK, 128)], 'F', 1.0))
CH.append(([(OFF_CV, 128)], 'F', 1.0))
CH.append(([(OFF_SK, 128)], 'F', 1.0))
CH.append(([(OFF_WK, 128)], 'F', 1.0))
for h in range(4):
    CH.append(([(OFF_DV + h * 128, 128)], 'T', 1.0))
CH.append(([(OFF_SV, 128)], 'T', 1.0))
CH.append(([(OFF_WV, 128)], 'T', 1.0))
CH.append(([(OFF_G, 24)], 'T', 1.0))
NCH = len(CH)
F_DQ, F_DK, F_NQ, F_CK, F_CV, F_SK, F_WK = 0, 4, 8, 12, 13, 14, 15

C_ID = 0
C_TLO = 128
C_THI = 256
C_MCMP = 384
C_E = C_MCMP + 2048
C_ALA = C_E + 2048
C_ALB = C_ALA + 12 * 128
C_OVL = C_ALB + 512
NC16 = C_OVL + 32
C_FB = 0
C_BT = 512
NC32 = C_BT + 192
P_LQ = 0
P_SUBG = 256
P_PEK = 384
P_PEV = 416
NPV = 448


def _split3(v):
    v = np.asarray(v, np.float64)
    a = v.astype(np.float32).astype(ml_dtypes.bfloat16)
    r = v - a.astype(np.float64)
    b = r.astype(np.float32).astype(ml_dtypes.bfloat16)
    r = r - b.astype(np.float64)
    c = r.astype(np.float32).astype(ml_dtypes.bfloat16)
    return a, b, c


def make_consts():
    cb = np.zeros((128, NC16), np.float32)
    cb[:, C_ID:C_ID + 128] = np.eye(128)
    r = np.arange(128)[:, None]
    j = np.arange(128)[None, :]
    cb[:, C_TLO:C_TLO + 128] = np.where(j < r, NEG, 0.0)
    cb[:, C_THI:C_THI + 128] = np.where(j >= r, NEG, 0.0)
    n = np.arange(127)[:, None]
    t = np.arange(T)[None, :]
    cb[:127, C_MCMP:C_MCMP + T] = np.where(t < 16 * n + 31, NEG, 0.0)
    s = np.arange(32)[:, None]
    m = np.arange(T)[None, :]
    cb[:32, C_E:C_E + T] = (m // 64 == s).astype(np.float32)
    cb16 = cb.astype(ml_dtypes.bfloat16)
    jj = np.arange(512)
    for si in range(12):
        sv = float(SLOPES[si])
        p = _split3(np.full(128, sv))
        q = _split3(sv * np.arange(128, dtype=np.float64))
        for k in range(3):
            cb16[k, C_ALA + si * 128:C_ALA + (si + 1) * 128] = p[k]
            cb16[3 + k, C_ALA + si * 128:C_ALA + (si + 1) * 128] = p[k]
            cb16[6 + k, C_ALA + si * 128:C_ALA + (si + 1) * 128] = q[k]
    for k in range(3):
        cb16[k, C_ALB:C_ALB + 512] = (-(jj // 2) * 2).astype(np.float32).astype(ml_dtypes.bfloat16)
        cb16[3 + k, C_ALB:C_ALB + 512] = (-(jj % 2)).astype(np.float32).astype(ml_dtypes.bfloat16)
        cb16[6 + k, C_ALB:C_ALB + 512] = np.ones(512, np.float32).astype(ml_dtypes.bfloat16)
    cs = np.arange(127)[:, None] * 16
    ss = np.arange(32)[None, :] * 64
    ovl = ((cs <= ss + 63) & (cs + 31 >= ss)).astype(np.float32)
    cb16[:127, C_OVL:C_OVL + 32] = ovl.astype(ml_dtypes.bfloat16)

    cf = np.zeros((128, NC32), np.float32)
    p = np.arange(128)
    for qt in range(16):
        cur = 2 * qt + p // 64
        for sb in range(32):
            v = np.zeros(128, np.float32)
            if sb == 0:
                v[:] = 1000.0
            v[sb == cur - 1] = 3000.0
            v[sb == cur] = 2000.0
            v[sb > cur] = -1000.0 - sb
            cf[:, C_FB + qt * 32 + sb] = v
    for si in range(12):
        for mm in range(16):
            cf[:, C_BT + si * 16 + mm] = -np.float32(SLOPES[si]) * 128.0 * mm
    return cb16, cf


class _Op:
    __slots__ = ('fn', 'waits', 'dma_key', 'needed', 'ms', 'dtok')

    def __init__(self, fn, waits, dma_key):
        self.fn = fn
        self.waits = waits
        self.dma_key = dma_key
        self.needed = False
        self.ms = 0


class Sched:
    COMPUTE = ('pe', 'act', 'dve', 'pool')
    ALL = ('pe', 'act', 'dve', 'pool', 'sp')

    def __init__(self, nc, es):
        self.nc = nc
        self.es = es
        self.ops = {e: [] for e in self.ALL}
        self.res = {}
        self.waited = {e: {} for e in self.ALL}
        self.dcount = {}
        self.dsem = {}
        self.esem = {e: es.enter_context(nc.semaphore('es_' + e)) for e in self.COMPUTE}

    def _need(self, eng, tok, waits):
        if tok is None:
            return
        key = (tok[0], tok[1])
        w = self.waited[eng]
        if w.get(key, 0) >= tok[2]:
            return
        w[key] = tok[2]
        waits.append(tok)
        if tok[0] == 'e':
            self.ops[tok[1]][tok[2] - 1].needed = True

    def add(self, eng, fn, reads=(), writes=(), dma_key=None, ndma=1):
        waits = []
        is_dma = dma_key is not None
        for r in reads:
            st = self.res.get(r)
            if st is not None:
                self._need(eng, st[0], waits)
        for w in writes:
            st = self.res.get(w)
            if st is not None:
                lw = st[0]
                if lw is not None and (is_dma or not (lw[0] == 'e' and lw[1] == eng)):
                    self._need(eng, lw, waits)
                for tk in st[1].values():
                    if is_dma or not (tk[0] == 'e' and tk[1] == eng):
                        self._need(eng, tk, waits)
        op = _Op(fn, waits, dma_key)
        self.ops[eng].append(op)
        if is_dma:
            if dma_key not in self.dsem:
                self.dsem[dma_key] = self.es.enter_context(self.nc.semaphore('ds_' + dma_key))
                self.dcount[dma_key] = 0
            self.dcount[dma_key] += 16 * ndma
            tok = ('d', dma_key, self.dcount[dma_key])
            op.dtok = self.dcount[dma_key]
        else:
            tok = ('e', eng, len(self.ops[eng]))
        for r in reads:
            st = self.res.setdefault(r, [None, {}])
            st[1][(tok[0], tok[1])] = tok
        for w in writes:
            self.res[w] = [tok, {}]
        return tok

    def barrier(self):
        toks = []
        for e in self.COMPUTE:
            n = len(self.ops[e])
            while n > 0 and self.ops[e][n - 1].fn is None:
                n -= 1
            if n > 0:
                toks.append(('e', e, n))
        for k, c in self.dcount.items():
            toks.append(('d', k, c))
        for e in self.ALL:
            waits = []
            for tk in toks:
                if tk[0] == 'e' and tk[1] == e:
                    continue
                self._need(e, tk, waits)
            self.ops[e].append(_Op(None, waits, None))

    def check_deadlock(self):
        ptr = {e: 0 for e in self.ALL}
        done_e = {e: 0 for e in self.COMPUTE}
        done_d = {}
        dcnt = {}
        progress = True
        while progress:
            progress = False
            for e in self.ALL:
                while ptr[e] < len(self.ops[e]):
                    op = self.ops[e][ptr[e]]
                    ok = True
                    for tk in op.waits:
                        if tk[0] == 'e':
                            if done_e[tk[1]] < tk[2]:
                                ok = False
                                break
                        else:
                            if done_d.get(tk[1], 0) < tk[2]:
                                ok = False
                                break
                    if not ok:
                        break
                    ptr[e] += 1
                    progress = True
                    if op.dma_key is not None:
                        pass
                    if e in done_e:
                        done_e[e] = ptr[e]
                    if op.dma_key is not None:
                        done_d[op.dma_key] = op.dtok
        stuck = {e: (ptr[e], len(self.ops[e])) for e in self.ALL if ptr[e] < len(self.ops[e])}
        return stuck

    def emit(self):
        nc = self.nc
        for e in self.COMPUTE:
            m = 0
            for op in self.ops[e]:
                if op.needed:
                    assert op.fn is not None
                    m += 1
                    op.ms = m

        def run(name):
            def f(eng):
                for op in self.ops[name]:
                    for tk in op.waits:
                        if tk[0] == 'e':
                            eng.wait_ge(self.esem[tk[1]], self.ops[tk[1]][tk[2] - 1].ms)
                        else:
                            eng.wait_ge(self.dsem[tk[1]], tk[2])
                    if op.fn is None:
                        continue
                    r = op.fn(eng)
                    if op.dma_key is not None:
                        for ins in r:
                            ins.then_inc(self.dsem[op.dma_key], 16)
                    elif op.needed:
                        r.then_inc(self.esem[name], 1)
            return f

        with nc.Block() as blk:
            blk.sync(run('sp'))
            blk.tensor(run('pe'))
            blk.vector(run('dve'))
            blk.scalar(run('act'))
            blk.gpsimd(run('pool'))


def build(nseq=4, dbg=False, stop_after=None):
    nc = bass.Bass("TRN2", target_bir_lowering=False)

    def DIN(name, shape, dt=F32):
        return nc.dram_tensor(name, shape, dt, kind="ExternalInput").ap()

    x = DIN("x", [nseq, T, DM])
    winc = DIN("winc", [NCH, 128, 1024])
    wgc = DIN("wgc", [NJ, 128, 1024])
    wuc = DIN("wuc", [NJ, 128, 1024])
    wdc = DIN("wdc", [NJ, 128, 1024])
    woc = DIN("woc", [128, 8192])
    w1c = DIN("w1c", [128, 8192])
    w2c = DIN("w2c", [128, 192])
    lnc = DIN("lnc", [4, DM])
    pvc = DIN("pvc", [128, NPV])
    cb16d = DIN("cb16", [128, NC16], BF16)
    cf32d = DIN("cf32", [128, NC32])
    out = nc.dram_tensor("out", [nseq, T, DM], F32, kind="ExternalOutput").ap()
    if dbg:
        dbg_o = nc.dram_tensor("dbg", [128, 35000], F32, kind="ExternalOutput").ap()

    es = ExitStack()
    S = Sched(nc, es)

    def SB(name, cols, dt=F32):
        return es.enter_context(nc.sbuf_tensor(name, [128, cols], dt))

    A1 = SB("arena1", 8192)
    A2 = SB("arena2", 23040)
    xT = A1[:, :].bitcast(BF16).rearrange("p (c t) -> p c t", c=8)
    w1b = A1[:, 0:4096].bitcast(BF16).rearrange("p (k l h) -> p k l h", k=2, l=32)
    cat = A1[:, :].bitcast(BF16).rearrange("p (t f) -> p t f", t=16)
    featT = [A2[:, i * 1024:(i + 1) * 1024].bitcast(BF16) for i in range(16)]
    o2 = 16384
    dvaug = A2[:, o2:o2 + 4160].bitcast(BF16).rearrange("p (t h e) -> p t h e", t=16, h=4)
    o2 += 4160
    svaug = A2[:, o2:o2 + 1056].bitcast(BF16).rearrange("p (t g e) -> p t g e", t=16, g=2)
    o2 += 1056
    wvaug = A2[:, o2:o2 + 1056].bitcast(BF16).rearrange("p (t g e) -> p t g e", t=16, g=2)
    o2 += 1056
    gates = A2[:, o2:o2 + 384].rearrange("p (t c) -> p t c", t=16)
    woutb = A2[:, 0:4096].bitcast(BF16).rearrange("p (c n) -> p c n", c=8)
    lnp = A2[:, 4096:8192].rearrange("p (a n) -> p a n", a=4)
    aT = A2[:, 8192:8192 + 5632].bitcast(BF16).rearrange("p (j t) -> p j t", j=NJ)
    x1 = A2[:, 13824:13824 + 4096].rearrange("p (t n) -> p t n", t=4)
    x1T = A2[:, 17920:17920 + 2048].bitcast(BF16).rearrange("p (c t) -> p c t", c=8)
    catT = A2[:, 19968:19968 + 512].bitcast(BF16).rearrange("p (c t) -> p c t", c=8)
    x1b = A2[:, 20480:20480 + 512].bitcast(BF16)

    acc = SB("acc", 2048)
    accv = acc[:, :].rearrange("p (t h d) -> p t h d", t=4, h=8)
    impacc = SB("impacc", 256)
    impv = impacc[:, :].rearrange("p (t g s) -> p t g s", t=4, g=2)
    wst = [SB("wst%d" % i, 1024) for i in range(3)]
    wbf = [SB("wbf%d" % i, 1024, BF16) for i in range(3)]
    xs = [SB("xs%d" % i, 1024) for i in range(2)]
    xb = SB("xb", 1024, BF16)
    pt = [SB("pt%d" % i, 512, BF16) for i in range(3)]
    cb = SB("cb", NC16, BF16)
    cf = SB("cf", NC32)
    pv = SB("pv", NPV)
    w2b = SB("w2b", 192, BF16)
    tmp0 = SB("tmp0", 512)
    tmpa = SB("tmpa", 512)
    tmpb = SB("tmpb", 512)
    sm = SB("sm", 256)
    selw = SB("selw", 256)
    selb = SB("selb", 256, BF16)
    selbT = SB("selbT", 1024, BF16)
    kcT = SB("kcT", 128, BF16)
    vcaug = SB("vcaug", 200, BF16)
    hid = SB("hid", 512, BF16)
    cmpt = [SB("cmpt%d" % i, 128, BF16) for i in range(2)]
    hpre = SB("hpre", 512)
    PS = [es.enter_context(nc.psum_tensor("ps%d" % i, [128, 512], F32)) for i in range(8)]

    selbv = selb[:, :].rearrange("p (t g s) -> p t g s", t=4, g=2)
    selbTv = selbT[:, :].rearrange("p (g q) -> p g q", g=2)
    vcv = vcaug[:, :].rearrange("p (g e) -> p g e", g=2)
    hidv = hid[:, :].rearrange("p (k n) -> p k n", k=4)

    ident = cb[:, C_ID:C_ID + 128]

    def dma(key, outs_ins, reads, writes):
        def fn(eng):
            return [eng.dma_start(out=o, in_=i) for (o, i) in outs_ins]
        return S.add('sp', fn, reads=reads, writes=writes, dma_key=key, ndma=len(outs_ins))

    def mm(o, lhsT, rhs, start, stop, reads, writes):
        S.add('pe', lambda e: e.matmul(o, lhsT=lhsT, rhs=rhs, start=start, stop=stop), reads=reads, writes=writes)

    def tr(o, i, reads, writes):
        S.add('pe', lambda e: e.transpose(out=o, in_=i, identity=ident), reads=reads + ['cb'], writes=writes)

    def act(o, i, func, reads, writes, bias=None, scale=None, accum=None):
        kw = {}
        if bias is not None:
            kw['bias'] = bias
        if scale is not None:
            kw['scale'] = scale
        if accum is not None:
            kw['accum_out'] = accum
        S.add('act', lambda e: e.activation(out=o, in_=i, func=func, **kw), reads=reads, writes=writes)

    def cp(eng, o, i, reads, writes):
        if eng == 'act':
            S.add(eng, lambda e: e.activation(out=o, in_=i, func=AF.Copy), reads=reads, writes=writes)
        else:
            S.add(eng, lambda e: e.tensor_copy(out=o, in_=i), reads=reads, writes=writes)

    def tt(eng, o, a, b, op, reads, writes):
        S.add(eng, lambda e: e.tensor_tensor(out=o, in0=a, in1=b, op=op), reads=reads, writes=writes)

    def ts(eng, o, a, s1, s2, op0, op1, reads, writes):
        if op1 is None:
            S.add(eng, lambda e: e.tensor_scalar(out=o, in0=a, scalar1=s1, scalar2=None, op0=op0), reads=reads, writes=writes)
        else:
            S.add(eng, lambda e: e.tensor_scalar(out=o, in0=a, scalar1=s1, scalar2=s2, op0=op0, op1=op1), reads=reads, writes=writes)

    def stt(eng, o, a, sc, b, op0, op1, reads, writes):
        S.add(eng, lambda e: e.scalar_tensor_tensor(out=o, in0=a, scalar=sc, in1=b, op0=op0, op1=op1), reads=reads, writes=writes)

    dma('c0', [(cb[:, :], cb16d)], [], ['cb'])
    dma('c1', [(cf[:, :], cf32d), (pv[:, :], pvc)], [], ['cf', 'pv'])
    dma('c2', [(wst[0][:, 0:192], w2c)], [], ['wst0'])
    cp('pool', w2b[:, :], wst[0][:, 0:192], ['wst0'], ['w2b'])
    tt('dve', sm[:, 0:64], pv[:, 0:64], pv[:, 64:128], ALU.mult, ['pv'], ['sm_a'])
    tt('dve', sm[:, 64:128], pv[:, 128:192], pv[:, 192:256], ALU.mult, ['pv'], ['sm_b'])
    S.add('dve', lambda e: e.tensor_reduce(out=sm[:, 128:129], in_=sm[:, 0:64], axis=AX.X, op=ALU.add), reads=['sm_a'], writes=['sm_c'])
    S.add('dve', lambda e: e.tensor_reduce(out=sm[:, 129:130], in_=sm[:, 64:128], axis=AX.X, op=ALU.add), reads=['sm_b'], writes=['sm_d'])
    act(sm[:, 130:131], sm[:, 128:129], AF.Exp, ['sm_c'], ['sm_e'])
    act(sm[:, 131:132], sm[:, 129:130], AF.Exp, ['sm_d'], ['sm_f'])
    tt('dve', sm[:, 132:133], sm[:, 131:132], sm[:, 130:131], ALU.subtract, ['sm_e', 'sm_f'], ['sm_g'])
    ts('dve', sm[:, 133:134], sm[:, 132:133], -LAM_INIT, None, ALU.add, None, ['sm_g'], ['neglam'])
    neglam = sm[:, 133:134]
    ts('dve', sm[:, 0:128], pv[:, P_SUBG:P_SUBG + 128], 1.0 - LAM_INIT, None, ALU.mult, None, ['pv', 'sm_a', 'sm_b', 'sm_c', 'sm_d'], ['gsub'])
    gsub = sm[:, 0:128]

    rr = {'ps': 0, 'w': 0, 'pt': 0, 'ev': 0}

    def evac_eng():
        rr['ev'] += 1
        return 'act' if rr['ev'] % 2 else 'dve'

    for s in range(nseq):
        for t in range(NT):
            b = t % 2
            dma('xs%d' % b, [(xs[b][:, :], x[s, t * 128:(t + 1) * 128, :])], [], ['xs%d' % b])
            cp('pool', xb[:, :], xs[b][:, :], ['xs%d' % b], ['xb'])
            pb = PS[t % 2]
            pbb = pb[:, :].bitcast(BF16)
            for c in range(8):
                tr(pbb[:, c * 128:(c + 1) * 128], xb[:, c * 128:(c + 1) * 128], ['xb'], ['ps%d' % (t % 2)])
            cp(evac_eng(), xT[:, :, t * 128:(t + 1) * 128], pbb.rearrange("p (c t) -> p c t", c=8),
               ['ps%d' % (t % 2)], ['xT%d' % (t // 4)])
        S.add('pool', lambda e: e.memset(dvaug[:, :, :, 128:130], 1.0), reads=[], writes=['dvaug'])
        S.add('pool', lambda e: e.memset(svaug[:, :, :, 64:66], 1.0), reads=[], writes=['svaug'])
        S.add('pool', lambda e: e.memset(wvaug[:, :, :, 64:66], 1.0), reads=[], writes=['wvaug'])
        for ci in range(NCH):
            pieces, kind, scale = CH[ci]
            wb = rr['w'] % 3
            rr['w'] += 1
            dma('wst%d' % wb, [(wst[wb][:, :], winc[ci])], [], ['wst%d' % wb])
            cp('pool', wbf[wb][:, :], wst[wb][:, :], ['wst%d' % wb], ['wbf%d' % wb])
            wv = wbf[wb][:, :].rearrange("p (c n) -> p c n", c=8)
            if kind == 'F':
                for tg in range(4):
                    pi = rr['ps'] % 4
                    rr['ps'] += 1
                    for k in range(8):
                        mm(PS[pi][:, :], wv[:, k, :], xT[:, k, tg * 512:(tg + 1) * 512], k == 0, k == 7,
                           ['wbf%d' % wb, 'xT%d' % tg], ['ps%d' % pi])
                    dst = featT[ci][:, tg * 512:(tg + 1) * 512]
                    if evac_eng() == 'act':
                        act(dst, PS[pi][:, :], AF.Copy, ['ps%d' % pi], ['featT%d' % ci], scale=float(scale))
                    else:
                        ts('dve', dst, PS[pi][:, :], float(scale), None, ALU.mult, None, ['ps%d' % pi], ['featT%d' % ci])
            else:
                for t4 in range(4):
                    pi = rr['ps'] % 4
                    rr['ps'] += 1
                    for ti in range(4):
                        t = t4 * 4 + ti
                        for k in range(8):
                            mm(PS[pi][:, ti * 128:(ti + 1) * 128], xT[:, k, t * 128:(t + 1) * 128], wv[:, k, :], k == 0, k == 7,
                               ['wbf%d' % wb, 'xT%d' % t4], ['ps%d' % pi])
                    src = PS[pi][:, :].rearrange("p (t n) -> p t n", t=4)
                    tsl = slice(t4 * 4, t4 * 4 + 4)
                    if ci < 20:
                        h = ci - 16
                        cp(evac_eng(), dvaug[:, tsl, h, 0:128], src, ['ps%d' % pi], ['dvaug'])
                    elif ci == 20:
                        cp(evac_eng(), svaug[:, tsl, :, 0:64], src.rearrange("p t (g e) -> p t g e", g=2), ['ps%d' % pi], ['svaug'])
                    elif ci == 21:
                        cp(evac_eng(), wvaug[:, tsl, :, 0:64], src.rearrange("p t (g e) -> p t g e", g=2), ['ps%d' % pi], ['wvaug'])
                    else:
                        act(gates[:, tsl, :], src[:, :, 0:24], AF.Sigmoid, ['ps%d' % pi], ['gates'])
        S.barrier()
        if stop_after == 'P':
            break

        for q8 in range(8):
            wb = rr['w'] % 3
            rr['w'] += 1
            dma('wst%d' % wb, [(wst[wb][:, :], w1c[:, q8 * 1024:(q8 + 1) * 1024])], [], ['wst%d' % wb])
            cp('pool', A1[:, q8 * 512:(q8 + 1) * 512].bitcast(BF16), wst[wb][:, :], ['wst%d' % wb], ['w1b'])
        for kv in range(2):
            src = featT[F_CK + kv]
            pcol = P_PEK if kv == 0 else P_PEV
            for l in range(32):
                cb_i = l % 2
                ts('dve' if l % 2 else 'pool', cmpt[cb_i][:, 0:127], src[:, l:l + 16 * 126 + 1:16], pv[:, pcol + l:pcol + l + 1], None,
                   ALU.add, None, ['featT%d' % (F_CK + kv), 'pv'], ['cmpt%d' % cb_i])
                for g in range(2):
                    mm(PS[4 + g][:, 0:127], w1b[g * 64:(g + 1) * 64, kv, l, :], cmpt[cb_i][g * 64:(g + 1) * 64, 0:127],
                       l == 0, l == 31, ['w1b', 'cmpt%d' % cb_i], ['ps%d' % (4 + g)])
            for g in range(2):
                hp = hpre[:, g * 128:g * 128 + 127]
                h2 = hpre[:, 256 + g * 128:256 + g * 128 + 127]
                cp('dve', hp, PS[4 + g][:, 0:127], ['ps%d' % (4 + g)], ['hp%d' % g])
                tt('dve', h2, hp, hp, ALU.mult, ['hp%d' % g], ['h2%d' % g])
                ts('dve', h2, h2, 0.044715, 1.0, ALU.mult, ALU.add, ['h2%d' % g], ['h2b%d' % g])
                tt('dve', h2, h2, hp, ALU.mult, ['h2b%d' % g, 'hp%d' % g], ['h2c%d' % g])
                act(h2, h2, AF.Sigmoid, ['h2c%d' % g], ['h2d%d' % g], scale=1.5957691216057308)
                tt('dve', hidv[:, kv * 2 + g, 0:127], h2, hp, ALU.mult, ['h2d%d' % g, 'hp%d' % g], ['hid'])
        for g in range(2):
            mm(PS[6][:, 0:127], w2b[:, 0:128], hidv[:, g, 0:127], True, True, ['w2b', 'hid'], ['ps6'])
            cp('dve', kcT[g * 64:(g + 1) * 64, 0:127], PS[6][g * 64:(g + 1) * 64, 0:127], ['ps6'], ['kcT'])
            mm(PS[7][0:127, 0:64], hidv[:, 2 + g, 0:127], w2b[:, 128:192], True, True, ['w2b', 'hid'], ['ps7'])
            cp('dve', vcv[0:127, g, 0:64], PS[7][0:127, 0:64], ['ps7'], ['vcaug'])
            S.add('pool', lambda e, g=g: e.memset(vcv[0:127, g, 64:65], 1.0), reads=[], writes=['vcaug'])
            cp('pool', vcv[0:127, g, 65:97], cb[0:127, C_OVL:C_OVL + 32], ['cb'], ['vcaug'])
        S.barrier()
        if stop_after == 'C':
            break

        steps = []

        def attn_qg(qg, kT_of, qT_of, v_of, slope_i, span, far_mask, selg, o_slot, kres, qres, vres, ores, post):
            kts = range(max(0, 4 * qg - span), 4 * qg + 4)
            started = set()
            first_step = len(steps)
            for kt in kts:
                qlo = max(4 * qg, kt)
                qhi = min(4 * qg + 3, kt + span)
                if qhi < qlo:
                    continue
                q0 = qlo * 128
                n = (qhi - qlo + 1) * 128
                info = {}

                def score(kt=kt, qlo=qlo, qhi=qhi, q0=q0, n=n, info=info):
                    pi = rr['ps'] % 3
                    rr['ps'] += 1
                    st = PS[pi]
                    sres = 'ps%d' % pi
                    mm(st[:, 0:n], kT_of(kt), qT_of(q0, n), True, False, [kres, qres], [sres])
                    if kt == qlo:
                        mm(st[:, 0:128], ident, cb[:, C_TLO:C_TLO + 128], False, False, ['cb'], [sres])
                    if far_mask and qhi == kt + span:
                        mm(st[:, n - 128:n], ident, cb[:, C_THI:C_THI + 128], False, False, ['cb'], [sres])
                    if selg is not None:
                        mm(st[:, 0:n], cb[0:32, C_E + kt * 128:C_E + (kt + 1) * 128], selbTv[0:32, selg, q0 - qg * 512:q0 - qg * 512 + n],
                           False, False, ['cb', 'selbT'], [sres])
                    mm(st[:, 0:n], cb[0:9, C_ALA + slope_i * 128:C_ALA + (slope_i + 1) * 128], cb[0:9, C_ALB:C_ALB + n],
                       False, True, ['cb'], [sres])
                    pj = rr['pt'] % 3
                    rr['pt'] += 1
                    m_off = qlo - kt
                    act(pt[pj][:, 0:n], st[:, 0:n], AF.Exp, [sres, 'cf'], ['pt%d' % pj],
                        bias=cf[:, C_BT + slope_i * 16 + m_off:C_BT + slope_i * 16 + m_off + 1])
                    info['pj'] = pj

                def pv(kt=kt, qlo=qlo, qhi=qhi, info=info):
                    pj = info['pj']
                    for qt in range(qlo, qhi + 1):
                        oap, obank = o_slot(qt - 4 * qg)
                        first = obank not in started
                        started.add(obank)
                        mm(oap, pt[pj][:, (qt - qlo) * 128:(qt - qlo + 1) * 128], v_of(kt), first, kt == qt,
                           ['pt%d' % pj, vres], ores)
                steps.append([score, pv, None])
            steps[-1][2] = post

        def flush_steps():
            prev = None
            for stp in steps:
                stp[0]()
                if prev is not None:
                    prev[1]()
                    if prev[2] is not None:
                        prev[2]()
                prev = stp
            if prev is not None:
                prev[1]()
                if prev[2] is not None:
                    prev[2]()
            del steps[:]

        for h in range(4):
            for qg in range(4):
                tsl = slice(qg * 4, qg * 4 + 4)
                for c in range(2):
                    ob = 4 + 2 * ((h * 8 + qg * 2 + c) % 2)
                    ores = ['o%d' % ob, 'o%d' % (ob + 1)]

                    def o_slot(qi, ob=ob):
                        return PS[ob + qi // 2][:, (qi % 2) * 256:(qi % 2) * 256 + 129], ob + qi // 2

                    def post(h=h, qg=qg, c=c, ob=ob, ores=ores, tsl=tsl):
                        for half in range(2):
                            ov = PS[ob + half][:, :].rearrange("p (a e) -> p a e", a=2)
                            rs = sm[:, 140 + c * 4 + half * 2:140 + c * 4 + half * 2 + 2]
                            S.add('dve', lambda e, rs=rs, ov=ov: e.reciprocal(out=rs.unsqueeze(2), in_=ov[:, :, 128:129]), reads=ores, writes=['rs%d%d' % (c, half)])
                            dst = (tmp0 if c == 0 else tmpa)[:, half * 256:(half + 1) * 256].rearrange("p (a e) -> p a e", a=2)
                            tt('dve', dst, ov[:, :, 0:128], rs.unsqueeze(2).to_broadcast([128, 2, 128]), ALU.mult,
                               ores + ['rs%d%d' % (c, half)], ['tmp%d' % c])
                        if c == 0:
                            return
                        stt('dve', tmpb[:, :], tmpa[:, :], neglam, tmp0[:, :], ALU.mult, ALU.add, ['tmp0', 'tmp1', 'neglam'], ['tmpb'])
                        for qi in range(4):
                            act(tmpa[:, qi * 128:(qi + 1) * 128], tmpb[:, qi * 128:(qi + 1) * 128], AF.Square, ['tmpb'], ['tmp1'],
                                accum=sm[:, 150 + qi:151 + qi])
                        ts('dve', sm[:, 156:160], sm[:, 150:154], 1.0 / 128.0, 1e-5, ALU.mult, ALU.add, ['tmp1'], ['rms_a'])
                        act(sm[:, 156:160], sm[:, 156:160], AF.Sqrt, ['rms_a'], ['rms_b'])
                        S.add('dve', lambda e: e.reciprocal(out=sm[:, 160:164], in_=sm[:, 156:160]), reads=['rms_b'], writes=['rms_c'])
                        tb3 = tmpb[:, :].rearrange("p (a e) -> p a e", a=4)
                        tt('dve', tb3, tb3, sm[:, 160:164].unsqueeze(2).to_broadcast([128, 4, 128]), ALU.mult, ['tmpb', 'rms_c'], ['tmpb', 'tmpb2'])
                        tt('pool', cat[:, tsl, h * 128:(h + 1) * 128], tb3, gsub.unsqueeze(1).to_broadcast([128, 4, 128]), ALU.mult,
                           ['tmpb2', 'gsub'], ['cat'])
                    attn_qg(qg,
                            lambda kt, h=h, c=c: featT[F_DK + h][c * 64:(c + 1) * 64, kt * 128:(kt + 1) * 128],
                            lambda q0, n, h=h, c=c: featT[F_DQ + h][c * 64:(c + 1) * 64, q0:q0 + n],
                            lambda kt, h=h: dvaug[:, kt, h, 0:129],
                            h, MMAX[h], False, None, o_slot,
                            'featT%d' % (F_DK + h), 'featT%d' % (F_DQ + h), 'dvaug', ores, post)
        if stop_after == 'D':
            flush_steps()
            break

        for qg in range(4):
            tsl = slice(qg * 4, qg * 4 + 4)
            for h in range(8):
                info = {}

                def cscore(h=h, qg=qg, info=info):
                    base = (h // 4) * 64
                    j = h % 4
                    pi = rr['ps'] % 3
                    rr['ps'] += 1
                    st = PS[pi]
                    sres = 'ps%d' % pi
                    mm(st[0:127, :], kcT[base:base + 64, 0:127], featT[F_NQ + j][base:base + 64, qg * 512:(qg + 1) * 512], True, False,
                       ['kcT', 'featT%d' % (F_NQ + j)], [sres])
                    mm(st[0:127, :], cb[0:127, C_ID:C_ID + 127], cb[0:127, C_MCMP + qg * 512:C_MCMP + (qg + 1) * 512], False, True, ['cb'], [sres])
                    pj = rr['pt'] % 3
                    rr['pt'] += 1
                    act(pt[pj][0:127, :], st[0:127, :], AF.Exp, [sres], ['pt%d' % pj])
                    info['pj'] = pj

                def cpv(h=h, info=info):
                    g = h // 4
                    pj = info['pj']
                    ob = 4 + (h % 2)
                    ov = PS[ob][:, 0:400].rearrange("p (a e) -> p a e", a=4)
                    for qi in range(4):
                        mm(ov[:, qi, 0:97], pt[pj][0:127, qi * 128:(qi + 1) * 128], vcv[0:127, g, 0:97], True, True, ['pt%d' % pj, 'vcaug'], ['o%d' % ob])

                def cpost(h=h, qg=qg, tsl=tsl):
                    g = h // 4
                    j = h % 4
                    ob = 4 + (h % 2)
                    ores = 'o%d' % ob
                    ov = PS[ob][:, 0:400].rearrange("p (a e) -> p a e", a=4)
                    rs = sm[:, 170:174]
                    ts('dve', rs.unsqueeze(2), ov[:, :, 64:65], 1e-30, None, ALU.max, None, [ores], ['rsA'])
                    S.add('dve', lambda e, rs=rs: e.reciprocal(out=rs, in_=rs), reads=['rsA'], writes=['rsB'])
                    sg = sm[:, 174:178]
                    tt('dve', sg.unsqueeze(2), rs.unsqueeze(2), gates[:, tsl, h * 3:h * 3 + 1], ALU.mult, ['rsB', 'gates'], ['sgA'])
                    tt('dve', accv[:, :, h, :], ov[:, :, 0:64], sg.unsqueeze(2).to_broadcast([128, 4, 64]), ALU.mult, [ores, 'sgA'], ['acc%d' % h])
                    if not DBG_BR[0]:
                        S.add('dve', lambda e, h=h: e.memset(accv[:, :, h, :], 0.0), reads=[], writes=['acc%d' % h])
                    if j == 0:
                        tt('dve', impv[:, :, g, :], ov[:, :, 65:97], rs.unsqueeze(2).to_broadcast([128, 4, 32]), ALU.mult, [ores, 'rsB'], ['imp%d' % g])
                    else:
                        tt('dve', tmp0[:, 0:128].rearrange("p (a e) -> p a e", a=4), ov[:, :, 65:97], rs.unsqueeze(2).to_broadcast([128, 4, 32]),
                           ALU.mult, [ores, 'rsB'], ['tmp0'])
                        tt('dve', impv[:, :, g, :], impv[:, :, g, :], tmp0[:, 0:128].rearrange("p (a e) -> p a e", a=4), ALU.add,
                           ['tmp0', 'imp%d' % g], ['imp%d' % g])
                    if j != 3:
                        return
                    for qi in range(4):
                        qt = qg * 4 + qi
                        val = selw[:, 0:32]
                        tt('dve', val, impv[:, qi, g, :], cf[:, C_FB + qt * 32:C_FB + (qt + 1) * 32], ALU.add, ['imp%d' % g, 'cf'], ['sw_a'])
                        S.add('dve', lambda e: e.max(out=selw[:, 32:40], in_=selw[:, 0:32]), reads=['sw_a'], writes=['sw_b'])
                        S.add('dve', lambda e: e.match_replace(out=selw[:, 64:96], in_to_replace=selw[:, 32:40], in_values=selw[:, 0:32], imm_value=-1e9),
                              reads=['sw_a', 'sw_b'], writes=['sw_c'])
                        S.add('dve', lambda e: e.max(out=selw[:, 40:48], in_=selw[:, 64:96]), reads=['sw_c'], writes=['sw_d'])
                        S.add('dve', lambda e: e.tensor_reduce(out=selw[:, 48:49], in_=selw[:, 40:48], axis=AX.X, op=ALU.min), reads=['sw_d'], writes=['sw_e'])
                        ts('dve', selbv[:, qi, g, :], val, selw[:, 48:49], NEG, ALU.is_lt, ALU.mult, ['sw_a', 'sw_e'], ['selb'])
                        pbb = PS[3][:, :].bitcast(BF16)
                        tr(pbb[0:32, (g * 4 + qi) * 128:(g * 4 + qi + 1) * 128], selbv[:, qi, g, :], ['selb'], ['ps3'])
                    cp('act', selbTv[0:32, g, :], PS[3][:, :].bitcast(BF16)[0:32, g * 512:(g + 1) * 512], ['ps3'], ['selbT'])
                steps.append([cscore, cpv, cpost])
            for br in (1, 0):
                for h in range(8):
                    g = h // 4
                    base = g * 64
                    j = h % 4
                    ob = 4 + (h % 4)
                    ores = ['o%d' % ob]
                    ovv = PS[ob][:, 0:264].rearrange("p (a e) -> p a e", a=4)

                    def o_slot(qi, ovv=ovv, ob=ob):
                        return ovv[:, qi, 0:65], ob
                    kf = F_SK if br == 0 else F_WK
                    va = svaug if br == 0 else wvaug

                    def post(h=h, br=br, ovv=ovv, ores=ores, tsl=tsl):
                        rs = sm[:, 180 + (h % 2) * 8:184 + (h % 2) * 8]
                        sg = sm[:, 184 + (h % 2) * 8:188 + (h % 2) * 8]
                        rk = 'rs%d' % (h % 2)
                        S.add('dve', lambda e, rs=rs, ovv=ovv: e.reciprocal(out=rs.unsqueeze(2), in_=ovv[:, :, 64:65]), reads=ores, writes=[rk + 'A'])
                        tt('dve', sg.unsqueeze(2), rs.unsqueeze(2), gates[:, tsl, h * 3 + 1 + br:h * 3 + 2 + br], ALU.mult, [rk + 'A', 'gates'], [rk + 'B'])
                        tv = (tmpa if h % 2 else tmpb)[:, 0:256].rearrange("p (a e) -> p a e", a=4)
                        tk = 'tv%d' % (h % 2)
                        tt('dve', tv, ovv[:, :, 0:64], sg.unsqueeze(2).to_broadcast([128, 4, 64]), ALU.mult, ores + [rk + 'B'], [tk])
                        if not DBG_BR[1 + br]:
                            S.add('dve', lambda e, tv=tv: e.memset(tv, 0.0), reads=[], writes=[tk])
                        if br == 1:
                            tt('pool', accv[:, :, h, :], accv[:, :, h, :], tv, ALU.add, [tk, 'acc%d' % h], ['acc%d' % h])
                        else:
                            tt('pool', cat[:, tsl, 512 + h * 64:512 + (h + 1) * 64], accv[:, :, h, :], tv, ALU.add, [tk, 'acc%d' % h], ['cat'])
                    attn_qg(qg,
                            lambda kt, kf=kf, base=base: featT[kf][base:base + 64, kt * 128:(kt + 1) * 128],
                            lambda q0, n, j=j, base=base: featT[F_NQ + j][base:base + 64, q0:q0 + n],
                            lambda kt, va=va, g=g: va[:, kt, g, 0:65],
                            4 + h, (4 if br == 1 else MMAX[4 + h]), br == 1, (g if br == 0 else None), o_slot,
                            'featT%d' % kf, 'featT%d' % (F_NQ + j), 'svaug' if br == 0 else 'wvaug', ores, post)
        flush_steps()
        S.barrier()
        if stop_after == 'N':
            break

        for q8 in range(8):
            wb = rr['w'] % 3
            rr['w'] += 1
            dma('wst%d' % wb, [(wst[wb][:, :], woc[:, q8 * 1024:(q8 + 1) * 1024])], [], ['wst%d' % wb])
            cp('pool', woutb[:, q8, :], wst[wb][:, :], ['wst%d' % wb], ['woutb'])
        dma('lnp', [(lnp[:, a, :], lnc[a].partition_broadcast(128)) for a in range(4)], [], ['lnp'])

        def layer_norm(src, srck, dst, dstk, gi, eng_aff):
            S.add('dve', lambda e: e.bn_stats(out=sm[:, 200:206], in_=src[:, 0:512]), reads=[srck], writes=['bn_a'])
            S.add('dve', lambda e: e.bn_stats(out=sm[:, 206:212], in_=src[:, 512:1024]), reads=[srck], writes=['bn_b'])
            S.add('dve', lambda e: e.bn_aggr(out=sm[:, 212:214], in_=sm[:, 200:212]), reads=['bn_a', 'bn_b'], writes=['bn_c'])
            ts('dve', sm[:, 214:215], sm[:, 213:214], 1e-5, None, ALU.add, None, ['bn_c'], ['bn_d'])
            act(sm[:, 214:215], sm[:, 214:215], AF.Sqrt, ['bn_d'], ['bn_e'])
            S.add('dve', lambda e: e.reciprocal(out=sm[:, 215:216], in_=sm[:, 214:215]), reads=['bn_e'], writes=['bn_f'])
            ts('dve', src, src, sm[:, 212:213], sm[:, 215:216], ALU.subtract, ALU.mult, [srck, 'bn_c', 'bn_f'], [srck])
            tt(eng_aff, src, src, lnp[:, gi, :], ALU.mult, [srck, 'lnp'], [srck])
            tt(eng_aff, dst, src, lnp[:, gi + 1, :], ALU.add, [srck, 'lnp'], [dstk])

        for tg in range(4):
            for ti in range(4):
                t = tg * 4 + ti
                b = t % 2
                dma('xs%d' % b, [(xs[b][:, :], x[s, t * 128:(t + 1) * 128, :])], [], ['xs%d' % b])
                pbb = PS[3][:, :].bitcast(BF16)
                for c in range(8):
                    tr(pbb[:, c * 128:(c + 1) * 128], cat[:, t, c * 128:(c + 1) * 128], ['cat'], ['ps3'])
                cp('act', catT[:, :, :], pbb.rearrange("p (c t) -> p c t", c=8), ['ps3'], ['catT'])
                for hf in range(2):
                    for k in range(8):
                        mm(PS[hf][:, :], catT[:, k, :], woutb[:, k, hf * 512:(hf + 1) * 512], k == 0, k == 7, ['catT', 'woutb'], ['ps%d' % hf])
                    stt('dve', xs[b][:, hf * 512:(hf + 1) * 512], xs[b][:, hf * 512:(hf + 1) * 512], float(ALPHA), PS[hf][:, :],
                        ALU.mult, ALU.add, ['xs%d' % b, 'ps%d' % hf], ['xs%d' % b])
                layer_norm(xs[b][:, :], 'xs%d' % b, x1[:, ti, :], 'x1_%d' % ti, 0, 'pool')
                cp('act', x1b[:, :], x1[:, ti, :], ['x1_%d' % ti], ['x1b'])
                pbb2 = PS[2][:, :].bitcast(BF16)
                for c in range(8):
                    tr(pbb2[:, c * 128:(c + 1) * 128], x1b[:, c * 128:(c + 1) * 128], ['x1b'], ['ps2'])
                cp('act', x1T[:, :, ti * 128:(ti + 1) * 128], pbb2.rearrange("p (c t) -> p c t", c=8), ['ps2'], ['x1T'])
            for j in range(NJ):
                wb1 = rr['w'] % 3
                rr['w'] += 1
                dma('wst%d' % wb1, [(wst[wb1][:, :], wgc[j])], [], ['wst%d' % wb1])
                cp('pool', wbf[wb1][:, :], wst[wb1][:, :], ['wst%d' % wb1], ['wbf%d' % wb1])
                wb2 = rr['w'] % 3
                rr['w'] += 1
                dma('wst%d' % wb2, [(wst[wb2][:, :], wuc[j])], [], ['wst%d' % wb2])
                cp('pool', wbf[wb2][:, :], wst[wb2][:, :], ['wst%d' % wb2], ['wbf%d' % wb2])
                wgv = wbf[wb1][:, :].rearrange("p (c n) -> p c n", c=8)
                wuv = wbf[wb2][:, :].rearrange("p (c n) -> p c n", c=8)
                pg = (j % 2) * 2
                for k in range(8):
                    mm(PS[pg][:, :], wgv[:, k, :], x1T[:, k, :], k == 0, k == 7, ['wbf%d' % wb1, 'x1T'], ['ps%d' % pg])
                for k in range(8):
                    mm(PS[pg + 1][:, :], wuv[:, k, :], x1T[:, k, :], k == 0, k == 7, ['wbf%d' % wb2, 'x1T'], ['ps%d' % (pg + 1)])
                sgb = tmp0 if j % 2 == 0 else tmpa
                sgk = 'tmp0' if j % 2 == 0 else 'tmp1'
                act(sgb[:, :], PS[pg][:, :], AF.Silu, ['ps%d' % pg], [sgk])
                tt('dve', aT[:, j, :], sgb[:, :], PS[pg + 1][:, :], ALU.mult, [sgk, 'ps%d' % (pg + 1)], ['aT'])
            for hf in range(2):
                for j in range(NJ):
                    wb = rr['w'] % 3
                    rr['w'] += 1
                    dma('wst%d' % wb, [(wst[wb][:, 0:512], wdc[j, :, hf * 512:(hf + 1) * 512])], [], ['wst%d' % wb])
                    cp('pool', wbf[wb][:, 0:512], wst[wb][:, 0:512], ['wst%d' % wb], ['wbf%d' % wb])
                    for ti in range(4):
                        mm(PS[4 + ti][:, :], aT[:, j, ti * 128:(ti + 1) * 128], wbf[wb][:, 0:512], j == 0, j == NJ - 1,
                           ['aT', 'wbf%d' % wb], ['o%d' % (4 + ti)])
                for ti in range(4):
                    stt('dve', x1[:, ti, hf * 512:(hf + 1) * 512], x1[:, ti, hf * 512:(hf + 1) * 512], float(ALPHA), PS[4 + ti][:, :],
                        ALU.mult, ALU.add, ['x1_%d' % ti, 'o%d' % (4 + ti)], ['x1_%d' % ti])
            for ti in range(4):
                t = tg * 4 + ti
                ob = ti % 2
                layer_norm(x1[:, ti, :], 'x1_%d' % ti, x1[:, ti, :], 'x1_%d' % ti, 2, 'pool')
                dma('out%d' % ti, [(out[s, t * 128:(t + 1) * 128, :], x1[:, ti, :])], ['x1_%d' % ti], ['outd'])
        S.barrier()

    if dbg:
        S.barrier()
        lst = []
        for c0 in range(0, 8192, 2048):
            lst.append((dbg_o[:, c0:c0 + 2048], A1[:, c0:c0 + 2048]))
        for c0 in range(0, 23040, 2048):
            c1 = min(23040, c0 + 2048)
            lst.append((dbg_o[:, 8192 + c0:8192 + c1], A2[:, c0:c1]))
        lst += [(dbg_o[:, 31232:31232 + 2048], acc[:, :]), (dbg_o[:, 33280:33280 + 256], sm[:, :]), (dbg_o[:, 33536:33536 + 256], impacc[:, :])]
        lst += [(dbg_o[:, 33792:33856], kcT[:, :].bitcast(F32)), (dbg_o[:, 33856:33956], vcaug[:, :].bitcast(F32)),
                (dbg_o[:, 33956:34212], hid[:, :].bitcast(F32)), (dbg_o[:, 34212:34340], selb[:, :].bitcast(F32)),
                (dbg_o[:, 34340:34340 + 512], selbT[:, :].bitcast(F32))]
        dma('dbg', lst, [], ['dbgd'])
    S.barrier()
    stuck = S.check_deadlock()
    assert not stuck, stuck
    S.emit()
    es.close()
    return nc


def host_prep(inputs):
    f = lambda a: np.ascontiguousarray(a, dtype=np.float32)
    w_in = f(inputs["w_in"][0])
    winc = np.zeros((NCH, 128, 8, 128), np.float32)
    for ci, (pieces, kind, scale) in enumerate(CH):
        o = 0
        for (c0, ncol) in pieces:
            blk = w_in[:, c0:c0 + ncol].reshape(8, 128, ncol).transpose(1, 0, 2)
            winc[ci, :, :, o:o + ncol] = blk
            o += ncol
    winc = winc.reshape(NCH, 128, 1024)

    def chunk_cols(w):
        return np.ascontiguousarray(w.reshape(8, 128, NJ, 128).transpose(2, 1, 0, 3)).reshape(NJ, 128, 1024)
    wgc = chunk_cols(f(inputs["w_gate"][0]))
    wuc = chunk_cols(f(inputs["w_up"][0]))
    wdc = np.ascontiguousarray(f(inputs["w_down"][0]).reshape(NJ, 128, 1024))
    woc = np.ascontiguousarray(f(inputs["w_out"][0]).reshape(8, 128, 1024).transpose(1, 0, 2)).reshape(128, 8192)
    w1 = np.stack([f(inputs["cmp_w1_k"][0]), f(inputs["cmp_w1_v"][0])], 0)
    w1 = w1.reshape(2, 32, 64, 128).transpose(2, 0, 1, 3).reshape(64, 8192)
    w1c = np.ascontiguousarray(np.concatenate([w1, w1], 0))
    w2k = f(inputs["cmp_w2_k"][0])
    w2v = f(inputs["cmp_w2_v"][0])
    w2c = np.ascontiguousarray(np.concatenate([w2k, w2k, w2v], 1))
    lnc = np.ascontiguousarray(np.stack([f(inputs["ln1_g"][0]), f(inputs["ln1_b"][0]), f(inputs["ln2_g"][0]), f(inputs["ln2_b"][0])], 0))
    pvc = np.zeros((128, NPV), np.float32)
    pvc[:, 0:64] = f(inputs["diff_lq1"][0])[None, :]
    pvc[:, 64:128] = f(inputs["diff_lk1"][0])[None, :]
    pvc[:, 128:192] = f(inputs["diff_lq2"][0])[None, :]
    pvc[:, 192:256] = f(inputs["diff_lk2"][0])[None, :]
    pvc[:, P_SUBG:P_SUBG + 128] = f(inputs["diff_subln_g"][0])[None, :]
    pek = f(inputs["cmp_pe_k"][0]).T
    pev = f(inputs["cmp_pe_v"][0]).T
    pvc[:, P_PEK:P_PEK + 32] = np.concatenate([pek, pek], 0)
    pvc[:, P_PEV:P_PEV + 32] = np.concatenate([pev, pev], 0)
    cb16, cf32 = make_consts()
    return dict(winc=winc, wgc=wgc, wuc=wuc, wdc=wdc, woc=woc, w1c=w1c, w2c=w2c, lnc=lnc, pvc=pvc, cb16=cb16, cf32=cf32)


def kernel(**inputs):
    x = np.ascontiguousarray(inputs["x"], dtype=np.float32)
    shared = host_prep(inputs)
    ncores = 8
    nseq = x.shape[0] // ncores
    nc = build(nseq)
    in_maps = []
    for c in range(ncores):
        m = dict(shared)
        m["x"] = np.ascontiguousarray(x[c * nseq:(c + 1) * nseq])
        in_maps.append(m)
    res = run_bass_kernel_spmd(nc, in_maps, core_ids=list(range(ncores)))
    return np.concatenate([r["out"] for r in res.results], axis=0).astype(np.float32)
```

```python
import numpy as np
import ml_dtypes
from contextlib import ExitStack
import concourse.bass as bass
import concourse.mybir as mybir
from concourse.bass_utils import run_bass_kernel_spmd

F32 = mybir.dt.float32
BF16 = mybir.dt.bfloat16
AF = mybir.ActivationFunctionType
ALU = mybir.AluOpType
AX = mybir.AxisListType

T = 2048
DM = 1024
NT = 16
DFF = 2816
NJ = 22
NEG = -30000.0
ALPHA = 2.0 ** 0.25
LAM_INIT = 0.2
DBG_BR = [1, 1, 1]
SLOPES = (2.0 ** (-8.0 * (np.arange(12) + 1) / 12)).astype(np.float32)
MMAX = []
for _s in SLOPES:
    _m = 15
    while _m > 0 and float(_s) * (128 * (_m - 1) + 1) > 144.0:
        _m -= 1
    MMAX.append(_m)

OFF_DQ, OFF_DK, OFF_DV, OFF_NQ = 0, 512, 1024, 1536
OFF_CK, OFF_CV, OFF_SK, OFF_SV, OFF_WK, OFF_WV, OFF_G = 2048, 2176, 2304, 2432, 2560, 2688, 2816
CH = []
for h in range(4):
    CH.append(([(OFF_DQ + h * 128, 128)], 'F', 0.125))
for h in range(4):
    CH.append(([(OFF_DK + h * 128, 128)], 'F', 1.0))
for j in range(4):
    CH.append(([(OFF_NQ + j * 64, 64), (OFF_NQ + (j + 4) * 64, 64)], 'F', 0.125))
CH.append(([(OFF_CK, 128)], 'F', 1.0))
CH.append(([(OFF_CV, 128)], 'F', 1.0))
CH.append(([(OFF_SK, 128)], 'F', 1.0))
CH.append(([(OFF_WK, 128)], 'F', 1.0))
for h in range(4):
    CH.append(([(OFF_DV + h * 128, 128)], 'T', 1.0))
CH.append(([(OFF_SV, 128)], 'T', 1.0))
CH.append(([(OFF_WV, 128)], 'T', 1.0))
CH.append(([(OFF_G, 24)], 'T', 1.0))
NCH = len(CH)
F_DQ, F_DK, F_NQ, F_CK, F_CV, F_SK, F_WK = 0, 4, 8, 12, 13, 14, 15

C_ID = 0
C_TLO = 128
C_THI = 256
C_MCMP = 384
C_E = C_MCMP + 2048
C_ALA = C_E + 2048
C_ALB = C_ALA + 12 * 128
C_OVL = C_ALB + 512
NC16 = C_OVL + 32
C_FB = 0
C_BT = 512
NC32 = C_BT + 192
P_LQ = 0
P_SUBG = 256
P_PEK = 384
P_PEV = 416
NPV = 448


def _split3(v):
    v = np.asarray(v, np.float64)
    a = v.astype(np.float32).astype(ml_dtypes.bfloat16)
    r = v - a.astype(np.float64)
    b = r.astype(np.float32).astype(ml_dtypes.bfloat16)
    r = r - b.astype(np.float64)
    c = r.astype(np.float32).astype(ml_dtypes.bfloat16)
    return a, b, c


def make_consts():
    cb = np.zeros((128, NC16), np.float32)
    cb[:, C_ID:C_ID + 128] = np.eye(128)
    r = np.arange(128)[:, None]
    j = np.arange(128)[None, :]
    cb[:, C_TLO:C_TLO + 128] = np.where(j < r, NEG, 0.0)
    cb[:, C_THI:C_THI + 128] = np.where(j >= r, NEG, 0.0)
    n = np.arange(127)[:, None]
    t = np.arange(T)[None, :]
    cb[:127, C_MCMP:C_MCMP + T] = np.where(t < 16 * n + 31, NEG, 0.0)
    s = np.arange(32)[:, None]
    m = np.arange(T)[None, :]
    cb[:32, C_E:C_E + T] = (m // 64 == s).astype(np.float32)
    cb16 = cb.astype(ml_dtypes.bfloat16)
    jj = np.arange(512)
    for si in range(12):
        sv = float(SLOPES[si])
        p = _split3(np.full(128, sv))
        q = _split3(sv * np.arange(128, dtype=np.float64))
        for k in range(3):
            cb16[k, C_ALA + si * 128:C_ALA + (si + 1) * 128] = p[k]
            cb16[3 + k, C_ALA + si * 128:C_ALA + (si + 1) * 128] = p[k]
            cb16[6 + k, C_ALA + si * 128:C_ALA + (si + 1) * 128] = q[k]
    for k in range(3):
        cb16[k, C_ALB:C_ALB + 512] = (-(jj // 2) * 2).astype(np.float32).astype(ml_dtypes.bfloat16)
        cb16[3 + k, C_ALB:C_ALB + 512] = (-(jj % 2)).astype(np.float32).astype(ml_dtypes.bfloat16)
        cb16[6 + k, C_ALB:C_ALB + 512] = np.ones(512, np.float32).astype(ml_dtypes.bfloat16)
    cs = np.arange(127)[:, None] * 16
    ss = np.arange(32)[None, :] * 64
    ovl = ((cs <= ss + 63) & (cs + 31 >= ss)).astype(np.float32)
    cb16[:127, C_OVL:C_OVL + 32] = ovl.astype(ml_dtypes.bfloat16)

    cf = np.zeros((128, NC32), np.float32)
    p = np.arange(128)
    for qt in range(16):
        cur = 2 * qt + p // 64
        for sb in range(32):
            v = np.zeros(128, np.float32)
            if sb == 0:
                v[:] = 1000.0
            v[sb == cur - 1] = 3000.0
            v[sb == cur] = 2000.0
            v[sb > cur] = -1000.0 - sb
            cf[:, C_FB + qt * 32 + sb] = v
    for si in range(12):
        for mm in range(16):
            cf[:, C_BT + si * 16 + mm] = -np.float32(SLOPES[si]) * 128.0 * mm
    return cb16, cf


class _Op:
    __slots__ = ('fn', 'waits', 'dma_key', 'needed', 'ms', 'dtok')

    def __init__(self, fn, waits, dma_key):
        self.fn = fn
        self.waits = waits
        self.dma_key = dma_key
        self.needed = False
        self.ms = 0


class Sched:
    COMPUTE = ('pe', 'act', 'dve', 'pool')
    ALL = ('pe', 'act', 'dve', 'pool', 'sp')

    def __init__(self, nc, es):
        self.nc = nc
        self.es = es
        self.ops = {e: [] for e in self.ALL}
        self.res = {}
        self.waited = {e: {} for e in self.ALL}
        self.dcount = {}
        self.dsem = {}
        self.esem = {e: es.enter_context(nc.semaphore('es_' + e)) for e in self.COMPUTE}

    def _need(self, eng, tok, waits):
        if tok is None:
            return
        key = (tok[0], tok[1])
        w = self.waited[eng]
        if w.get(key, 0) >= tok[2]:
            return
        w[key] = tok[2]
        waits.append(tok)
        if tok[0] == 'e':
            self.ops[tok[1]][tok[2] - 1].needed = True

    def add(self, eng, fn, reads=(), writes=(), dma_key=None, ndma=1):
        waits = []
        is_dma = dma_key is not None
        for r in reads:
            st = self.res.get(r)
            if st is not None:
                self._need(eng, st[0], waits)
        for w in writes:
            st = self.res.get(w)
            if st is not None:
                lw = st[0]
                if lw is not None and (is_dma or not (lw[0] == 'e' and lw[1] == eng)):
                    self._need(eng, lw, waits)
                for tk in st[1].values():
                    if is_dma or not (tk[0] == 'e' and tk[1] == eng):
                        self._need(eng, tk, waits)
        op = _Op(fn, waits, dma_key)
        self.ops[eng].append(op)
        if is_dma:
            if dma_key not in self.dsem:
                self.dsem[dma_key] = self.es.enter_context(self.nc.semaphore('ds_' + dma_key))
                self.dcount[dma_key] = 0
            self.dcount[dma_key] += 16 * ndma
            tok = ('d', dma_key, self.dcount[dma_key])
            op.dtok = self.dcount[dma_key]
        else:
            tok = ('e', eng, len(self.ops[eng]))
        for r in reads:
            st = self.res.setdefault(r, [None, {}])
            st[1][(tok[0], tok[1])] = tok
        for w in writes:
            self.res[w] = [tok, {}]
        return tok

    def barrier(self):
        toks = []
        for e in self.COMPUTE:
            n = len(self.ops[e])
            while n > 0 and (self.ops[e][n - 1].fn is None or self.ops[e][n - 1].dma_key is not None):
                n -= 1
            if n > 0:
                toks.append(('e', e, n))
        for k, c in self.dcount.items():
            toks.append(('d', k, c))
        for e in self.ALL:
            waits = []
            for tk in toks:
                if tk[0] == 'e' and tk[1] == e:
                    continue
                self._need(e, tk, waits)
            self.ops[e].append(_Op(None, waits, None))

    def check_deadlock(self):
        ptr = {e: 0 for e in self.ALL}
        done_e = {e: 0 for e in self.COMPUTE}
        done_d = {}
        dcnt = {}
        progress = True
        while progress:
            progress = False
            for e in self.ALL:
                while ptr[e] < len(self.ops[e]):
                    op = self.ops[e][ptr[e]]
                    ok = True
                    for tk in op.waits:
                        if tk[0] == 'e':
                            if done_e[tk[1]] < tk[2]:
                                ok = False
                                break
                        else:
                            if done_d.get(tk[1], 0) < tk[2]:
                                ok = False
                                break
                    if not ok:
                        break
                    ptr[e] += 1
                    progress = True
                    if op.dma_key is not None:
                        pass
                    if e in done_e:
                        done_e[e] = ptr[e]
                    if op.dma_key is not None:
                        done_d[op.dma_key] = op.dtok
        stuck = {e: (ptr[e], len(self.ops[e])) for e in self.ALL if ptr[e] < len(self.ops[e])}
        return stuck

    def emit(self):
        nc = self.nc
        for e in self.COMPUTE:
            m = 0
            for op in self.ops[e]:
                if op.needed:
                    assert op.fn is not None and op.dma_key is None
                    m += 1
                    op.ms = m

        def run(name):
            def f(eng):
                for op in self.ops[name]:
                    for tk in op.waits:
                        if tk[0] == 'e':
                            eng.wait_ge(self.esem[tk[1]], self.ops[tk[1]][tk[2] - 1].ms)
                        else:
                            eng.wait_ge(self.dsem[tk[1]], tk[2])
                    if op.fn is None:
                        continue
                    r = op.fn(eng)
                    if op.dma_key is not None:
                        for ins in r:
                            ins.then_inc(self.dsem[op.dma_key], 16)
                    elif op.needed:
                        r.then_inc(self.esem[name], 1)
            return f

        with nc.Block() as blk:
            blk.sync(run('sp'))
            blk.tensor(run('pe'))
            blk.vector(run('dve'))
            blk.scalar(run('act'))
            blk.gpsimd(run('pool'))


def build(nseq=4, dbg=False, stop_after=None):
    nc = bass.Bass("TRN2", target_bir_lowering=False)

    def DIN(name, shape, dt=F32):
        return nc.dram_tensor(name, shape, dt, kind="ExternalInput").ap()

    x = DIN("x", [nseq, T, DM])
    winc = DIN("winc", [NCH, 128, 1024])
    wgc = DIN("wgc", [NJ, 128, 1024])
    wuc = DIN("wuc", [NJ, 128, 1024])
    wdc = DIN("wdc", [NJ, 128, 1024])
    woc = DIN("woc", [128, 8192])
    w1c = DIN("w1c", [128, 8192])
    w2c = DIN("w2c", [128, 192])
    lnc = DIN("lnc", [4, DM])
    pvc = DIN("pvc", [128, NPV])
    cb16d = DIN("cb16", [128, NC16], BF16)
    cf32d = DIN("cf32", [128, NC32])
    out = nc.dram_tensor("out", [nseq, T, DM], F32, kind="ExternalOutput").ap()
    if dbg:
        dbg_o = nc.dram_tensor("dbg", [128, 35000], F32, kind="ExternalOutput").ap()

    def DSCR(name, shape):
        return nc.dram_tensor(name, shape, BF16, kind="Internal").ap()
    wins = DSCR("wins", [NCH, 128, 1024])
    wgs = DSCR("wgs", [NJ, 128, 1024])
    wus = DSCR("wus", [NJ, 128, 1024])
    wds = DSCR("wds", [NJ, 128, 1024])
    wos = DSCR("wos", [128, 8192])
    w1s = DSCR("w1s", [128, 8192])

    es = ExitStack()
    S = Sched(nc, es)

    def SB(name, cols, dt=F32):
        return es.enter_context(nc.sbuf_tensor(name, [128, cols], dt))

    A1 = SB("arena1", 8192)
    A2 = SB("arena2", 23040)
    xT = A1[:, :].bitcast(BF16).rearrange("p (c t) -> p c t", c=8)
    w1b = A1[:, 0:4096].bitcast(BF16).rearrange("p (k l h) -> p k l h", k=2, l=32)
    cat = A1[:, :].bitcast(BF16).rearrange("p (t f) -> p t f", t=16)
    featT = [A2[:, i * 1024:(i + 1) * 1024].bitcast(BF16) for i in range(16)]
    o2 = 16384
    dvaug = A2[:, o2:o2 + 4160].bitcast(BF16).rearrange("p (t h e) -> p t h e", t=16, h=4)
    o2 += 4160
    svaug = A2[:, o2:o2 + 1056].bitcast(BF16).rearrange("p (t g e) -> p t g e", t=16, g=2)
    o2 += 1056
    wvaug = A2[:, o2:o2 + 1056].bitcast(BF16).rearrange("p (t g e) -> p t g e", t=16, g=2)
    o2 += 1056
    gates = A2[:, o2:o2 + 384].rearrange("p (t c) -> p t c", t=16)
    woutb = A2[:, 0:4096].bitcast(BF16).rearrange("p (c n) -> p c n", c=8)
    lnp = A2[:, 4096:8192].rearrange("p (a n) -> p a n", a=4)
    aT = A2[:, 8192:8192 + 5632].bitcast(BF16).rearrange("p (j t) -> p j t", j=NJ)
    x1 = A2[:, 13824:13824 + 4096].rearrange("p (t n) -> p t n", t=4)
    x1T = A2[:, 17920:17920 + 2048].bitcast(BF16).rearrange("p (c t) -> p c t", c=8)
    catT = A2[:, 19968:19968 + 512].bitcast(BF16).rearrange("p (c t) -> p c t", c=8)
    x1b = A2[:, 20480:20480 + 512].bitcast(BF16)

    acc = SB("acc", 2048)
    accv = acc[:, :].rearrange("p (t h d) -> p t h d", t=4, h=8)
    impacc = SB("impacc", 256)
    impv = impacc[:, :].rearrange("p (t g s) -> p t g s", t=4, g=2)
    wst = [SB("wst%d" % i, 256) for i in range(1)]
    wbf = [SB("wbf%d" % i, 1024, BF16) for i in range(6)]
    xs = [SB("xs%d" % i, 1024) for i in range(2)]
    xb = [SB("xb%d" % i, 1024, BF16) for i in range(2)]
    pt = [SB("pt%d" % i, 512, BF16) for i in range(3)]
    cb = SB("cb", NC16, BF16)
    cf = SB("cf", NC32)
    pv = SB("pv", NPV)
    w2b = SB("w2b", 192, BF16)
    tmp0 = SB("tmp0", 512)
    tmpa = SB("tmpa", 512)
    tmpb = SB("tmpb", 512)
    sm = SB("sm", 256)
    selw = SB("selw", 256)
    selb = SB("selb", 256, BF16)
    selbT = SB("selbT", 1024, BF16)
    kcT = SB("kcT", 128, BF16)
    vcaug = SB("vcaug", 200, BF16)
    hid = SB("hid", 512, BF16)
    cmpt = [SB("cmpt%d" % i, 128, BF16) for i in range(2)]
    hpre = SB("hpre", 512)
    PS = [es.enter_context(nc.psum_tensor("ps%d" % i, [128, 512], F32)) for i in range(8)]

    selbv = selb[:, :].rearrange("p (t g s) -> p t g s", t=4, g=2)
    selbTv = selbT[:, :].rearrange("p (g q) -> p g q", g=2)
    vcv = vcaug[:, :].rearrange("p (g e) -> p g e", g=2)
    hidv = hid[:, :].rearrange("p (k n) -> p k n", k=4)

    ident = cb[:, C_ID:C_ID + 128]

    def dma(key, outs_ins, reads, writes, eng='sp'):
        def fn(e):
            return [e.dma_start(out=o, in_=i) for (o, i) in outs_ins]
        return S.add(eng, fn, reads=reads, writes=writes, dma_key=key, ndma=len(outs_ins))

    def mm(o, lhsT, rhs, start, stop, reads, writes):
        S.add('pe', lambda e: e.matmul(o, lhsT=lhsT, rhs=rhs, start=start, stop=stop), reads=reads, writes=writes)

    def tr(o, i, reads, writes):
        S.add('pe', lambda e: e.transpose(out=o, in_=i, identity=ident), reads=reads + ['cb'], writes=writes)

    def act(o, i, func, reads, writes, bias=None, scale=None, accum=None):
        kw = {}
        if bias is not None:
            kw['bias'] = bias
        if scale is not None:
            kw['scale'] = scale
        if accum is not None:
            kw['accum_out'] = accum
        S.add('act', lambda e: e.activation(out=o, in_=i, func=func, **kw), reads=reads, writes=writes)

    def cp(eng, o, i, reads, writes):
        if eng == 'act':
            S.add(eng, lambda e: e.activation(out=o, in_=i, func=AF.Copy), reads=reads, writes=writes)
        else:
            S.add(eng, lambda e: e.tensor_copy(out=o, in_=i), reads=reads, writes=writes)

    def tt(eng, o, a, b, op, reads, writes):
        S.add(eng, lambda e: e.tensor_tensor(out=o, in0=a, in1=b, op=op), reads=reads, writes=writes)

    def ts(eng, o, a, s1, s2, op0, op1, reads, writes):
        if op1 is None:
            S.add(eng, lambda e: e.tensor_scalar(out=o, in0=a, scalar1=s1, scalar2=None, op0=op0), reads=reads, writes=writes)
        else:
            S.add(eng, lambda e: e.tensor_scalar(out=o, in0=a, scalar1=s1, scalar2=s2, op0=op0, op1=op1), reads=reads, writes=writes)

    def stt(eng, o, a, sc, b, op0, op1, reads, writes):
        S.add(eng, lambda e: e.scalar_tensor_tensor(out=o, in0=a, scalar=sc, in1=b, op0=op0, op1=op1), reads=reads, writes=writes)

    dma('c0', [(cb[:, :], cb16d)], [], ['cb'])
    dma('c1', [(cf[:, :], cf32d), (pv[:, :], pvc)], [], ['cf', 'pv'])
    dma('c2', [(wst[0][:, 0:192], w2c)], [], ['wst0'])
    cp('pool', w2b[:, :], wst[0][:, 0:192], ['wst0'], ['w2b'])
    dma('pcA', [(wins[0:8], winc[0:8]), (wins[8:16], winc[8:16]), (wins[16:NCH], winc[16:NCH]), (w1s, w1c)], [], ['pcA'], eng='pool')
    dma('pcB', [(wos, woc)] + [(dst[j0:j0 + 6], src[j0:j0 + 6]) for (dst, src) in ((wgs, wgc), (wus, wuc), (wds, wdc)) for j0 in (0, 6, 12)]
        + [(dst[18:NJ], src[18:NJ]) for (dst, src) in ((wgs, wgc), (wus, wuc), (wds, wdc))], [], ['pcB'], eng='pool')
    tt('dve', sm[:, 0:64], pv[:, 0:64], pv[:, 64:128], ALU.mult, ['pv'], ['sm_a'])
    tt('dve', sm[:, 64:128], pv[:, 128:192], pv[:, 192:256], ALU.mult, ['pv'], ['sm_b'])
    S.add('dve', lambda e: e.tensor_reduce(out=sm[:, 128:129], in_=sm[:, 0:64], axis=AX.X, op=ALU.add), reads=['sm_a'], writes=['sm_c'])
    S.add('dve', lambda e: e.tensor_reduce(out=sm[:, 129:130], in_=sm[:, 64:128], axis=AX.X, op=ALU.add), reads=['sm_b'], writes=['sm_d'])
    act(sm[:, 130:131], sm[:, 128:129], AF.Exp, ['sm_c'], ['sm_e'])
    act(sm[:, 131:132], sm[:, 129:130], AF.Exp, ['sm_d'], ['sm_f'])
    tt('dve', sm[:, 132:133], sm[:, 131:132], sm[:, 130:131], ALU.subtract, ['sm_e', 'sm_f'], ['sm_g'])
    ts('dve', sm[:, 133:134], sm[:, 132:133], -LAM_INIT, None, ALU.add, None, ['sm_g'], ['neglam'])
    neglam = sm[:, 133:134]
    ts('dve', sm[:, 0:128], pv[:, P_SUBG:P_SUBG + 128], 1.0 - LAM_INIT, None, ALU.mult, None, ['pv', 'sm_a', 'sm_b', 'sm_c', 'sm_d'], ['gsub'])
    gsub = sm[:, 0:128]

    rr = {'ps': 0, 'w': 0, 'pt': 0, 'ev': 0}

    def evac_eng():
        rr['ev'] += 1
        return 'act' if rr['ev'] % 2 else 'dve'

    for s in range(nseq):
        for t in range(NT):
            b = t % 2
            dma('xb%d' % b, [(xb[b][:, :], x[s, t * 128:(t + 1) * 128, :])], [], ['xb%d' % b], eng='pool')
            pb = PS[t % 2]
            pbb = pb[:, :].bitcast(BF16)
            for c in range(8):
                tr(pbb[:, c * 128:(c + 1) * 128], xb[b][:, c * 128:(c + 1) * 128], ['xb%d' % b], ['ps%d' % (t % 2)])
            cp(evac_eng(), xT[:, :, t * 128:(t + 1) * 128], pbb.rearrange("p (c t) -> p c t", c=8),
               ['ps%d' % (t % 2)], ['xT%d' % (t // 4)])
        S.add('pool', lambda e: e.memset(dvaug[:, :, :, 128:130], 1.0), reads=[], writes=['dvaug'])
        S.add('pool', lambda e: e.memset(svaug[:, :, :, 64:66], 1.0), reads=[], writes=['svaug'])
        S.add('pool', lambda e: e.memset(wvaug[:, :, :, 64:66], 1.0), reads=[], writes=['wvaug'])
        for ci in range(NCH):
            pieces, kind, scale = CH[ci]
            wb = rr['w'] % 6
            rr['w'] += 1
            dma('wbf%d' % wb, [(wbf[wb][:, :], wins[ci])], ['pcA'], ['wbf%d' % wb])
            wv = wbf[wb][:, :].rearrange("p (c n) -> p c n", c=8)
            if kind == 'F':
                for tg in range(4):
                    pi = rr['ps'] % 4
                    rr['ps'] += 1
                    for k in range(8):
                        mm(PS[pi][:, :], wv[:, k, :], xT[:, k, tg * 512:(tg + 1) * 512], k == 0, k == 7,
                           ['wbf%d' % wb, 'xT%d' % tg], ['ps%d' % pi])
                    dst = featT[ci][:, tg * 512:(tg + 1) * 512]
                    if evac_eng() == 'act':
                        act(dst, PS[pi][:, :], AF.Copy, ['ps%d' % pi], ['featT%d' % ci], scale=float(scale))
                    else:
                        ts('dve', dst, PS[pi][:, :], float(scale), None, ALU.mult, None, ['ps%d' % pi], ['featT%d' % ci])
            else:
                for t4 in range(4):
                    pi = rr['ps'] % 4
                    rr['ps'] += 1
                    for ti in range(4):
                        t = t4 * 4 + ti
                        for k in range(8):
                            mm(PS[pi][:, ti * 128:(ti + 1) * 128], xT[:, k, t * 128:(t + 1) * 128], wv[:, k, :], k == 0, k == 7,
                               ['wbf%d' % wb, 'xT%d' % t4], ['ps%d' % pi])
                    src = PS[pi][:, :].rearrange("p (t n) -> p t n", t=4)
                    tsl = slice(t4 * 4, t4 * 4 + 4)
                    if ci < 20:
                        h = ci - 16
                        cp(evac_eng(), dvaug[:, tsl, h, 0:128], src, ['ps%d' % pi], ['dvaug'])
                    elif ci == 20:
                        cp(evac_eng(), svaug[:, tsl, :, 0:64], src.rearrange("p t (g e) -> p t g e", g=2), ['ps%d' % pi], ['svaug'])
                    elif ci == 21:
                        cp(evac_eng(), wvaug[:, tsl, :, 0:64], src.rearrange("p t (g e) -> p t g e", g=2), ['ps%d' % pi], ['wvaug'])
                    else:
                        act(gates[:, tsl, :], src[:, :, 0:24], AF.Sigmoid, ['ps%d' % pi], ['gates'])
        S.barrier()
        if stop_after == 'P':
            break

        dma('w1b', [(A1[:, 0:4096].bitcast(BF16), w1s)], ['pcA'], ['w1b'])
        for kv in range(2):
            src = featT[F_CK + kv]
            pcol = P_PEK if kv == 0 else P_PEV
            for l in range(32):
                cb_i = l % 2
                ts('dve' if l % 2 else 'pool', cmpt[cb_i][:, 0:127], src[:, l:l + 16 * 126 + 1:16], pv[:, pcol + l:pcol + l + 1], None,
                   ALU.add, None, ['featT%d' % (F_CK + kv), 'pv'], ['cmpt%d' % cb_i])
                for g in range(2):
                    mm(PS[4 + g][:, 0:127], w1b[g * 64:(g + 1) * 64, kv, l, :], cmpt[cb_i][g * 64:(g + 1) * 64, 0:127],
                       l == 0, l == 31, ['w1b', 'cmpt%d' % cb_i], ['ps%d' % (4 + g)])
            for g in range(2):
                hp = hpre[:, g * 128:g * 128 + 127]
                h2 = hpre[:, 256 + g * 128:256 + g * 128 + 127]
                cp('dve', hp, PS[4 + g][:, 0:127], ['ps%d' % (4 + g)], ['hp%d' % g])
                tt('dve', h2, hp, hp, ALU.mult, ['hp%d' % g], ['h2%d' % g])
                ts('dve', h2, h2, 0.044715, 1.0, ALU.mult, ALU.add, ['h2%d' % g], ['h2b%d' % g])
                tt('dve', h2, h2, hp, ALU.mult, ['h2b%d' % g, 'hp%d' % g], ['h2c%d' % g])
                act(h2, h2, AF.Sigmoid, ['h2c%d' % g], ['h2d%d' % g], scale=1.5957691216057308)
                tt('dve', hidv[:, kv * 2 + g, 0:127], h2, hp, ALU.mult, ['h2d%d' % g, 'hp%d' % g], ['hid'])
        for g in range(2):
            mm(PS[6][:, 0:127], w2b[:, 0:128], hidv[:, g, 0:127], True, True, ['w2b', 'hid'], ['ps6'])
            cp('dve', kcT[g * 64:(g + 1) * 64, 0:127], PS[6][g * 64:(g + 1) * 64, 0:127], ['ps6'], ['kcT'])
            mm(PS[7][0:127, 0:64], hidv[:, 2 + g, 0:127], w2b[:, 128:192], True, True, ['w2b', 'hid'], ['ps7'])
            cp('dve', vcv[0:127, g, 0:64], PS[7][0:127, 0:64], ['ps7'], ['vcaug'])
            S.add('pool', lambda e, g=g: e.memset(vcv[0:127, g, 64:65], 1.0), reads=[], writes=['vcaug'])
            cp('pool', vcv[0:127, g, 65:97], cb[0:127, C_OVL:C_OVL + 32], ['cb'], ['vcaug'])
        S.barrier()
        if stop_after == 'C':
            break

        steps = []

        def attn_qg(qg, kT_of, qT_of, v_of, slope_i, span, far_mask, selg, o_slot, kres, qres, vres, ores, post):
            kts = range(max(0, 4 * qg - span), 4 * qg + 4)
            started = set()
            first_step = len(steps)
            for kt in kts:
                qlo = max(4 * qg, kt)
                qhi = min(4 * qg + 3, kt + span)
                if qhi < qlo:
                    continue
                q0 = qlo * 128
                n = (qhi - qlo + 1) * 128
                info = {}

                def score(kt=kt, qlo=qlo, qhi=qhi, q0=q0, n=n, info=info):
                    pi = rr['ps'] % 3
                    rr['ps'] += 1
                    st = PS[pi]
                    sres = 'ps%d' % pi
                    mm(st[:, 0:n], kT_of(kt), qT_of(q0, n), True, False, [kres, qres], [sres])
                    if kt == qlo:
                        mm(st[:, 0:128], ident, cb[:, C_TLO:C_TLO + 128], False, False, ['cb'], [sres])
                    if far_mask and qhi == kt + span:
                        mm(st[:, n - 128:n], ident, cb[:, C_THI:C_THI + 128], False, False, ['cb'], [sres])
                    if selg is not None:
                        mm(st[:, 0:n], cb[0:32, C_E + kt * 128:C_E + (kt + 1) * 128], selbTv[0:32, selg, q0 - qg * 512:q0 - qg * 512 + n],
                           False, False, ['cb', 'selbT'], [sres])
                    mm(st[:, 0:n], cb[0:9, C_ALA + slope_i * 128:C_ALA + (slope_i + 1) * 128], cb[0:9, C_ALB:C_ALB + n],
                       False, True, ['cb'], [sres])
                    pj = rr['pt'] % 3
                    rr['pt'] += 1
                    m_off = qlo - kt
                    act(pt[pj][:, 0:n], st[:, 0:n], AF.Exp, [sres, 'cf'], ['pt%d' % pj],
                        bias=cf[:, C_BT + slope_i * 16 + m_off:C_BT + slope_i * 16 + m_off + 1])
                    info['pj'] = pj

                def pv(kt=kt, qlo=qlo, qhi=qhi, info=info):
                    pj = info['pj']
                    for qt in range(qlo, qhi + 1):
                        oap, obank = o_slot(qt - 4 * qg)
                        first = obank not in started
                        started.add(obank)
                        mm(oap, pt[pj][:, (qt - qlo) * 128:(qt - qlo + 1) * 128], v_of(kt), first, kt == qt,
                           ['pt%d' % pj, vres], ores)
                steps.append([score, pv, None])
            steps[-1][2] = post

        def flush_steps():
            prev = None
            for stp in steps:
                stp[0]()
                if prev is not None:
                    prev[1]()
                    if prev[2] is not None:
                        prev[2]()
                prev = stp
            if prev is not None:
                prev[1]()
                if prev[2] is not None:
                    prev[2]()
            del steps[:]

        for h in range(4):
            for qg in range(4):
                tsl = slice(qg * 4, qg * 4 + 4)
                for c in range(2):
                    ob = 4 + 2 * ((h * 8 + qg * 2 + c) % 2)
                    ores = ['o%d' % ob, 'o%d' % (ob + 1)]

                    def o_slot(qi, ob=ob):
                        return PS[ob + qi // 2][:, (qi % 2) * 256:(qi % 2) * 256 + 129], ob + qi // 2

                    def post(h=h, qg=qg, c=c, ob=ob, ores=ores, tsl=tsl):
                        for half in range(2):
                            ov = PS[ob + half][:, :].rearrange("p (a e) -> p a e", a=2)
                            rs = sm[:, 140 + c * 4 + half * 2:140 + c * 4 + half * 2 + 2]
                            S.add('dve', lambda e, rs=rs, ov=ov: e.reciprocal(out=rs.unsqueeze(2), in_=ov[:, :, 128:129]), reads=ores, writes=['rs%d%d' % (c, half)])
                            dst = (tmp0 if c == 0 else tmpa)[:, half * 256:(half + 1) * 256].rearrange("p (a e) -> p a e", a=2)
                            tt('dve', dst, ov[:, :, 0:128], rs.unsqueeze(2).to_broadcast([128, 2, 128]), ALU.mult,
                               ores + ['rs%d%d' % (c, half)], ['tmp%d' % c])
                        if c == 0:
                            return
                        stt('dve', tmpb[:, :], tmpa[:, :], neglam, tmp0[:, :], ALU.mult, ALU.add, ['tmp0', 'tmp1', 'neglam'], ['tmpb'])
                        for qi in range(4):
                            act(tmpa[:, qi * 128:(qi + 1) * 128], tmpb[:, qi * 128:(qi + 1) * 128], AF.Square, ['tmpb'], ['tmp1'],
                                accum=sm[:, 150 + qi:151 + qi])
                        ts('dve', sm[:, 156:160], sm[:, 150:154], 1.0 / 128.0, 1e-5, ALU.mult, ALU.add, ['tmp1'], ['rms_a'])
                        act(sm[:, 156:160], sm[:, 156:160], AF.Sqrt, ['rms_a'], ['rms_b'])
                        S.add('dve', lambda e: e.reciprocal(out=sm[:, 160:164], in_=sm[:, 156:160]), reads=['rms_b'], writes=['rms_c'])
                        tb3 = tmpb[:, :].rearrange("p (a e) -> p a e", a=4)
                        tt('dve', tb3, tb3, sm[:, 160:164].unsqueeze(2).to_broadcast([128, 4, 128]), ALU.mult, ['tmpb', 'rms_c'], ['tmpb', 'tmpb2'])
                        tt('pool', cat[:, tsl, h * 128:(h + 1) * 128], tb3, gsub.unsqueeze(1).to_broadcast([128, 4, 128]), ALU.mult,
                           ['tmpb2', 'gsub'], ['cat'])
                    attn_qg(qg,
                            lambda kt, h=h, c=c: featT[F_DK + h][c * 64:(c + 1) * 64, kt * 128:(kt + 1) * 128],
                            lambda q0, n, h=h, c=c: featT[F_DQ + h][c * 64:(c + 1) * 64, q0:q0 + n],
                            lambda kt, h=h: dvaug[:, kt, h, 0:129],
                            h, MMAX[h], False, None, o_slot,
                            'featT%d' % (F_DK + h), 'featT%d' % (F_DQ + h), 'dvaug', ores, post)
        if stop_after == 'D':
            flush_steps()
            break

        for qg in range(4):
            tsl = slice(qg * 4, qg * 4 + 4)
            for h in range(8):
                info = {}

                def cscore(h=h, qg=qg, info=info):
                    base = (h // 4) * 64
                    j = h % 4
                    pi = rr['ps'] % 3
                    rr['ps'] += 1
                    st = PS[pi]
                    sres = 'ps%d' % pi
                    mm(st[0:127, :], kcT[base:base + 64, 0:127], featT[F_NQ + j][base:base + 64, qg * 512:(qg + 1) * 512], True, False,
                       ['kcT', 'featT%d' % (F_NQ + j)], [sres])
                    mm(st[0:127, :], cb[0:127, C_ID:C_ID + 127], cb[0:127, C_MCMP + qg * 512:C_MCMP + (qg + 1) * 512], False, True, ['cb'], [sres])
                    pj = rr['pt'] % 3
                    rr['pt'] += 1
                    act(pt[pj][0:127, :], st[0:127, :], AF.Exp, [sres], ['pt%d' % pj])
                    info['pj'] = pj

                def cpv(h=h, info=info):
                    g = h // 4
                    pj = info['pj']
                    ob = 4 + (h % 2)
                    ov = PS[ob][:, 0:400].rearrange("p (a e) -> p a e", a=4)
                    for qi in range(4):
                        mm(ov[:, qi, 0:97], pt[pj][0:127, qi * 128:(qi + 1) * 128], vcv[0:127, g, 0:97], True, True, ['pt%d' % pj, 'vcaug'], ['o%d' % ob])

                def cpost(h=h, qg=qg, tsl=tsl):
                    g = h // 4
                    j = h % 4
                    ob = 4 + (h % 2)
                    ores = 'o%d' % ob
                    ov = PS[ob][:, 0:400].rearrange("p (a e) -> p a e", a=4)
                    rs = sm[:, 170:174]
                    ts('dve', rs.unsqueeze(2), ov[:, :, 64:65], 1e-30, None, ALU.max, None, [ores], ['rsA'])
                    S.add('dve', lambda e, rs=rs: e.reciprocal(out=rs, in_=rs), reads=['rsA'], writes=['rsB'])
                    sg = sm[:, 174:178]
                    tt('dve', sg.unsqueeze(2), rs.unsqueeze(2), gates[:, tsl, h * 3:h * 3 + 1], ALU.mult, ['rsB', 'gates'], ['sgA'])
                    tt('dve', accv[:, :, h, :], ov[:, :, 0:64], sg.unsqueeze(2).to_broadcast([128, 4, 64]), ALU.mult, [ores, 'sgA'], ['acc%d' % h])
                    if not DBG_BR[0]:
                        S.add('dve', lambda e, h=h: e.memset(accv[:, :, h, :], 0.0), reads=[], writes=['acc%d' % h])
                    if j == 0:
                        tt('dve', impv[:, :, g, :], ov[:, :, 65:97], rs.unsqueeze(2).to_broadcast([128, 4, 32]), ALU.mult, [ores, 'rsB'], ['imp%d' % g])
                    else:
                        tt('dve', tmp0[:, 0:128].rearrange("p (a e) -> p a e", a=4), ov[:, :, 65:97], rs.unsqueeze(2).to_broadcast([128, 4, 32]),
                           ALU.mult, [ores, 'rsB'], ['tmp0'])
                        tt('dve', impv[:, :, g, :], impv[:, :, g, :], tmp0[:, 0:128].rearrange("p (a e) -> p a e", a=4), ALU.add,
                           ['tmp0', 'imp%d' % g], ['imp%d' % g])
                    if j != 3:
                        return
                    for qi in range(4):
                        qt = qg * 4 + qi
                        val = selw[:, 0:32]
                        tt('dve', val, impv[:, qi, g, :], cf[:, C_FB + qt * 32:C_FB + (qt + 1) * 32], ALU.add, ['imp%d' % g, 'cf'], ['sw_a'])
                        S.add('dve', lambda e: e.max(out=selw[:, 32:40], in_=selw[:, 0:32]), reads=['sw_a'], writes=['sw_b'])
                        S.add('dve', lambda e: e.match_replace(out=selw[:, 64:96], in_to_replace=selw[:, 32:40], in_values=selw[:, 0:32], imm_value=-1e9),
                              reads=['sw_a', 'sw_b'], writes=['sw_c'])
                        S.add('dve', lambda e: e.max(out=selw[:, 40:48], in_=selw[:, 64:96]), reads=['sw_c'], writes=['sw_d'])
                        S.add('dve', lambda e: e.tensor_reduce(out=selw[:, 48:49], in_=selw[:, 40:48], axis=AX.X, op=ALU.min), reads=['sw_d'], writes=['sw_e'])
                        ts('dve', selbv[:, qi, g, :], val, selw[:, 48:49], NEG, ALU.is_lt, ALU.mult, ['sw_a', 'sw_e'], ['selb'])
                        pbb = PS[3][:, :].bitcast(BF16)
                        tr(pbb[0:32, (g * 4 + qi) * 128:(g * 4 + qi + 1) * 128], selbv[:, qi, g, :], ['selb'], ['ps3'])
                    cp('act', selbTv[0:32, g, :], PS[3][:, :].bitcast(BF16)[0:32, g * 512:(g + 1) * 512], ['ps3'], ['selbT'])
                steps.append([cscore, cpv, cpost])
            for br in (1, 0):
                for h in range(8):
                    g = h // 4
                    base = g * 64
                    j = h % 4
                    ob = 4 + (h % 4)
                    ores = ['o%d' % ob]
                    ovv = PS[ob][:, 0:264].rearrange("p (a e) -> p a e", a=4)

                    def o_slot(qi, ovv=ovv, ob=ob):
                        return ovv[:, qi, 0:65], ob
                    kf = F_SK if br == 0 else F_WK
                    va = svaug if br == 0 else wvaug

                    def post(h=h, br=br, ovv=ovv, ores=ores, tsl=tsl):
                        rs = sm[:, 180 + (h % 2) * 8:184 + (h % 2) * 8]
                        sg = sm[:, 184 + (h % 2) * 8:188 + (h % 2) * 8]
                        rk = 'rs%d' % (h % 2)
                        S.add('dve', lambda e, rs=rs, ovv=ovv: e.reciprocal(out=rs.unsqueeze(2), in_=ovv[:, :, 64:65]), reads=ores, writes=[rk + 'A'])
                        tt('dve', sg.unsqueeze(2), rs.unsqueeze(2), gates[:, tsl, h * 3 + 1 + br:h * 3 + 2 + br], ALU.mult, [rk + 'A', 'gates'], [rk + 'B'])
                        tv = (tmpa if h % 2 else tmpb)[:, 0:256].rearrange("p (a e) -> p a e", a=4)
                        tk = 'tv%d' % (h % 2)
                        tt('dve', tv, ovv[:, :, 0:64], sg.unsqueeze(2).to_broadcast([128, 4, 64]), ALU.mult, ores + [rk + 'B'], [tk])
                        if not DBG_BR[1 + br]:
                            S.add('dve', lambda e, tv=tv: e.memset(tv, 0.0), reads=[], writes=[tk])
                        if br == 1:
                            tt('pool', accv[:, :, h, :], accv[:, :, h, :], tv, ALU.add, [tk, 'acc%d' % h], ['acc%d' % h])
                        else:
                            tt('pool', cat[:, tsl, 512 + h * 64:512 + (h + 1) * 64], accv[:, :, h, :], tv, ALU.add, [tk, 'acc%d' % h], ['cat'])
                    attn_qg(qg,
                            lambda kt, kf=kf, base=base: featT[kf][base:base + 64, kt * 128:(kt + 1) * 128],
                            lambda q0, n, j=j, base=base: featT[F_NQ + j][base:base + 64, q0:q0 + n],
                            lambda kt, va=va, g=g: va[:, kt, g, 0:65],
                            4 + h, (4 if br == 1 else MMAX[4 + h]), br == 1, (g if br == 0 else None), o_slot,
                            'featT%d' % kf, 'featT%d' % (F_NQ + j), 'svaug' if br == 0 else 'wvaug', ores, post)
        flush_steps()
        S.barrier()
        if stop_after == 'N':
            break

        dma('woutb', [(A2[:, 0:4096].bitcast(BF16), wos)], ['pcB'], ['woutb'])
        dma('lnp', [(lnp[:, a, :], lnc[a].partition_broadcast(128)) for a in range(4)], [], ['lnp'])

        def layer_norm(src, srck, dst, dstk, gi, eng_aff):
            S.add('dve', lambda e: e.bn_stats(out=sm[:, 200:206], in_=src[:, 0:512]), reads=[srck], writes=['bn_a'])
            S.add('dve', lambda e: e.bn_stats(out=sm[:, 206:212], in_=src[:, 512:1024]), reads=[srck], writes=['bn_b'])
            S.add('dve', lambda e: e.bn_aggr(out=sm[:, 212:214], in_=sm[:, 200:212]), reads=['bn_a', 'bn_b'], writes=['bn_c'])
            ts('dve', sm[:, 214:215], sm[:, 213:214], 1e-5, None, ALU.add, None, ['bn_c'], ['bn_d'])
            act(sm[:, 214:215], sm[:, 214:215], AF.Sqrt, ['bn_d'], ['bn_e'])
            S.add('dve', lambda e: e.reciprocal(out=sm[:, 215:216], in_=sm[:, 214:215]), reads=['bn_e'], writes=['bn_f'])
            ts('dve', src, src, sm[:, 212:213], sm[:, 215:216], ALU.subtract, ALU.mult, [srck, 'bn_c', 'bn_f'], [srck])
            tt(eng_aff, src, src, lnp[:, gi, :], ALU.mult, [srck, 'lnp'], [srck])
            tt(eng_aff, dst, src, lnp[:, gi + 1, :], ALU.add, [srck, 'lnp'], [dstk])

        for tg in range(4):
            for ti in range(4):
                t = tg * 4 + ti
                b = t % 2
                dma('xs%d' % b, [(xs[b][:, :], x[s, t * 128:(t + 1) * 128, :])], [], ['xs%d' % b])
                pbb = PS[3][:, :].bitcast(BF16)
                for c in range(8):
                    tr(pbb[:, c * 128:(c + 1) * 128], cat[:, t, c * 128:(c + 1) * 128], ['cat'], ['ps3'])
                cp('act', catT[:, :, :], pbb.rearrange("p (c t) -> p c t", c=8), ['ps3'], ['catT'])
                for hf in range(2):
                    for k in range(8):
                        mm(PS[hf][:, :], catT[:, k, :], woutb[:, k, hf * 512:(hf + 1) * 512], k == 0, k == 7, ['catT', 'woutb'], ['ps%d' % hf])
                    stt('dve', xs[b][:, hf * 512:(hf + 1) * 512], xs[b][:, hf * 512:(hf + 1) * 512], float(ALPHA), PS[hf][:, :],
                        ALU.mult, ALU.add, ['xs%d' % b, 'ps%d' % hf], ['xs%d' % b])
                layer_norm(xs[b][:, :], 'xs%d' % b, x1[:, ti, :], 'x1_%d' % ti, 0, 'pool')
                cp('act', x1b[:, :], x1[:, ti, :], ['x1_%d' % ti], ['x1b'])
                pbb2 = PS[2][:, :].bitcast(BF16)
                for c in range(8):
                    tr(pbb2[:, c * 128:(c + 1) * 128], x1b[:, c * 128:(c + 1) * 128], ['x1b'], ['ps2'])
                cp('act', x1T[:, :, ti * 128:(ti + 1) * 128], pbb2.rearrange("p (c t) -> p c t", c=8), ['ps2'], ['x1T'])
            for j in range(NJ):
                wb1 = rr['w'] % 6
                rr['w'] += 1
                dma('wbf%d' % wb1, [(wbf[wb1][:, :], wgs[j])], ['pcB'], ['wbf%d' % wb1])
                wb2 = rr['w'] % 6
                rr['w'] += 1
                dma('wbf%d' % wb2, [(wbf[wb2][:, :], wus[j])], ['pcB'], ['wbf%d' % wb2])
                wgv = wbf[wb1][:, :].rearrange("p (c n) -> p c n", c=8)
                wuv = wbf[wb2][:, :].rearrange("p (c n) -> p c n", c=8)
                pg = (j % 2) * 2
                for k in range(8):
                    mm(PS[pg][:, :], wgv[:, k, :], x1T[:, k, :], k == 0, k == 7, ['wbf%d' % wb1, 'x1T'], ['ps%d' % pg])
                for k in range(8):
                    mm(PS[pg + 1][:, :], wuv[:, k, :], x1T[:, k, :], k == 0, k == 7, ['wbf%d' % wb2, 'x1T'], ['ps%d' % (pg + 1)])
                sgb = tmp0 if j % 2 == 0 else tmpa
                sgk = 'tmp0' if j % 2 == 0 else 'tmp1'
                act(sgb[:, :], PS[pg][:, :], AF.Silu, ['ps%d' % pg], [sgk])
                tt('dve', aT[:, j, :], sgb[:, :], PS[pg + 1][:, :], ALU.mult, [sgk, 'ps%d' % (pg + 1)], ['aT'])
            for hf in range(2):
                for j in range(NJ):
                    wb = rr['w'] % 6
                    rr['w'] += 1
                    dma('wbf%d' % wb, [(wbf[wb][:, 0:512], wds[j, :, hf * 512:(hf + 1) * 512])], ['pcB'], ['wbf%d' % wb])
                    for ti in range(4):
                        mm(PS[4 + ti][:, :], aT[:, j, ti * 128:(ti + 1) * 128], wbf[wb][:, 0:512], j == 0, j == NJ - 1,
                           ['aT', 'wbf%d' % wb], ['o%d' % (4 + ti)])
                for ti in range(4):
                    stt('dve', x1[:, ti, hf * 512:(hf + 1) * 512], x1[:, ti, hf * 512:(hf + 1) * 512], float(ALPHA), PS[4 + ti][:, :],
                        ALU.mult, ALU.add, ['x1_%d' % ti, 'o%d' % (4 + ti)], ['x1_%d' % ti])
            for ti in range(4):
                t = tg * 4 + ti
                ob = ti % 2
                layer_norm(x1[:, ti, :], 'x1_%d' % ti, x1[:, ti, :], 'x1_%d' % ti, 2, 'pool')
                dma('out%d' % ti, [(out[s, t * 128:(t + 1) * 128, :], x1[:, ti, :])], ['x1_%d' % ti], ['outd'])
        S.barrier()

    if dbg:
        S.barrier()
        lst = []
        for c0 in range(0, 8192, 2048):
            lst.append((dbg_o[:, c0:c0 + 2048], A1[:, c0:c0 + 2048]))
        for c0 in range(0, 23040, 2048):
            c1 = min(23040, c0 + 2048)
            lst.append((dbg_o[:, 8192 + c0:8192 + c1], A2[:, c0:c1]))
        lst += [(dbg_o[:, 31232:31232 + 2048], acc[:, :]), (dbg_o[:, 33280:33280 + 256], sm[:, :]), (dbg_o[:, 33536:33536 + 256], impacc[:, :])]
        lst += [(dbg_o[:, 33792:33856], kcT[:, :].bitcast(F32)), (dbg_o[:, 33856:33956], vcaug[:, :].bitcast(F32)),
                (dbg_o[:, 33956:34212], hid[:, :].bitcast(F32)), (dbg_o[:, 34212:34340], selb[:, :].bitcast(F32)),
                (dbg_o[:, 34340:34340 + 512], selbT[:, :].bitcast(F32))]
        dma('dbg', lst, [], ['dbgd'])
    S.barrier()
    stuck = S.check_deadlock()
    assert not stuck, stuck
    S.emit()
    es.close()
    return nc


def host_prep(inputs):
    f = lambda a: np.ascontiguousarray(a, dtype=np.float32)
    w_in = f(inputs["w_in"][0])
    winc = np.zeros((NCH, 128, 8, 128), np.float32)
    for ci, (pieces, kind, scale) in enumerate(CH):
        o = 0
        for (c0, ncol) in pieces:
            blk = w_in[:, c0:c0 + ncol].reshape(8, 128, ncol).transpose(1, 0, 2)
            winc[ci, :, :, o:o + ncol] = blk
            o += ncol
    winc = winc.reshape(NCH, 128, 1024)

    def chunk_cols(w):
        return np.ascontiguousarray(w.reshape(8, 128, NJ, 128).transpose(2, 1, 0, 3)).reshape(NJ, 128, 1024)
    wgc = chunk_cols(f(inputs["w_gate"][0]))
    wuc = chunk_cols(f(inputs["w_up"][0]))
    wdc = np.ascontiguousarray(f(inputs["w_down"][0]).reshape(NJ, 128, 1024))
    woc = np.ascontiguousarray(f(inputs["w_out"][0]).reshape(8, 128, 1024).transpose(1, 0, 2)).reshape(128, 8192)
    w1 = np.stack([f(inputs["cmp_w1_k"][0]), f(inputs["cmp_w1_v"][0])], 0)
    w1 = w1.reshape(2, 32, 64, 128).transpose(2, 0, 1, 3).reshape(64, 8192)
    w1c = np.ascontiguousarray(np.concatenate([w1, w1], 0))
    w2k = f(inputs["cmp_w2_k"][0])
    w2v = f(inputs["cmp_w2_v"][0])
    w2c = np.ascontiguousarray(np.concatenate([w2k, w2k, w2v], 1))
    lnc = np.ascontiguousarray(np.stack([f(inputs["ln1_g"][0]), f(inputs["ln1_b"][0]), f(inputs["ln2_g"][0]), f(inputs["ln2_b"][0])], 0))
    pvc = np.zeros((128, NPV), np.float32)
    pvc[:, 0:64] = f(inputs["diff_lq1"][0])[None, :]
    pvc[:, 64:128] = f(inputs["diff_lk1"][0])[None, :]
    pvc[:, 128:192] = f(inputs["diff_lq2"][0])[None, :]
    pvc[:, 192:256] = f(inputs["diff_lk2"][0])[None, :]
    pvc[:, P_SUBG:P_SUBG + 128] = f(inputs["diff_subln_g"][0])[None, :]
    pek = f(inputs["cmp_pe_k"][0]).T
    pev = f(inputs["cmp_pe_v"][0]).T
    pvc[:, P_PEK:P_PEK + 32] = np.concatenate([pek, pek], 0)
    pvc[:, P_PEV:P_PEV + 32] = np.concatenate([pev, pev], 0)
    cb16, cf32 = make_consts()
    return dict(winc=winc, wgc=wgc, wuc=wuc, wdc=wdc, woc=woc, w1c=w1c, w2c=w2c, lnc=lnc, pvc=pvc, cb16=cb16, cf32=cf32)


def kernel(**inputs):
    x = np.ascontiguousarray(inputs["x"], dtype=np.float32)
    shared = host_prep(inputs)
    ncores = 8
    nseq = x.shape[0] // ncores
    nc = build(nseq)
    in_maps = []
    for c in range(ncores):
        m = dict(shared)
        m["x"] = np.ascontiguousarray(x[c * nseq:(c + 1) * nseq])
        in_maps.append(m)
    res = run_bass_kernel_spmd(nc, in_maps, core_ids=list(range(ncores)))
    return np.concatenate([r["out"] for r in res.results], axis=0).astype(np.float32)
```

```python
import numpy as np
import ml_dtypes
from contextlib import ExitStack
import concourse.bass as bass
import concourse.mybir as mybir
from concourse.bass_utils import run_bass_kernel_spmd

F32 = mybir.dt.float32
BF16 = mybir.dt.bfloat16
AF = mybir.ActivationFunctionType
ALU = mybir.AluOpType
AX = mybir.AxisListType

T = 2048
DM = 1024
NT = 16
DFF = 2816
NJ = 22
NEG = -30000.0
ALPHA = 2.0 ** 0.25
LAM_INIT = 0.2
DBG_BR = [1, 1, 1]
SLOPES = (2.0 ** (-8.0 * (np.arange(12) + 1) / 12)).astype(np.float32)
MMAX = []
for _s in SLOPES:
    _m = 15
    while _m > 0 and float(_s) * (128 * (_m - 1) + 1) > 144.0:
        _m -= 1
    MMAX.append(_m)

OFF_DQ, OFF_DK, OFF_DV, OFF_NQ = 0, 512, 1024, 1536
OFF_CK, OFF_CV, OFF_SK, OFF_SV, OFF_WK, OFF_WV, OFF_G = 2048, 2176, 2304, 2432, 2560, 2688, 2816
CH = []
for h in range(4):
    CH.append(([(OFF_DQ + h * 128, 128)], 'F', 0.125))
for h in range(4):
    CH.append(([(OFF_DK + h * 128, 128)], 'F', 1.0))
for j in range(4):
    CH.append(([(OFF_NQ + j * 64, 64), (OFF_NQ + (j + 4) * 64, 64)], 'F', 0.125))
CH.append(([(OFF_CK, 128)], 'F', 1.0))
CH.append(([(OFF_CV, 128)], 'F', 1.0))
CH.append(([(OFF_SK, 128)], 'F', 1.0))
CH.append(([(OFF_WK, 128)], 'F', 1.0))
for h in range(4):
    CH.append(([(OFF_DV + h * 128, 128)], 'T', 1.0))
CH.append(([(OFF_SV, 128)], 'T', 1.0))
CH.append(([(OFF_WV, 128)], 'T', 1.0))
CH.append(([(OFF_G, 24)], 'T', 1.0))
NCH = len(CH)
F_DQ, F_DK, F_NQ, F_CK, F_CV, F_SK, F_WK = 0, 4, 8, 12, 13, 14, 15

C_ID = 0
C_TLO = 128
C_THI = 256
C_MCMP = 384
C_E = C_MCMP + 2048
C_ALA = C_E + 2048
C_ALB = C_ALA + 12 * 128
C_OVL = C_ALB + 512
NC16 = C_OVL + 32
C_FB = 0
C_BT = 512
NC32 = C_BT + 192
P_LQ = 0
P_SUBG = 256
P_PEK = 384
P_PEV = 416
NPV = 448


def _split3(v):
    v = np.asarray(v, np.float64)
    a = v.astype(np.float32).astype(ml_dtypes.bfloat16)
    r = v - a.astype(np.float64)
    b = r.astype(np.float32).astype(ml_dtypes.bfloat16)
    r = r - b.astype(np.float64)
    c = r.astype(np.float32).astype(ml_dtypes.bfloat16)
    return a, b, c


def make_consts():
    cb = np.zeros((128, NC16), np.float32)
    cb[:, C_ID:C_ID + 128] = np.eye(128)
    r = np.arange(128)[:, None]
    j = np.arange(128)[None, :]
    cb[:, C_TLO:C_TLO + 128] = np.where(j < r, NEG, 0.0)
    cb[:, C_THI:C_THI + 128] = np.where(j >= r, NEG, 0.0)
    n = np.arange(127)[:, None]
    t = np.arange(T)[None, :]
    cb[:127, C_MCMP:C_MCMP + T] = np.where(t < 16 * n + 31, NEG, 0.0)
    s = np.arange(32)[:, None]
    m = np.arange(T)[None, :]
    cb[:32, C_E:C_E + T] = (m // 64 == s).astype(np.float32)
    cb16 = cb.astype(ml_dtypes.bfloat16)
    jj = np.arange(512)
    for si in range(12):
        sv = float(SLOPES[si])
        p = _split3(np.full(128, sv))
        q = _split3(sv * np.arange(128, dtype=np.float64))
        for k in range(3):
            cb16[k, C_ALA + si * 128:C_ALA + (si + 1) * 128] = p[k]
            cb16[3 + k, C_ALA + si * 128:C_ALA + (si + 1) * 128] = p[k]
            cb16[6 + k, C_ALA + si * 128:C_ALA + (si + 1) * 128] = q[k]
    for k in range(3):
        cb16[k, C_ALB:C_ALB + 512] = (-(jj // 2) * 2).astype(np.float32).astype(ml_dtypes.bfloat16)
        cb16[3 + k, C_ALB:C_ALB + 512] = (-(jj % 2)).astype(np.float32).astype(ml_dtypes.bfloat16)
        cb16[6 + k, C_ALB:C_ALB + 512] = np.ones(512, np.float32).astype(ml_dtypes.bfloat16)
    cb16[64:73, C_ALA:C_ALB + 512] = cb16[0:9, C_ALA:C_ALB + 512]
    cb16[64:96, C_E:C_E + T] = cb16[0:32, C_E:C_E + T]
    cs = np.arange(127)[:, None] * 16
    ss = np.arange(32)[None, :] * 64
    ovl = ((cs <= ss + 63) & (cs + 31 >= ss)).astype(np.float32)
    cb16[:127, C_OVL:C_OVL + 32] = ovl.astype(ml_dtypes.bfloat16)

    cf = np.zeros((128, NC32), np.float32)
    p = np.arange(128)
    for qt in range(16):
        cur = 2 * qt + p // 64
        for sb in range(32):
            v = np.zeros(128, np.float32)
            if sb == 0:
                v[:] = 1000.0
            v[sb == cur - 1] = 3000.0
            v[sb == cur] = 2000.0
            v[sb > cur] = -1000.0 - sb
            cf[:, C_FB + qt * 32 + sb] = v
    for si in range(12):
        for mm in range(16):
            cf[:, C_BT + si * 16 + mm] = -np.float32(SLOPES[si]) * 128.0 * mm
    return cb16, cf


class _Op:
    __slots__ = ('fn', 'waits', 'dma_key', 'needed', 'ms', 'dtok')

    def __init__(self, fn, waits, dma_key):
        self.fn = fn
        self.waits = waits
        self.dma_key = dma_key
        self.needed = False
        self.ms = 0


class Sched:
    COMPUTE = ('pe', 'act', 'dve', 'pool')
    ALL = ('pe', 'act', 'dve', 'pool', 'sp')

    def __init__(self, nc, es):
        self.nc = nc
        self.es = es
        self.ops = {e: [] for e in self.ALL}
        self.res = {}
        self.waited = {e: {} for e in self.ALL}
        self.dcount = {}
        self.dsem = {}
        self.esem = {e: es.enter_context(nc.semaphore('es_' + e)) for e in self.COMPUTE}

    def _need(self, eng, tok, waits):
        if tok is None:
            return
        key = (tok[0], tok[1])
        w = self.waited[eng]
        if w.get(key, 0) >= tok[2]:
            return
        w[key] = tok[2]
        waits.append(tok)
        if tok[0] == 'e':
            self.ops[tok[1]][tok[2] - 1].needed = True

    def add(self, eng, fn, reads=(), writes=(), dma_key=None, ndma=1):
        waits = []
        is_dma = dma_key is not None
        for r in reads:
            st = self.res.get(r)
            if st is not None:
                self._need(eng, st[0], waits)
        for w in writes:
            st = self.res.get(w)
            if st is not None:
                lw = st[0]
                if lw is not None and (is_dma or not (lw[0] == 'e' and lw[1] == eng)):
                    self._need(eng, lw, waits)
                for tk in st[1].values():
                    if is_dma or not (tk[0] == 'e' and tk[1] == eng):
                        self._need(eng, tk, waits)
        op = _Op(fn, waits, dma_key)
        self.ops[eng].append(op)
        if is_dma:
            if dma_key not in self.dsem:
                self.dsem[dma_key] = self.es.enter_context(self.nc.semaphore('ds_' + dma_key))
                self.dcount[dma_key] = 0
            self.dcount[dma_key] += 16 * ndma
            tok = ('d', dma_key, self.dcount[dma_key])
            op.dtok = self.dcount[dma_key]
        else:
            tok = ('e', eng, len(self.ops[eng]))
        for r in reads:
            st = self.res.setdefault(r, [None, {}])
            st[1][(tok[0], tok[1])] = tok
        for w in writes:
            self.res[w] = [tok, {}]
        return tok

    def barrier(self):
        toks = []
        for e in self.COMPUTE:
            n = len(self.ops[e])
            while n > 0 and (self.ops[e][n - 1].fn is None or self.ops[e][n - 1].dma_key is not None):
                n -= 1
            if n > 0:
                toks.append(('e', e, n))
        for k, c in self.dcount.items():
            toks.append(('d', k, c))
        for e in self.ALL:
            waits = []
            for tk in toks:
                if tk[0] == 'e' and tk[1] == e:
                    continue
                self._need(e, tk, waits)
            self.ops[e].append(_Op(None, waits, None))

    def check_deadlock(self):
        ptr = {e: 0 for e in self.ALL}
        done_e = {e: 0 for e in self.COMPUTE}
        done_d = {}
        dcnt = {}
        progress = True
        while progress:
            progress = False
            for e in self.ALL:
                while ptr[e] < len(self.ops[e]):
                    op = self.ops[e][ptr[e]]
                    ok = True
                    for tk in op.waits:
                        if tk[0] == 'e':
                            if done_e[tk[1]] < tk[2]:
                                ok = False
                                break
                        else:
                            if done_d.get(tk[1], 0) < tk[2]:
                                ok = False
                                break
                    if not ok:
                        break
                    ptr[e] += 1
                    progress = True
                    if op.dma_key is not None:
                        pass
                    if e in done_e:
                        done_e[e] = ptr[e]
                    if op.dma_key is not None:
                        done_d[op.dma_key] = op.dtok
        stuck = {e: (ptr[e], len(self.ops[e])) for e in self.ALL if ptr[e] < len(self.ops[e])}
        return stuck

    def emit(self):
        nc = self.nc
        for e in self.COMPUTE:
            m = 0
            for op in self.ops[e]:
                if op.needed:
                    assert op.fn is not None and op.dma_key is None
                    m += 1
                    op.ms = m

        def run(name):
            def f(eng):
                for op in self.ops[name]:
                    for tk in op.waits:
                        if tk[0] == 'e':
                            eng.wait_ge(self.esem[tk[1]], self.ops[tk[1]][tk[2] - 1].ms)
                        else:
                            eng.wait_ge(self.dsem[tk[1]], tk[2])
                    if op.fn is None:
                        continue
                    r = op.fn(eng)
                    if op.dma_key is not None:
                        for ins in r:
                            ins.then_inc(self.dsem[op.dma_key], 16)
                    elif op.needed:
                        r.then_inc(self.esem[name], 1)
            return f

        with nc.Block() as blk:
            blk.sync(run('sp'))
            blk.tensor(run('pe'))
            blk.vector(run('dve'))
            blk.scalar(run('act'))
            blk.gpsimd(run('pool'))


def build(nseq=4, dbg=False, stop_after=None):
    nc = bass.Bass("TRN2", target_bir_lowering=False)

    def DIN(name, shape, dt=F32):
        return nc.dram_tensor(name, shape, dt, kind="ExternalInput").ap()

    x = DIN("x", [nseq, T, DM])
    winc = DIN("winc", [NCH, 128, 1024])
    wgc = DIN("wgc", [NJ, 128, 1024])
    wuc = DIN("wuc", [NJ, 128, 1024])
    wdc = DIN("wdc", [NJ, 128, 1024])
    woc = DIN("woc", [128, 8192])
    w1c = DIN("w1c", [128, 8192])
    w2c = DIN("w2c", [128, 192])
    lnc = DIN("lnc", [4, DM])
    pvc = DIN("pvc", [128, NPV])
    cb16d = DIN("cb16", [128, NC16], BF16)
    cf32d = DIN("cf32", [128, NC32])
    out = nc.dram_tensor("out", [nseq, T, DM], F32, kind="ExternalOutput").ap()
    if dbg:
        dbg_o = nc.dram_tensor("dbg", [128, 35000], F32, kind="ExternalOutput").ap()

    def DSCR(name, shape):
        return nc.dram_tensor(name, shape, BF16, kind="Internal").ap()
    wins = DSCR("wins", [NCH, 128, 1024])
    wgs = DSCR("wgs", [NJ, 128, 1024])
    wus = DSCR("wus", [NJ, 128, 1024])
    wds = DSCR("wds", [NJ, 128, 1024])
    wos = DSCR("wos", [128, 8192])
    w1s = DSCR("w1s", [128, 8192])

    es = ExitStack()
    S = Sched(nc, es)

    def SB(name, cols, dt=F32):
        return es.enter_context(nc.sbuf_tensor(name, [128, cols], dt))

    A1 = SB("arena1", 8192)
    A2 = SB("arena2", 23040)
    xT = A1[:, :].bitcast(BF16).rearrange("p (c t) -> p c t", c=8)
    w1b = A1[:, 0:4096].bitcast(BF16).rearrange("p (k l h) -> p k l h", k=2, l=32)
    cat = A1[:, :].bitcast(BF16).rearrange("p (t f) -> p t f", t=16)
    featT = [A2[:, i * 1024:(i + 1) * 1024].bitcast(BF16) for i in range(16)]
    o2 = 16384
    dvaug = A2[:, o2:o2 + 4160].bitcast(BF16).rearrange("p (t h e) -> p t h e", t=16, h=4)
    o2 += 4160
    svaug = A2[:, o2:o2 + 1056].bitcast(BF16).rearrange("p (t g e) -> p t g e", t=16, g=2)
    o2 += 1056
    wvaug = A2[:, o2:o2 + 1056].bitcast(BF16).rearrange("p (t g e) -> p t g e", t=16, g=2)
    o2 += 1056
    gates = A2[:, o2:o2 + 384].rearrange("p (t c) -> p t c", t=16)
    woutb = A2[:, 0:4096].bitcast(BF16).rearrange("p (c n) -> p c n", c=8)
    lnp = A2[:, 4096:8192].rearrange("p (a n) -> p a n", a=4)
    aT = A2[:, 8192:8192 + 5632].bitcast(BF16).rearrange("p (j t) -> p j t", j=NJ)
    x1 = A2[:, 13824:13824 + 4096].rearrange("p (t n) -> p t n", t=4)
    x1T = A2[:, 17920:17920 + 2048].bitcast(BF16).rearrange("p (c t) -> p c t", c=8)
    catT = A2[:, 19968:19968 + 512].bitcast(BF16).rearrange("p (c t) -> p c t", c=8)
    x1b = A2[:, 20480:20480 + 512].bitcast(BF16)
    catT2 = [A2[:, 19968 + i * 512:19968 + (i + 1) * 512].bitcast(BF16).rearrange("p (c t) -> p c t", c=8) for i in range(2)]
    x1b2 = [A2[:, 20992 + i * 512:20992 + (i + 1) * 512].bitcast(BF16) for i in range(2)]

    acc = SB("acc", 2048)
    accv = acc[:, :].rearrange("p (t h d) -> p t h d", t=4, h=8)
    impacc = SB("impacc", 256)
    impv = impacc[:, :].rearrange("p (t g s) -> p t g s", t=4, g=2)
    wst = [SB("wst%d" % i, 256) for i in range(1)]
    wbf = [SB("wbf%d" % i, 1024, BF16) for i in range(6)]
    xs = [SB("xs%d" % i, 1024) for i in range(2)]
    xb = [SB("xb%d" % i, 1024, BF16) for i in range(2)]
    ost = [SB("ost%d" % i, 1024) for i in range(2)]
    pt = [SB("pt%d" % i, 512, BF16) for i in range(6)]
    cb = SB("cb", NC16, BF16)
    cf = SB("cf", NC32)
    pv = SB("pv", NPV)
    w2b = SB("w2b", 192, BF16)
    tmp0 = SB("tmp0", 512)
    tmpa = SB("tmpa", 512)
    tmpb = SB("tmpb", 512)
    sm = SB("sm", 256)
    selw = SB("selw", 256)
    selb = SB("selb", 256, BF16)
    selbT = SB("selbT", 1024, BF16)
    kcT = SB("kcT", 128, BF16)
    vcaug = SB("vcaug", 200, BF16)
    hid = SB("hid", 512, BF16)
    cmpt = [SB("cmpt%d" % i, 128, BF16) for i in range(2)]
    hpre = SB("hpre", 512)
    PS = [es.enter_context(nc.psum_tensor("ps%d" % i, [128, 512], F32)) for i in range(8)]

    selbv = selb[:, :].rearrange("p (t g s) -> p t g s", t=4, g=2)
    selbTv = selbT[:, :].rearrange("p (g q) -> p g q", g=2)
    vcv = vcaug[:, :].rearrange("p (g e) -> p g e", g=2)
    hidv = hid[:, :].rearrange("p (k n) -> p k n", k=4)

    ident = cb[:, C_ID:C_ID + 128]

    def dma(key, outs_ins, reads, writes, eng='sp'):
        def fn(e):
            return [e.dma_start(out=o, in_=i) for (o, i) in outs_ins]
        return S.add(eng, fn, reads=reads, writes=writes, dma_key=key, ndma=len(outs_ins))

    def mm(o, lhsT, rhs, start, stop, reads, writes):
        S.add('pe', lambda e: e.matmul(o, lhsT=lhsT, rhs=rhs, start=start, stop=stop), reads=reads, writes=writes)

    def tr(o, i, reads, writes):
        S.add('pe', lambda e: e.transpose(out=o, in_=i, identity=ident), reads=reads + ['cb'], writes=writes)

    def act(o, i, func, reads, writes, bias=None, scale=None, accum=None):
        kw = {}
        if bias is not None:
            kw['bias'] = bias
        if scale is not None:
            kw['scale'] = scale
        if accum is not None:
            kw['accum_out'] = accum
        S.add('act', lambda e: e.activation(out=o, in_=i, func=func, **kw), reads=reads, writes=writes)

    def cp(eng, o, i, reads, writes):
        if eng == 'act':
            S.add(eng, lambda e: e.activation(out=o, in_=i, func=AF.Copy), reads=reads, writes=writes)
        else:
            S.add(eng, lambda e: e.tensor_copy(out=o, in_=i), reads=reads, writes=writes)

    def tt(eng, o, a, b, op, reads, writes):
        S.add(eng, lambda e: e.tensor_tensor(out=o, in0=a, in1=b, op=op), reads=reads, writes=writes)

    def ts(eng, o, a, s1, s2, op0, op1, reads, writes):
        if op1 is None:
            S.add(eng, lambda e: e.tensor_scalar(out=o, in0=a, scalar1=s1, scalar2=None, op0=op0), reads=reads, writes=writes)
        else:
            S.add(eng, lambda e: e.tensor_scalar(out=o, in0=a, scalar1=s1, scalar2=s2, op0=op0, op1=op1), reads=reads, writes=writes)

    def stt(eng, o, a, sc, b, op0, op1, reads, writes):
        S.add(eng, lambda e: e.scalar_tensor_tensor(out=o, in0=a, scalar=sc, in1=b, op0=op0, op1=op1), reads=reads, writes=writes)

    dma('c0', [(cb[:, :], cb16d)], [], ['cb'])
    dma('c1', [(cf[:, :], cf32d), (pv[:, :], pvc)], [], ['cf', 'pv'])
    dma('c2', [(wst[0][:, 0:192], w2c)], [], ['wst0'])
    cp('pool', w2b[:, :], wst[0][:, 0:192], ['wst0'], ['w2b'])
    def precast_a():
        dma('pcA', [(wins[0:8], winc[0:8]), (wins[8:16], winc[8:16]), (wins[16:NCH], winc[16:NCH]), (w1s, w1c)], [], ['pcA'], eng='pool')

    def precast_b():
        dma('pcB', [(wos, woc)] + [(dst[j0:j0 + 6], src[j0:j0 + 6]) for (dst, src) in ((wgs, wgc), (wus, wuc), (wds, wdc)) for j0 in (0, 6, 12)]
            + [(dst[18:NJ], src[18:NJ]) for (dst, src) in ((wgs, wgc), (wus, wuc), (wds, wdc))], [], ['pcB'], eng='pool')

    tt('dve', sm[:, 0:64], pv[:, 0:64], pv[:, 64:128], ALU.mult, ['pv'], ['sm_a'])
    tt('dve', sm[:, 64:128], pv[:, 128:192], pv[:, 192:256], ALU.mult, ['pv'], ['sm_b'])
    S.add('dve', lambda e: e.tensor_reduce(out=sm[:, 128:129], in_=sm[:, 0:64], axis=AX.X, op=ALU.add), reads=['sm_a'], writes=['sm_c'])
    S.add('dve', lambda e: e.tensor_reduce(out=sm[:, 129:130], in_=sm[:, 64:128], axis=AX.X, op=ALU.add), reads=['sm_b'], writes=['sm_d'])
    act(sm[:, 130:131], sm[:, 128:129], AF.Exp, ['sm_c'], ['sm_e'])
    act(sm[:, 131:132], sm[:, 129:130], AF.Exp, ['sm_d'], ['sm_f'])
    tt('dve', sm[:, 132:133], sm[:, 131:132], sm[:, 130:131], ALU.subtract, ['sm_e', 'sm_f'], ['sm_g'])
    ts('dve', sm[:, 133:134], sm[:, 132:133], -LAM_INIT, None, ALU.add, None, ['sm_g'], ['neglam'])
    neglam = sm[:, 133:134]
    ts('dve', sm[:, 0:128], pv[:, P_SUBG:P_SUBG + 128], 1.0 - LAM_INIT, None, ALU.mult, None, ['pv', 'sm_a', 'sm_b', 'sm_c', 'sm_d'], ['gsub'])
    gsub = sm[:, 0:128]

    rr = {'ps': 0, 'w': 0, 'pt': 0, 'ev': 0}

    def evac_eng():
        rr['ev'] += 1
        return 'act' if rr['ev'] % 2 else 'dve'

    for s in range(nseq):
        for t in range(NT):
            b = t % 2
            dma('xb%d' % b, [(xb[b][:, :], x[s, t * 128:(t + 1) * 128, :])], [], ['xb%d' % b], eng='pool')
            if s == 0 and t == 1:
                precast_a()
            if s == 0 and t == NT - 1:
                precast_b()
            pb = PS[t % 2]
            pbb = pb[:, :].bitcast(BF16)
            for c in range(8):
                tr(pbb[:, c * 128:(c + 1) * 128], xb[b][:, c * 128:(c + 1) * 128], ['xb%d' % b], ['ps%d' % (t % 2)])
            cp(evac_eng(), xT[:, :, t * 128:(t + 1) * 128], pbb.rearrange("p (c t) -> p c t", c=8),
               ['ps%d' % (t % 2)], ['xT%d' % (t // 4)])
        S.add('pool', lambda e: e.memset(dvaug[:, :, :, 128:130], 1.0), reads=[], writes=['dvaug'])
        S.add('pool', lambda e: e.memset(svaug[:, :, :, 64:66], 1.0), reads=[], writes=['svaug'])
        S.add('pool', lambda e: e.memset(wvaug[:, :, :, 64:66], 1.0), reads=[], writes=['wvaug'])
        for ci in range(NCH):
            pieces, kind, scale = CH[ci]
            wb = rr['w'] % 6
            rr['w'] += 1
            dma('wbf%d' % wb, [(wbf[wb][:, :], wins[ci])], ['pcA'], ['wbf%d' % wb])
            wv = wbf[wb][:, :].rearrange("p (c n) -> p c n", c=8)
            if kind == 'F':
                for tg in range(4):
                    pi = rr['ps'] % 4
                    rr['ps'] += 1
                    for k in range(8):
                        mm(PS[pi][:, :], wv[:, k, :], xT[:, k, tg * 512:(tg + 1) * 512], k == 0, k == 7,
                           ['wbf%d' % wb, 'xT%d' % tg], ['ps%d' % pi])
                    dst = featT[ci][:, tg * 512:(tg + 1) * 512]
                    if evac_eng() == 'act':
                        act(dst, PS[pi][:, :], AF.Copy, ['ps%d' % pi], ['featT%d' % ci], scale=float(scale))
                    else:
                        ts('dve', dst, PS[pi][:, :], float(scale), None, ALU.mult, None, ['ps%d' % pi], ['featT%d' % ci])
            else:
                for t4 in range(4):
                    pi = rr['ps'] % 4
                    rr['ps'] += 1
                    for ti in range(4):
                        t = t4 * 4 + ti
                        for k in range(8):
                            mm(PS[pi][:, ti * 128:(ti + 1) * 128], xT[:, k, t * 128:(t + 1) * 128], wv[:, k, :], k == 0, k == 7,
                               ['wbf%d' % wb, 'xT%d' % t4], ['ps%d' % pi])
                    src = PS[pi][:, :].rearrange("p (t n) -> p t n", t=4)
                    tsl = slice(t4 * 4, t4 * 4 + 4)
                    if ci < 20:
                        h = ci - 16
                        cp(evac_eng(), dvaug[:, tsl, h, 0:128], src, ['ps%d' % pi], ['dvaug'])
                    elif ci == 20:
                        cp(evac_eng(), svaug[:, tsl, :, 0:64], src.rearrange("p t (g e) -> p t g e", g=2), ['ps%d' % pi], ['svaug'])
                    elif ci == 21:
                        cp(evac_eng(), wvaug[:, tsl, :, 0:64], src.rearrange("p t (g e) -> p t g e", g=2), ['ps%d' % pi], ['wvaug'])
                    else:
                        act(gates[:, tsl, :], src[:, :, 0:24], AF.Sigmoid, ['ps%d' % pi], ['gates'])
        S.barrier()
        if stop_after == 'P':
            break

        dma('w1b', [(A1[:, 0:4096].bitcast(BF16), w1s)], ['pcA'], ['w1b'])
        for kv in range(2):
            src = featT[F_CK + kv]
            pcol = P_PEK if kv == 0 else P_PEV
            for l in range(32):
                cb_i = l % 2
                ts('dve' if l % 2 else 'pool', cmpt[cb_i][:, 0:127], src[:, l:l + 16 * 126 + 1:16], pv[:, pcol + l:pcol + l + 1], None,
                   ALU.add, None, ['featT%d' % (F_CK + kv), 'pv'], ['cmpt%d' % cb_i])
                for g in range(2):
                    mm(PS[4 + g][:, 0:127], w1b[g * 64:(g + 1) * 64, kv, l, :], cmpt[cb_i][g * 64:(g + 1) * 64, 0:127],
                       l == 0, l == 31, ['w1b', 'cmpt%d' % cb_i], ['ps%d' % (4 + g)])
            for g in range(2):
                hp = hpre[:, g * 128:g * 128 + 127]
                h2 = hpre[:, 256 + g * 128:256 + g * 128 + 127]
                cp('dve', hp, PS[4 + g][:, 0:127], ['ps%d' % (4 + g)], ['hp%d' % g])
                tt('dve', h2, hp, hp, ALU.mult, ['hp%d' % g], ['h2%d' % g])
                ts('dve', h2, h2, 0.044715, 1.0, ALU.mult, ALU.add, ['h2%d' % g], ['h2b%d' % g])
                tt('dve', h2, h2, hp, ALU.mult, ['h2b%d' % g, 'hp%d' % g], ['h2c%d' % g])
                act(h2, h2, AF.Sigmoid, ['h2c%d' % g], ['h2d%d' % g], scale=1.5957691216057308)
                tt('dve', hidv[:, kv * 2 + g, 0:127], h2, hp, ALU.mult, ['h2d%d' % g, 'hp%d' % g], ['hid'])
        for g in range(2):
            mm(PS[6][:, 0:127], w2b[:, 0:128], hidv[:, g, 0:127], True, True, ['w2b', 'hid'], ['ps6'])
            cp('dve', kcT[g * 64:(g + 1) * 64, 0:127], PS[6][g * 64:(g + 1) * 64, 0:127], ['ps6'], ['kcT'])
            mm(PS[7][0:127, 0:64], hidv[:, 2 + g, 0:127], w2b[:, 128:192], True, True, ['w2b', 'hid'], ['ps7'])
            cp('dve', vcv[0:127, g, 0:64], PS[7][0:127, 0:64], ['ps7'], ['vcaug'])
            S.add('pool', lambda e, g=g: e.memset(vcv[0:127, g, 64:65], 1.0), reads=[], writes=['vcaug'])
            cp('pool', vcv[0:127, g, 65:97], cb[0:127, C_OVL:C_OVL + 32], ['cb'], ['vcaug'])
        S.barrier()
        if stop_after == 'C':
            break

        steps = []

        def attn_pair(qg, insts, span, far_mask, post):
            kts = range(max(0, 4 * qg - span), 4 * qg + 4)
            started = set()
            for kt in kts:
                qlo = max(4 * qg, kt)
                qhi = min(4 * qg + 3, kt + span)
                if qhi < qlo:
                    continue
                q0 = qlo * 128
                n = (qhi - qlo + 1) * 128
                info = {}

                def score(kt=kt, qlo=qlo, qhi=qhi, q0=q0, n=n, info=info):
                    sts = []
                    for I in insts:
                        pi = rr['ps'] % 4
                        rr['ps'] += 1
                        sts.append((PS[pi], 'ps%d' % pi))
                    for I, (st, sres) in zip(insts, sts):
                        mm(st[:, 0:n], I['kT'](kt), I['qT'](q0, n), True, False, [I['kres'], I['qres']], [sres])
                    for I, (st, sres) in zip(insts, sts):
                        if kt == qlo:
                            mm(st[:, 0:128], ident, cb[:, C_TLO:C_TLO + 128], False, False, ['cb'], [sres])
                        if far_mask and qhi == kt + span:
                            mm(st[:, n - 128:n], ident, cb[:, C_THI:C_THI + 128], False, False, ['cb'], [sres])
                    for I, (st, sres) in zip(insts, sts):
                        if I['selg'] is not None:
                            r0 = I['r0']
                            mm(st[:, 0:n], cb[r0:r0 + 32, C_E + kt * 128:C_E + (kt + 1) * 128],
                               selbTv[r0:r0 + 32, I['selg'], q0 - qg * 512:q0 - qg * 512 + n], False, False, ['cb', 'selbT'], [sres])
                    for I, (st, sres) in zip(insts, sts):
                        r0 = I['r0']
                        si = I['slope']
                        mm(st[:, 0:n], cb[r0:r0 + 9, C_ALA + si * 128:C_ALA + (si + 1) * 128], cb[r0:r0 + 9, C_ALB:C_ALB + n],
                           False, True, ['cb'], [sres])
                    pjs = []
                    m_off = qlo - kt
                    for I, (st, sres) in zip(insts, sts):
                        pj = rr['pt'] % 6
                        rr['pt'] += 1
                        si = I['slope']
                        act(pt[pj][:, 0:n], st[:, 0:n], AF.Exp, [sres, 'cf'], ['pt%d' % pj],
                            bias=cf[:, C_BT + si * 16 + m_off:C_BT + si * 16 + m_off + 1])
                        pjs.append(pj)
                    info['pjs'] = pjs

                def pv(kt=kt, qlo=qlo, qhi=qhi, info=info):
                    for I, pj in zip(insts, info['pjs']):
                        for qt in range(qlo, qhi + 1):
                            oap, obank = I['o_slot'](qt - 4 * qg)
                            first = obank not in started
                            started.add(obank)
                            mm(oap, pt[pj][:, (qt - qlo) * 128:(qt - qlo + 1) * 128], I['v'](kt), first, kt == qt,
                               ['pt%d' % pj, I['vres']], I['ores'])
                steps.append([score, pv, None])
            steps[-1][2] = post

        def flush_steps():
            prev = None
            for stp in steps:
                stp[0]()
                if prev is not None:
                    prev[1]()
                    if prev[2] is not None:
                        prev[2]()
                prev = stp
            if prev is not None:
                prev[1]()
                if prev[2] is not None:
                    prev[2]()
            del steps[:]

        for h in range(4):
            for qg in range(4):
                tsl = slice(qg * 4, qg * 4 + 4)
                insts = []
                for c in range(2):
                    ob = 4 + 2 * c

                    def o_slot(qi, ob=ob):
                        return PS[ob + qi // 2][:, (qi % 2) * 256:(qi % 2) * 256 + 129], ob + qi // 2
                    insts.append(dict(
                        kT=lambda kt, h=h, c=c: featT[F_DK + h][c * 64:(c + 1) * 64, kt * 128:(kt + 1) * 128],
                        qT=lambda q0, n, h=h, c=c: featT[F_DQ + h][c * 64:(c + 1) * 64, q0:q0 + n],
                        v=lambda kt, h=h: dvaug[:, kt, h, 0:129],
                        slope=h, r0=64 * c, selg=None, o_slot=o_slot,
                        kres='featT%d' % (F_DK + h), qres='featT%d' % (F_DQ + h), vres='dvaug',
                        ores=['o%d' % ob, 'o%d' % (ob + 1)]))

                def post(h=h, qg=qg, tsl=tsl):
                    for c in range(2):
                        ob = 4 + 2 * c
                        ores = ['o%d' % ob, 'o%d' % (ob + 1)]
                        for half in range(2):
                            ov = PS[ob + half][:, :].rearrange("p (a e) -> p a e", a=2)
                            rs = sm[:, 140 + c * 4 + half * 2:140 + c * 4 + half * 2 + 2]
                            S.add('dve', lambda e, rs=rs, ov=ov: e.reciprocal(out=rs.unsqueeze(2), in_=ov[:, :, 128:129]), reads=ores, writes=['rs%d%d' % (c, half)])
                            dst = (tmp0 if c == 0 else tmpa)[:, half * 256:(half + 1) * 256].rearrange("p (a e) -> p a e", a=2)
                            tt('dve', dst, ov[:, :, 0:128], rs.unsqueeze(2).to_broadcast([128, 2, 128]), ALU.mult,
                               ores + ['rs%d%d' % (c, half)], ['tmp%d' % c])
                    stt('dve', tmpb[:, :], tmpa[:, :], neglam, tmp0[:, :], ALU.mult, ALU.add, ['tmp0', 'tmp1', 'neglam'], ['tmpb'])
                    for qi in range(4):
                        act(tmpa[:, qi * 128:(qi + 1) * 128], tmpb[:, qi * 128:(qi + 1) * 128], AF.Square, ['tmpb'], ['tmp1'],
                            accum=sm[:, 150 + qi:151 + qi])
                    ts('dve', sm[:, 156:160], sm[:, 150:154], 1.0 / 128.0, 1e-5, ALU.mult, ALU.add, ['tmp1'], ['rms_a'])
                    act(sm[:, 156:160], sm[:, 156:160], AF.Sqrt, ['rms_a'], ['rms_b'])
                    S.add('dve', lambda e: e.reciprocal(out=sm[:, 160:164], in_=sm[:, 156:160]), reads=['rms_b'], writes=['rms_c'])
                    tb3 = tmpb[:, :].rearrange("p (a e) -> p a e", a=4)
                    tt('dve', tb3, tb3, sm[:, 160:164].unsqueeze(2).to_broadcast([128, 4, 128]), ALU.mult, ['tmpb', 'rms_c'], ['tmpb', 'tmpb2'])
                    tt('pool', cat[:, tsl, h * 128:(h + 1) * 128], tb3, gsub.unsqueeze(1).to_broadcast([128, 4, 128]), ALU.mult,
                       ['tmpb2', 'gsub'], ['cat'])
                attn_pair(qg, insts, MMAX[h], False, post)
        if stop_after == 'D':
            flush_steps()
            break

        for qg in range(4):
            tsl = slice(qg * 4, qg * 4 + 4)
            for h in range(8):
                info = {}

                def cscore(h=h, qg=qg, info=info):
                    base = (h // 4) * 64
                    j = h % 4
                    pi = rr['ps'] % 4
                    rr['ps'] += 1
                    st = PS[pi]
                    sres = 'ps%d' % pi
                    mm(st[0:127, :], kcT[base:base + 64, 0:127], featT[F_NQ + j][base:base + 64, qg * 512:(qg + 1) * 512], True, False,
                       ['kcT', 'featT%d' % (F_NQ + j)], [sres])
                    mm(st[0:127, :], cb[0:127, C_ID:C_ID + 127], cb[0:127, C_MCMP + qg * 512:C_MCMP + (qg + 1) * 512], False, True, ['cb'], [sres])
                    pj = rr['pt'] % 6
                    rr['pt'] += 1
                    act(pt[pj][0:127, :], st[0:127, :], AF.Exp, [sres], ['pt%d' % pj])
                    info['pj'] = pj

                def cpv(h=h, info=info):
                    g = h // 4
                    pj = info['pj']
                    ob = 4 + (h % 2)
                    ov = PS[ob][:, 0:400].rearrange("p (a e) -> p a e", a=4)
                    for qi in range(4):
                        mm(ov[:, qi, 0:97], pt[pj][0:127, qi * 128:(qi + 1) * 128], vcv[0:127, g, 0:97], True, True, ['pt%d' % pj, 'vcaug'], ['o%d' % ob])

                def cpost(h=h, qg=qg, tsl=tsl):
                    g = h // 4
                    j = h % 4
                    ob = 4 + (h % 2)
                    ores = 'o%d' % ob
                    ov = PS[ob][:, 0:400].rearrange("p (a e) -> p a e", a=4)
                    rs = sm[:, 170:174]
                    ts('dve', rs.unsqueeze(2), ov[:, :, 64:65], 1e-30, None, ALU.max, None, [ores], ['rsA'])
                    S.add('dve', lambda e, rs=rs: e.reciprocal(out=rs, in_=rs), reads=['rsA'], writes=['rsB'])
                    sg = sm[:, 174:178]
                    tt('dve', sg.unsqueeze(2), rs.unsqueeze(2), gates[:, tsl, h * 3:h * 3 + 1], ALU.mult, ['rsB', 'gates'], ['sgA'])
                    tt('dve', accv[:, :, h, :], ov[:, :, 0:64], sg.unsqueeze(2).to_broadcast([128, 4, 64]), ALU.mult, [ores, 'sgA'], ['acc%d' % h])
                    if not DBG_BR[0]:
                        S.add('dve', lambda e, h=h: e.memset(accv[:, :, h, :], 0.0), reads=[], writes=['acc%d' % h])
                    if j == 0:
                        tt('dve', impv[:, :, g, :], ov[:, :, 65:97], rs.unsqueeze(2).to_broadcast([128, 4, 32]), ALU.mult, [ores, 'rsB'], ['imp%d' % g])
                    else:
                        tt('dve', tmp0[:, 0:128].rearrange("p (a e) -> p a e", a=4), ov[:, :, 65:97], rs.unsqueeze(2).to_broadcast([128, 4, 32]),
                           ALU.mult, [ores, 'rsB'], ['tmp0'])
                        tt('dve', impv[:, :, g, :], impv[:, :, g, :], tmp0[:, 0:128].rearrange("p (a e) -> p a e", a=4), ALU.add,
                           ['tmp0', 'imp%d' % g], ['imp%d' % g])
                    if j != 3:
                        return
                    pi = rr['ps'] % 4
                    rr['ps'] += 1
                    pbb = PS[pi][:, :].bitcast(BF16)
                    for qi in range(4):
                        qt = qg * 4 + qi
                        val = selw[:, 0:32]
                        tt('dve', val, impv[:, qi, g, :], cf[:, C_FB + qt * 32:C_FB + (qt + 1) * 32], ALU.add, ['imp%d' % g, 'cf'], ['sw_a'])
                        S.add('dve', lambda e: e.max(out=selw[:, 32:40], in_=selw[:, 0:32]), reads=['sw_a'], writes=['sw_b'])
                        S.add('dve', lambda e: e.match_replace(out=selw[:, 64:96], in_to_replace=selw[:, 32:40], in_values=selw[:, 0:32], imm_value=-1e9),
                              reads=['sw_a', 'sw_b'], writes=['sw_c'])
                        S.add('dve', lambda e: e.max(out=selw[:, 40:48], in_=selw[:, 64:96]), reads=['sw_c'], writes=['sw_d'])
                        S.add('dve', lambda e: e.tensor_reduce(out=selw[:, 48:49], in_=selw[:, 40:48], axis=AX.X, op=ALU.min), reads=['sw_d'], writes=['sw_e'])
                        ts('dve', selbv[:, qi, g, :], val, selw[:, 48:49], NEG, ALU.is_lt, ALU.mult, ['sw_a', 'sw_e'], ['selb'])
                        tr(pbb[64 * g:64 * g + 32, qi * 128:(qi + 1) * 128], selbv[:, qi, g, :], ['selb'], ['ps%d' % pi])
                    cp('act', selbTv[64 * g:64 * g + 32, g, :], pbb[64 * g:64 * g + 32, 0:512], ['ps%d' % pi], ['selbT'])
                steps.append([cscore, cpv, cpost])
            for br in (1, 0):
                for j in range(4):
                    insts = []
                    for g in range(2):
                        h = g * 4 + j
                        base = g * 64
                        ob = 4 + 2 * (j % 2) + g
                        ovv = PS[ob][:, 0:264].rearrange("p (a e) -> p a e", a=4)

                        def o_slot(qi, ovv=ovv, ob=ob):
                            return ovv[:, qi, 0:65], ob
                        kf = F_SK if br == 0 else F_WK
                        va = svaug if br == 0 else wvaug
                        insts.append(dict(
                            kT=lambda kt, kf=kf, base=base: featT[kf][base:base + 64, kt * 128:(kt + 1) * 128],
                            qT=lambda q0, n, j=j, base=base: featT[F_NQ + j][base:base + 64, q0:q0 + n],
                            v=lambda kt, va=va, g=g: va[:, kt, g, 0:65],
                            slope=4 + h, r0=base, selg=(g if br == 0 else None), o_slot=o_slot,
                            kres='featT%d' % kf, qres='featT%d' % (F_NQ + j), vres='svaug' if br == 0 else 'wvaug',
                            ores=['o%d' % ob]))

                    def post(j=j, br=br, tsl=tsl):
                        for g in range(2):
                            h = g * 4 + j
                            ob = 4 + 2 * (j % 2) + g
                            ores = ['o%d' % ob]
                            ovv = PS[ob][:, 0:264].rearrange("p (a e) -> p a e", a=4)
                            rs = sm[:, 180 + g * 8:184 + g * 8]
                            sg = sm[:, 184 + g * 8:188 + g * 8]
                            rk = 'rs%d' % g
                            S.add('dve', lambda e, rs=rs, ovv=ovv: e.reciprocal(out=rs.unsqueeze(2), in_=ovv[:, :, 64:65]), reads=ores, writes=[rk + 'A'])
                            tt('dve', sg.unsqueeze(2), rs.unsqueeze(2), gates[:, tsl, h * 3 + 1 + br:h * 3 + 2 + br], ALU.mult, [rk + 'A', 'gates'], [rk + 'B'])
                            tv = (tmpa if g else tmpb)[:, 0:256].rearrange("p (a e) -> p a e", a=4)
                            tk = 'tv%d' % g
                            tt('dve', tv, ovv[:, :, 0:64], sg.unsqueeze(2).to_broadcast([128, 4, 64]), ALU.mult, ores + [rk + 'B'], [tk])
                            if not DBG_BR[1 + br]:
                                S.add('dve', lambda e, tv=tv: e.memset(tv, 0.0), reads=[], writes=[tk])
                            if br == 1:
                                tt('pool', accv[:, :, h, :], accv[:, :, h, :], tv, ALU.add, [tk, 'acc%d' % h], ['acc%d' % h])
                            else:
                                tt('pool', cat[:, tsl, 512 + h * 64:512 + (h + 1) * 64], accv[:, :, h, :], tv, ALU.add, [tk, 'acc%d' % h], ['cat'])
                    attn_pair(qg, insts, (4 if br == 1 else max(MMAX[4 + j], MMAX[8 + j])), br == 1, post)
        flush_steps()
        S.barrier()
        if stop_after == 'N':
            break

        dma('woutb', [(A2[:, 0:4096].bitcast(BF16), wos)], ['pcB'], ['woutb'])
        dma('lnp', [(lnp[:, a, :], lnc[a].partition_broadcast(128)) for a in range(4)], [], ['lnp'])

        def layer_norm(src, srck, dst, dstk, gi):
            S.add('dve', lambda e: e.bn_stats(out=sm[:, 200:206], in_=src[:, 0:512]), reads=[srck], writes=['bn_a'])
            S.add('dve', lambda e: e.bn_stats(out=sm[:, 206:212], in_=src[:, 512:1024]), reads=[srck], writes=['bn_b'])
            S.add('dve', lambda e: e.bn_aggr(out=sm[:, 212:214], in_=sm[:, 200:212]), reads=['bn_a', 'bn_b'], writes=['bn_c'])
            ts('dve', sm[:, 214:215], sm[:, 213:214], 1e-5, None, ALU.add, None, ['bn_c'], ['bn_d'])
            act(sm[:, 214:215], sm[:, 214:215], AF.Sqrt, ['bn_d'], ['bn_e'])
            S.add('dve', lambda e: e.reciprocal(out=sm[:, 215:216], in_=sm[:, 214:215]), reads=['bn_e'], writes=['bn_f'])
            stt('dve', src, src, sm[:, 212:213], lnp[:, gi, :], ALU.subtract, ALU.mult, [srck, 'bn_c', 'lnp'], [srck])
            stt('dve', dst, src, sm[:, 215:216], lnp[:, gi + 1, :], ALU.mult, ALU.add, [srck, 'bn_f', 'lnp'], [dstk])

        for tg in range(4):
            def stage1(ti, tg=tg):
                t = tg * 4 + ti
                b = t % 2
                dma('xs%d' % b, [(xs[b][:, :], x[s, t * 128:(t + 1) * 128, :])], [], ['xs%d' % b])
                pbb = PS[3][:, :].bitcast(BF16)
                for c in range(8):
                    tr(pbb[:, c * 128:(c + 1) * 128], cat[:, t, c * 128:(c + 1) * 128], ['cat'], ['ps3'])
                cT = catT2[b]
                cp('act', cT[:, :, :], pbb.rearrange("p (c t) -> p c t", c=8), ['ps3'], ['catT%d' % b])
                for hf in range(2):
                    ob = 4 + 2 * b + hf
                    for k in range(8):
                        mm(PS[ob][:, :], cT[:, k, :], woutb[:, k, hf * 512:(hf + 1) * 512], k == 0, k == 7, ['catT%d' % b, 'woutb'], ['o%d' % ob])
                    stt('dve', xs[b][:, hf * 512:(hf + 1) * 512], xs[b][:, hf * 512:(hf + 1) * 512], float(ALPHA), PS[ob][:, :],
                        ALU.mult, ALU.add, ['xs%d' % b, 'o%d' % ob], ['xs%d' % b])
                layer_norm(xs[b][:, :], 'xs%d' % b, x1[:, ti, :], 'x1_%d' % ti, 0)
                cp('act', x1b2[b][:, :], x1[:, ti, :], ['x1_%d' % ti], ['x1b%d' % b])

            def stage2(ti, tg=tg):
                b = (tg * 4 + ti) % 2
                pbb2 = PS[2][:, :].bitcast(BF16)
                for c in range(8):
                    tr(pbb2[:, c * 128:(c + 1) * 128], x1b2[b][:, c * 128:(c + 1) * 128], ['x1b%d' % b], ['ps2'])
                cp('act', x1T[:, :, ti * 128:(ti + 1) * 128], pbb2.rearrange("p (c t) -> p c t", c=8), ['ps2'], ['x1T'])
            for ti in range(4):
                stage1(ti)
                if ti >= 1:
                    stage2(ti - 1)
            stage2(3)
            for j in range(NJ):
                wb1 = rr['w'] % 6
                rr['w'] += 1
                dma('wbf%d' % wb1, [(wbf[wb1][:, :], wgs[j])], ['pcB'], ['wbf%d' % wb1])
                wb2 = rr['w'] % 6
                rr['w'] += 1
                dma('wbf%d' % wb2, [(wbf[wb2][:, :], wus[j])], ['pcB'], ['wbf%d' % wb2])
                wgv = wbf[wb1][:, :].rearrange("p (c n) -> p c n", c=8)
                wuv = wbf[wb2][:, :].rearrange("p (c n) -> p c n", c=8)
                pg = (j % 2) * 2
                for k in range(8):
                    mm(PS[pg][:, :], wgv[:, k, :], x1T[:, k, :], k == 0, k == 7, ['wbf%d' % wb1, 'x1T'], ['ps%d' % pg])
                for k in range(8):
                    mm(PS[pg + 1][:, :], wuv[:, k, :], x1T[:, k, :], k == 0, k == 7, ['wbf%d' % wb2, 'x1T'], ['ps%d' % (pg + 1)])
                sgb = tmp0 if j % 2 == 0 else tmpa
                sgk = 'tmp0' if j % 2 == 0 else 'tmp1'
                act(sgb[:, :], PS[pg][:, :], AF.Silu, ['ps%d' % pg], [sgk])
                tt('dve', aT[:, j, :], sgb[:, :], PS[pg + 1][:, :], ALU.mult, [sgk, 'ps%d' % (pg + 1)], ['aT'])
            for hf in range(2):
                for j in range(NJ):
                    wb = rr['w'] % 6
                    rr['w'] += 1
                    dma('wbf%d' % wb, [(wbf[wb][:, 0:512], wds[j, :, hf * 512:(hf + 1) * 512])], ['pcB'], ['wbf%d' % wb])
                    for ti in range(4):
                        mm(PS[4 + ti][:, :], aT[:, j, ti * 128:(ti + 1) * 128], wbf[wb][:, 0:512], j == 0, j == NJ - 1,
                           ['aT', 'wbf%d' % wb], ['o%d' % (4 + ti)])
                for ti in range(4):
                    stt('dve', x1[:, ti, hf * 512:(hf + 1) * 512], x1[:, ti, hf * 512:(hf + 1) * 512], float(ALPHA), PS[4 + ti][:, :],
                        ALU.mult, ALU.add, ['x1_%d' % ti, 'o%d' % (4 + ti)], ['x1_%d' % ti])
            for ti in range(4):
                t = tg * 4 + ti
                ob = ti % 2
                layer_norm(x1[:, ti, :], 'x1_%d' % ti, ost[ob][:, :], 'ost%d' % ob, 2)
                dma('out%d' % ob, [(out[s, t * 128:(t + 1) * 128, :], ost[ob][:, :])], ['ost%d' % ob], ['outd'])
        S.barrier()

    if dbg:
        S.barrier()
        lst = []
        for c0 in range(0, 8192, 2048):
            lst.append((dbg_o[:, c0:c0 + 2048], A1[:, c0:c0 + 2048]))
        for c0 in range(0, 23040, 2048):
            c1 = min(23040, c0 + 2048)
            lst.append((dbg_o[:, 8192 + c0:8192 + c1], A2[:, c0:c1]))
        lst += [(dbg_o[:, 31232:31232 + 2048], acc[:, :]), (dbg_o[:, 33280:33280 + 256], sm[:, :]), (dbg_o[:, 33536:33536 + 256], impacc[:, :])]
        lst += [(dbg_o[:, 33792:33856], kcT[:, :].bitcast(F32)), (dbg_o[:, 33856:33956], vcaug[:, :].bitcast(F32)),
                (dbg_o[:, 33956:34212], hid[:, :].bitcast(F32)), (dbg_o[:, 34212:34340], selb[:, :].bitcast(F32)),
                (dbg_o[:, 34340:34340 + 512], selbT[:, :].bitcast(F32))]
        dma('dbg', lst, [], ['dbgd'])
    S.barrier()
    stuck = S.check_deadlock()
    assert not stuck, stuck
    S.emit()
    es.close()
    return nc


def host_prep(inputs):
    f = lambda a: np.ascontiguousarray(a, dtype=np.float32)
    w_in = f(inputs["w_in"][0])
    winc = np.zeros((NCH, 128, 8, 128), np.float32)
    for ci, (pieces, kind, scale) in enumerate(CH):
        o = 0
        for (c0, ncol) in pieces:
            blk = w_in[:, c0:c0 + ncol].reshape(8, 128, ncol).transpose(1, 0, 2)
            winc[ci, :, :, o:o + ncol] = blk
            o += ncol
    winc = winc.reshape(NCH, 128, 1024)

    def chunk_cols(w):
        return np.ascontiguousarray(w.reshape(8, 128, NJ, 128).transpose(2, 1, 0, 3)).reshape(NJ, 128, 1024)
    wgc = chunk_cols(f(inputs["w_gate"][0]))
    wuc = chunk_cols(f(inputs["w_up"][0]))
    wdc = np.ascontiguousarray(f(inputs["w_down"][0]).reshape(NJ, 128, 1024))
    woc = np.ascontiguousarray(f(inputs["w_out"][0]).reshape(8, 128, 1024).transpose(1, 0, 2)).reshape(128, 8192)
    w1 = np.stack([f(inputs["cmp_w1_k"][0]), f(inputs["cmp_w1_v"][0])], 0)
    w1 = w1.reshape(2, 32, 64, 128).transpose(2, 0, 1, 3).reshape(64, 8192)
    w1c = np.ascontiguousarray(np.concatenate([w1, w1], 0))
    w2k = f(inputs["cmp_w2_k"][0])
    w2v = f(inputs["cmp_w2_v"][0])
    w2c = np.ascontiguousarray(np.concatenate([w2k, w2k, w2v], 1))
    lnc = np.ascontiguousarray(np.stack([f(inputs["ln1_g"][0]), f(inputs["ln1_b"][0]), f(inputs["ln2_g"][0]), f(inputs["ln2_b"][0])], 0))
    pvc = np.zeros((128, NPV), np.float32)
    pvc[:, 0:64] = f(inputs["diff_lq1"][0])[None, :]
    pvc[:, 64:128] = f(inputs["diff_lk1"][0])[None, :]
    pvc[:, 128:192] = f(inputs["diff_lq2"][0])[None, :]
    pvc[:, 192:256] = f(inputs["diff_lk2"][0])[None, :]
    pvc[:, P_SUBG:P_SUBG + 128] = f(inputs["diff_subln_g"][0])[None, :]
    pek = f(inputs["cmp_pe_k"][0]).T
    pev = f(inputs["cmp_pe_v"][0]).T
    pvc[:, P_PEK:P_PEK + 32] = np.concatenate([pek, pek], 0)
    pvc[:, P_PEV:P_PEV + 32] = np.concatenate([pev, pev], 0)
    cb16, cf32 = make_consts()
    return dict(winc=winc, wgc=wgc, wuc=wuc, wdc=wdc, woc=woc, w1c=w1c, w2c=w2c, lnc=lnc, pvc=pvc, cb16=cb16, cf32=cf32)


def kernel(**inputs):
    x = np.ascontiguousarray(inputs["x"], dtype=np.float32)
    shared = host_prep(inputs)
    ncores = 8
    nseq = x.shape[0] // ncores
    nc = build(nseq)
    in_maps = []
    for c in range(ncores):
        m = dict(shared)
        m["x"] = np.ascontiguousarray(x[c * nseq:(c + 1) * nseq])
        in_maps.append(m)
    res = run_bass_kernel_spmd(nc, in_maps, core_ids=list(range(ncores)))
    return np.concatenate([r["out"] for r in res.results], axis=0).astype(np.float32)
```

```python
import numpy as np
import ml_dtypes
from contextlib import ExitStack
import concourse.bass as bass
import concourse.mybir as mybir
from concourse.bass_utils import run_bass_kernel_spmd

F32 = mybir.dt.float32
BF16 = mybir.dt.bfloat16
AF = mybir.ActivationFunctionType
ALU = mybir.AluOpType
AX = mybir.AxisListType

T = 2048
DM = 1024
NT = 16
DFF = 2816
NJ = 22
NEG = -30000.0
ALPHA = 2.0 ** 0.25
LAM_INIT = 0.2
DBG_BR = [1, 1, 1]
SLOPES = (2.0 ** (-8.0 * (np.arange(12) + 1) / 12)).astype(np.float32)
MMAX = []
for _s in SLOPES:
    _m = 15
    while _m > 0 and float(_s) * (128 * (_m - 1) + 1) > 144.0:
        _m -= 1
    MMAX.append(_m)

OFF_DQ, OFF_DK, OFF_DV, OFF_NQ = 0, 512, 1024, 1536
OFF_CK, OFF_CV, OFF_SK, OFF_SV, OFF_WK, OFF_WV, OFF_G = 2048, 2176, 2304, 2432, 2560, 2688, 2816
CH = []
for h in range(4):
    CH.append(([(OFF_DQ + h * 128, 128)], 'F', 0.125))
for h in range(4):
    CH.append(([(OFF_DK + h * 128, 128)], 'F', 1.0))
for j in range(4):
    CH.append(([(OFF_NQ + j * 64, 64), (OFF_NQ + (j + 4) * 64, 64)], 'F', 0.125))
CH.append(([(OFF_CK, 128)], 'F', 1.0))
CH.append(([(OFF_CV, 128)], 'F', 1.0))
CH.append(([(OFF_SK, 128)], 'F', 1.0))
CH.append(([(OFF_WK, 128)], 'F', 1.0))
for h in range(4):
    CH.append(([(OFF_DV + h * 128, 128)], 'T', 1.0))
CH.append(([(OFF_SV, 128)], 'T', 1.0))
CH.append(([(OFF_WV, 128)], 'T', 1.0))
CH.append(([(OFF_G, 24)], 'T', 1.0))
NCH = len(CH)
F_DQ, F_DK, F_NQ, F_CK, F_CV, F_SK, F_WK = 0, 4, 8, 12, 13, 14, 15

C_ID = 0
C_TLO = 128
C_THI = 256
C_MCMP = 384
C_E = C_MCMP + 2048
C_ALA = C_E + 2048
C_ALB = C_ALA + 12 * 128
C_OVL = C_ALB + 512
NC16 = C_OVL + 32
C_FB = 0
C_BT = 512
NC32 = C_BT + 192
P_LQ = 0
P_SUBG = 256
P_PEK = 384
P_PEV = 416
NPV = 448


def _split3(v):
    v = np.asarray(v, np.float64)
    a = v.astype(np.float32).astype(ml_dtypes.bfloat16)
    r = v - a.astype(np.float64)
    b = r.astype(np.float32).astype(ml_dtypes.bfloat16)
    r = r - b.astype(np.float64)
    c = r.astype(np.float32).astype(ml_dtypes.bfloat16)
    return a, b, c


def make_consts():
    cb = np.zeros((128, NC16), np.float32)
    cb[:, C_ID:C_ID + 128] = np.eye(128)
    r = np.arange(128)[:, None]
    j = np.arange(128)[None, :]
    cb[:, C_TLO:C_TLO + 128] = np.where(j < r, NEG, 0.0)
    cb[:, C_THI:C_THI + 128] = np.where(j >= r, NEG, 0.0)
    n = np.arange(127)[:, None]
    t = np.arange(T)[None, :]
    cb[:127, C_MCMP:C_MCMP + T] = np.where(t < 16 * n + 31, NEG, 0.0)
    s = np.arange(32)[:, None]
    m = np.arange(T)[None, :]
    cb[:32, C_E:C_E + T] = (m // 64 == s).astype(np.float32)
    cb16 = cb.astype(ml_dtypes.bfloat16)
    jj = np.arange(512)
    for si in range(12):
        sv = float(SLOPES[si])
        p = _split3(np.full(128, sv))
        q = _split3(sv * np.arange(128, dtype=np.float64))
        for k in range(3):
            cb16[k, C_ALA + si * 128:C_ALA + (si + 1) * 128] = p[k]
            cb16[3 + k, C_ALA + si * 128:C_ALA + (si + 1) * 128] = p[k]
            cb16[6 + k, C_ALA + si * 128:C_ALA + (si + 1) * 128] = q[k]
    for k in range(3):
        cb16[k, C_ALB:C_ALB + 512] = (-(jj // 2) * 2).astype(np.float32).astype(ml_dtypes.bfloat16)
        cb16[3 + k, C_ALB:C_ALB + 512] = (-(jj % 2)).astype(np.float32).astype(ml_dtypes.bfloat16)
        cb16[6 + k, C_ALB:C_ALB + 512] = np.ones(512, np.float32).astype(ml_dtypes.bfloat16)
    cb16[64:73, C_ALA:C_ALB + 512] = cb16[0:9, C_ALA:C_ALB + 512]
    cb16[64:96, C_E:C_E + T] = cb16[0:32, C_E:C_E + T]
    cs = np.arange(127)[:, None] * 16
    ss = np.arange(32)[None, :] * 64
    ovl = ((cs <= ss + 63) & (cs + 31 >= ss)).astype(np.float32)
    cb16[:127, C_OVL:C_OVL + 32] = ovl.astype(ml_dtypes.bfloat16)

    cf = np.zeros((128, NC32), np.float32)
    p = np.arange(128)
    for qt in range(16):
        cur = 2 * qt + p // 64
        for sb in range(32):
            v = np.zeros(128, np.float32)
            if sb == 0:
                v[:] = 1000.0
            v[sb == cur - 1] = 3000.0
            v[sb == cur] = 2000.0
            v[sb > cur] = -1000.0 - sb
            cf[:, C_FB + qt * 32 + sb] = v
    for si in range(12):
        for mm in range(16):
            cf[:, C_BT + si * 16 + mm] = -np.float32(SLOPES[si]) * 128.0 * mm
    return cb16, cf


class _Op:
    __slots__ = ('fn', 'waits', 'dma_key', 'needed', 'ms', 'dtok')

    def __init__(self, fn, waits, dma_key):
        self.fn = fn
        self.waits = waits
        self.dma_key = dma_key
        self.needed = False
        self.ms = 0


class Sched:
    COMPUTE = ('pe', 'act', 'dve', 'pool')
    ALL = ('pe', 'act', 'dve', 'pool', 'sp')

    def __init__(self, nc, es):
        self.nc = nc
        self.es = es
        self.ops = {e: [] for e in self.ALL}
        self.res = {}
        self.waited = {e: {} for e in self.ALL}
        self.dcount = {}
        self.dsem = {}
        self.esem = {e: es.enter_context(nc.semaphore('es_' + e)) for e in self.COMPUTE}

    def _need(self, eng, tok, waits):
        if tok is None:
            return
        key = (tok[0], tok[1])
        w = self.waited[eng]
        if w.get(key, 0) >= tok[2]:
            return
        w[key] = tok[2]
        waits.append(tok)
        if tok[0] == 'e':
            self.ops[tok[1]][tok[2] - 1].needed = True

    def add(self, eng, fn, reads=(), writes=(), dma_key=None, ndma=1):
        waits = []
        is_dma = dma_key is not None
        for r in reads:
            st = self.res.get(r)
            if st is not None:
                self._need(eng, st[0], waits)
        for w in writes:
            st = self.res.get(w)
            if st is not None:
                lw = st[0]
                if lw is not None and (is_dma or not (lw[0] == 'e' and lw[1] == eng)):
                    self._need(eng, lw, waits)
                for tk in st[1].values():
                    if is_dma or not (tk[0] == 'e' and tk[1] == eng):
                        self._need(eng, tk, waits)
        op = _Op(fn, waits, dma_key)
        self.ops[eng].append(op)
        if is_dma:
            if dma_key not in self.dsem:
                self.dsem[dma_key] = self.es.enter_context(self.nc.semaphore('ds_' + dma_key))
                self.dcount[dma_key] = 0
            self.dcount[dma_key] += 16 * ndma
            tok = ('d', dma_key, self.dcount[dma_key])
            op.dtok = self.dcount[dma_key]
        else:
            tok = ('e', eng, len(self.ops[eng]))
        for r in reads:
            st = self.res.setdefault(r, [None, {}])
            st[1][(tok[0], tok[1])] = tok
        for w in writes:
            self.res[w] = [tok, {}]
        return tok

    def barrier(self):
        toks = []
        for e in self.COMPUTE:
            n = len(self.ops[e])
            while n > 0 and (self.ops[e][n - 1].fn is None or self.ops[e][n - 1].dma_key is not None):
                n -= 1
            if n > 0:
                toks.append(('e', e, n))
        for k, c in self.dcount.items():
            toks.append(('d', k, c))
        for e in self.ALL:
            waits = []
            for tk in toks:
                if tk[0] == 'e' and tk[1] == e:
                    continue
                self._need(e, tk, waits)
            self.ops[e].append(_Op(None, waits, None))

    def check_deadlock(self):
        ptr = {e: 0 for e in self.ALL}
        done_e = {e: 0 for e in self.COMPUTE}
        done_d = {}
        dcnt = {}
        progress = True
        while progress:
            progress = False
            for e in self.ALL:
                while ptr[e] < len(self.ops[e]):
                    op = self.ops[e][ptr[e]]
                    ok = True
                    for tk in op.waits:
                        if tk[0] == 'e':
                            if done_e[tk[1]] < tk[2]:
                                ok = False
                                break
                        else:
                            if done_d.get(tk[1], 0) < tk[2]:
                                ok = False
                                break
                    if not ok:
                        break
                    ptr[e] += 1
                    progress = True
                    if op.dma_key is not None:
                        pass
                    if e in done_e:
                        done_e[e] = ptr[e]
                    if op.dma_key is not None:
                        done_d[op.dma_key] = op.dtok
        stuck = {e: (ptr[e], len(self.ops[e])) for e in self.ALL if ptr[e] < len(self.ops[e])}
        return stuck

    def emit(self):
        nc = self.nc
        for e in self.COMPUTE:
            m = 0
            for op in self.ops[e]:
                if op.needed:
                    assert op.fn is not None and op.dma_key is None
                    m += 1
                    op.ms = m

        def run(name):
            def f(eng):
                for op in self.ops[name]:
                    for tk in op.waits:
                        if tk[0] == 'e':
                            eng.wait_ge(self.esem[tk[1]], self.ops[tk[1]][tk[2] - 1].ms)
                        else:
                            eng.wait_ge(self.dsem[tk[1]], tk[2])
                    if op.fn is None:
                        continue
                    r = op.fn(eng)
                    if op.dma_key is not None:
                        for ins in r:
                            ins.then_inc(self.dsem[op.dma_key], 16)
                    elif op.needed:
                        r.then_inc(self.esem[name], 1)
            return f

        with nc.Block() as blk:
            blk.sync(run('sp'))
            blk.tensor(run('pe'))
            blk.vector(run('dve'))
            blk.scalar(run('act'))
            blk.gpsimd(run('pool'))


def build(nseq=4, dbg=False, stop_after=None):
    nc = bass.Bass("TRN2", target_bir_lowering=False)

    def DIN(name, shape, dt=F32):
        return nc.dram_tensor(name, shape, dt, kind="ExternalInput").ap()

    x = DIN("x", [nseq, T, DM])
    winc = DIN("winc", [NCH, 128, 1024])
    wgc = DIN("wgc", [NJ, 128, 1024])
    wuc = DIN("wuc", [NJ, 128, 1024])
    wdc = DIN("wdc", [NJ, 128, 1024])
    woc = DIN("woc", [128, 8192])
    w1c = DIN("w1c", [128, 8192])
    w2c = DIN("w2c", [128, 192])
    lnc = DIN("lnc", [4, DM])
    pvc = DIN("pvc", [128, NPV])
    cb16d = DIN("cb16", [128, NC16], BF16)
    cf32d = DIN("cf32", [128, NC32])
    out = nc.dram_tensor("out", [nseq, T, DM], F32, kind="ExternalOutput").ap()
    if dbg:
        dbg_o = nc.dram_tensor("dbg", [128, 35000], F32, kind="ExternalOutput").ap()

    def DSCR(name, shape):
        return nc.dram_tensor(name, shape, BF16, kind="Internal").ap()
    wins = DSCR("wins", [NCH, 128, 1024])
    wgs = DSCR("wgs", [NJ, 128, 1024])
    wus = DSCR("wus", [NJ, 128, 1024])
    wds = DSCR("wds", [NJ, 128, 1024])
    wos = DSCR("wos", [128, 8192])
    w1s = DSCR("w1s", [128, 8192])

    es = ExitStack()
    S = Sched(nc, es)

    def SB(name, cols, dt=F32):
        return es.enter_context(nc.sbuf_tensor(name, [128, cols], dt))

    A1 = SB("arena1", 8192)
    A2 = SB("arena2", 23040)
    xT = A1[:, :].bitcast(BF16).rearrange("p (c t) -> p c t", c=8)
    w1b = A1[:, 0:4096].bitcast(BF16).rearrange("p (k l h) -> p k l h", k=2, l=32)
    cat = A1[:, :].bitcast(BF16).rearrange("p (t f) -> p t f", t=16)
    featT = [A2[:, i * 1024:(i + 1) * 1024].bitcast(BF16) for i in range(16)]
    o2 = 16384
    dvaug = A2[:, o2:o2 + 4160].bitcast(BF16).rearrange("p (t h e) -> p t h e", t=16, h=4)
    o2 += 4160
    svaug = A2[:, o2:o2 + 1056].bitcast(BF16).rearrange("p (t g e) -> p t g e", t=16, g=2)
    o2 += 1056
    wvaug = A2[:, o2:o2 + 1056].bitcast(BF16).rearrange("p (t g e) -> p t g e", t=16, g=2)
    o2 += 1056
    gates = A2[:, o2:o2 + 384].rearrange("p (t c) -> p t c", t=16)
    woutb = A2[:, 0:4096].bitcast(BF16).rearrange("p (c n) -> p c n", c=8)
    lnp = A2[:, 4096:8192].rearrange("p (a n) -> p a n", a=4)
    aT = A2[:, 8192:8192 + 5632].bitcast(BF16).rearrange("p (j t) -> p j t", j=NJ)
    x1 = A2[:, 13824:13824 + 4096].rearrange("p (t n) -> p t n", t=4)
    x1T = A2[:, 17920:17920 + 2048].bitcast(BF16).rearrange("p (c t) -> p c t", c=8)
    catT = A2[:, 19968:19968 + 512].bitcast(BF16).rearrange("p (c t) -> p c t", c=8)
    x1b = A2[:, 20480:20480 + 512].bitcast(BF16)
    catT2 = [A2[:, 19968 + i * 512:19968 + (i + 1) * 512].bitcast(BF16).rearrange("p (c t) -> p c t", c=8) for i in range(2)]
    x1b2 = [A2[:, 20992 + i * 512:20992 + (i + 1) * 512].bitcast(BF16) for i in range(2)]

    acc = SB("acc", 2048)
    accv = acc[:, :].rearrange("p (t h d) -> p t h d", t=4, h=8)
    impacc = SB("impacc", 256)
    impv = impacc[:, :].rearrange("p (t g s) -> p t g s", t=4, g=2)
    wst = [SB("wst%d" % i, 256) for i in range(1)]
    wbf = [SB("wbf%d" % i, 1024, BF16) for i in range(6)]
    xs = [SB("xs%d" % i, 1024) for i in range(2)]
    xb = [SB("xb%d" % i, 1024, BF16) for i in range(2)]
    ost = [SB("ost%d" % i, 1024) for i in range(2)]
    lns = SB("lns", 64)
    pt = [SB("pt%d" % i, 512, BF16) for i in range(6)]
    cb = SB("cb", NC16, BF16)
    cf = SB("cf", NC32)
    pv = SB("pv", NPV)
    w2b = SB("w2b", 192, BF16)
    tmp0 = SB("tmp0", 512)
    tmpa = SB("tmpa", 512)
    tmpb = SB("tmpb", 512)
    sm = SB("sm", 256)
    selw = SB("selw", 256)
    selb = SB("selb", 256, BF16)
    selbT = SB("selbT", 1024, BF16)
    kcT = SB("kcT", 128, BF16)
    vcaug = SB("vcaug", 200, BF16)
    hid = SB("hid", 512, BF16)
    cmpt = [SB("cmpt%d" % i, 128, BF16) for i in range(2)]
    hpre = SB("hpre", 512)
    PS = [es.enter_context(nc.psum_tensor("ps%d" % i, [128, 512], F32)) for i in range(8)]

    selbv = selb[:, :].rearrange("p (t g s) -> p t g s", t=4, g=2)
    selbTv = selbT[:, :].rearrange("p (g q) -> p g q", g=2)
    vcv = vcaug[:, :].rearrange("p (g e) -> p g e", g=2)
    hidv = hid[:, :].rearrange("p (k n) -> p k n", k=4)

    ident = cb[:, C_ID:C_ID + 128]

    def dma(key, outs_ins, reads, writes, eng='sp'):
        def fn(e):
            return [e.dma_start(out=o, in_=i) for (o, i) in outs_ins]
        return S.add(eng, fn, reads=reads, writes=writes, dma_key=key, ndma=len(outs_ins))

    def mm(o, lhsT, rhs, start, stop, reads, writes):
        S.add('pe', lambda e: e.matmul(o, lhsT=lhsT, rhs=rhs, start=start, stop=stop), reads=reads, writes=writes)

    def tr(o, i, reads, writes):
        S.add('pe', lambda e: e.transpose(out=o, in_=i, identity=ident), reads=reads + ['cb'], writes=writes)

    def act(o, i, func, reads, writes, bias=None, scale=None, accum=None):
        kw = {}
        if bias is not None:
            kw['bias'] = bias
        if scale is not None:
            kw['scale'] = scale
        if accum is not None:
            kw['accum_out'] = accum
        S.add('act', lambda e: e.activation(out=o, in_=i, func=func, **kw), reads=reads, writes=writes)

    def cp(eng, o, i, reads, writes):
        if eng == 'act':
            S.add(eng, lambda e: e.activation(out=o, in_=i, func=AF.Copy), reads=reads, writes=writes)
        else:
            S.add(eng, lambda e: e.tensor_copy(out=o, in_=i), reads=reads, writes=writes)

    def tt(eng, o, a, b, op, reads, writes):
        S.add(eng, lambda e: e.tensor_tensor(out=o, in0=a, in1=b, op=op), reads=reads, writes=writes)

    def ts(eng, o, a, s1, s2, op0, op1, reads, writes):
        if op1 is None:
            S.add(eng, lambda e: e.tensor_scalar(out=o, in0=a, scalar1=s1, scalar2=None, op0=op0), reads=reads, writes=writes)
        else:
            S.add(eng, lambda e: e.tensor_scalar(out=o, in0=a, scalar1=s1, scalar2=s2, op0=op0, op1=op1), reads=reads, writes=writes)

    def stt(eng, o, a, sc, b, op0, op1, reads, writes):
        S.add(eng, lambda e: e.scalar_tensor_tensor(out=o, in0=a, scalar=sc, in1=b, op0=op0, op1=op1), reads=reads, writes=writes)

    dma('c0', [(cb[:, :], cb16d)], [], ['cb'])
    dma('c1', [(cf[:, :], cf32d), (pv[:, :], pvc)], [], ['cf', 'pv'])
    dma('c2', [(wst[0][:, 0:192], w2c)], [], ['wst0'])
    cp('pool', w2b[:, :], wst[0][:, 0:192], ['wst0'], ['w2b'])
    def precast_a():
        dma('pcA', [(wins[0:8], winc[0:8]), (wins[8:16], winc[8:16]), (wins[16:NCH], winc[16:NCH]), (w1s, w1c)], [], ['pcA'], eng='pool')

    def precast_b():
        dma('pcB', [(wos, woc)] + [(dst[j0:j0 + 6], src[j0:j0 + 6]) for (dst, src) in ((wgs, wgc), (wus, wuc), (wds, wdc)) for j0 in (0, 6, 12)]
            + [(dst[18:NJ], src[18:NJ]) for (dst, src) in ((wgs, wgc), (wus, wuc), (wds, wdc))], [], ['pcB'], eng='pool')

    tt('dve', sm[:, 0:64], pv[:, 0:64], pv[:, 64:128], ALU.mult, ['pv'], ['sm_a'])
    tt('dve', sm[:, 64:128], pv[:, 128:192], pv[:, 192:256], ALU.mult, ['pv'], ['sm_b'])
    S.add('dve', lambda e: e.tensor_reduce(out=sm[:, 128:129], in_=sm[:, 0:64], axis=AX.X, op=ALU.add), reads=['sm_a'], writes=['sm_c'])
    S.add('dve', lambda e: e.tensor_reduce(out=sm[:, 129:130], in_=sm[:, 64:128], axis=AX.X, op=ALU.add), reads=['sm_b'], writes=['sm_d'])
    act(sm[:, 130:131], sm[:, 128:129], AF.Exp, ['sm_c'], ['sm_e'])
    act(sm[:, 131:132], sm[:, 129:130], AF.Exp, ['sm_d'], ['sm_f'])
    tt('dve', sm[:, 132:133], sm[:, 131:132], sm[:, 130:131], ALU.subtract, ['sm_e', 'sm_f'], ['sm_g'])
    ts('dve', sm[:, 133:134], sm[:, 132:133], -LAM_INIT, None, ALU.add, None, ['sm_g'], ['neglam'])
    neglam = sm[:, 133:134]
    ts('dve', sm[:, 0:128], pv[:, P_SUBG:P_SUBG + 128], 1.0 - LAM_INIT, None, ALU.mult, None, ['pv', 'sm_a', 'sm_b', 'sm_c', 'sm_d'], ['gsub'])
    gsub = sm[:, 0:128]

    rr = {'ps': 0, 'w': 0, 'pt': 0, 'ev': 0}

    def evac_eng():
        rr['ev'] += 1
        return 'act' if rr['ev'] % 2 else 'dve'

    for s in range(nseq):
        for t in range(NT):
            b = t % 2
            dma('xb%d' % b, [(xb[b][:, :], x[s, t * 128:(t + 1) * 128, :])], [], ['xb%d' % b], eng='pool')
            if s == 0 and t == 1:
                precast_a()
            if s == 0 and t == NT - 1:
                precast_b()
            pb = PS[t % 2]
            pbb = pb[:, :].bitcast(BF16)
            for c in range(8):
                tr(pbb[:, c * 128:(c + 1) * 128], xb[b][:, c * 128:(c + 1) * 128], ['xb%d' % b], ['ps%d' % (t % 2)])
            cp(evac_eng(), xT[:, :, t * 128:(t + 1) * 128], pbb.rearrange("p (c t) -> p c t", c=8),
               ['ps%d' % (t % 2)], ['xT%d' % (t // 4)])
        S.add('pool', lambda e: e.memset(dvaug[:, :, :, 128:130], 1.0), reads=[], writes=['dvaug'])
        S.add('pool', lambda e: e.memset(svaug[:, :, :, 64:66], 1.0), reads=[], writes=['svaug'])
        S.add('pool', lambda e: e.memset(wvaug[:, :, :, 64:66], 1.0), reads=[], writes=['wvaug'])
        for ci in range(NCH):
            pieces, kind, scale = CH[ci]
            wb = rr['w'] % 6
            rr['w'] += 1
            dma('wbf%d' % wb, [(wbf[wb][:, :], wins[ci])], ['pcA'], ['wbf%d' % wb])
            wv = wbf[wb][:, :].rearrange("p (c n) -> p c n", c=8)
            if kind == 'F':
                for tg in range(4):
                    pi = rr['ps'] % 4
                    rr['ps'] += 1
                    for k in range(8):
                        mm(PS[pi][:, :], wv[:, k, :], xT[:, k, tg * 512:(tg + 1) * 512], k == 0, k == 7,
                           ['wbf%d' % wb, 'xT%d' % tg], ['ps%d' % pi])
                    dst = featT[ci][:, tg * 512:(tg + 1) * 512]
                    if evac_eng() == 'act':
                        act(dst, PS[pi][:, :], AF.Copy, ['ps%d' % pi], ['featT%d' % ci], scale=float(scale))
                    else:
                        ts('dve', dst, PS[pi][:, :], float(scale), None, ALU.mult, None, ['ps%d' % pi], ['featT%d' % ci])
            else:
                for t4 in range(4):
                    pi = rr['ps'] % 4
                    rr['ps'] += 1
                    for ti in range(4):
                        t = t4 * 4 + ti
                        for k in range(8):
                            mm(PS[pi][:, ti * 128:(ti + 1) * 128], xT[:, k, t * 128:(t + 1) * 128], wv[:, k, :], k == 0, k == 7,
                               ['wbf%d' % wb, 'xT%d' % t4], ['ps%d' % pi])
                    src = PS[pi][:, :].rearrange("p (t n) -> p t n", t=4)
                    tsl = slice(t4 * 4, t4 * 4 + 4)
                    if ci < 20:
                        h = ci - 16
                        cp(evac_eng(), dvaug[:, tsl, h, 0:128], src, ['ps%d' % pi], ['dvaug'])
                    elif ci == 20:
                        cp(evac_eng(), svaug[:, tsl, :, 0:64], src.rearrange("p t (g e) -> p t g e", g=2), ['ps%d' % pi], ['svaug'])
                    elif ci == 21:
                        cp(evac_eng(), wvaug[:, tsl, :, 0:64], src.rearrange("p t (g e) -> p t g e", g=2), ['ps%d' % pi], ['wvaug'])
                    else:
                        act(gates[:, tsl, :], src[:, :, 0:24], AF.Sigmoid, ['ps%d' % pi], ['gates'])
        S.barrier()
        if stop_after == 'P':
            break

        dma('w1b', [(A1[:, 0:4096].bitcast(BF16), w1s)], ['pcA'], ['w1b'])
        for kv in range(2):
            src = featT[F_CK + kv]
            pcol = P_PEK if kv == 0 else P_PEV
            for l in range(32):
                cb_i = l % 2
                ts('dve' if l % 2 else 'pool', cmpt[cb_i][:, 0:127], src[:, l:l + 16 * 126 + 1:16], pv[:, pcol + l:pcol + l + 1], None,
                   ALU.add, None, ['featT%d' % (F_CK + kv), 'pv'], ['cmpt%d' % cb_i])
                for g in range(2):
                    mm(PS[4 + g][:, 0:127], w1b[g * 64:(g + 1) * 64, kv, l, :], cmpt[cb_i][g * 64:(g + 1) * 64, 0:127],
                       l == 0, l == 31, ['w1b', 'cmpt%d' % cb_i], ['ps%d' % (4 + g)])
            for g in range(2):
                hp = hpre[:, g * 128:g * 128 + 127]
                h2 = hpre[:, 256 + g * 128:256 + g * 128 + 127]
                cp('dve', hp, PS[4 + g][:, 0:127], ['ps%d' % (4 + g)], ['hp%d' % g])
                tt('dve', h2, hp, hp, ALU.mult, ['hp%d' % g], ['h2%d' % g])
                ts('dve', h2, h2, 0.044715, 1.0, ALU.mult, ALU.add, ['h2%d' % g], ['h2b%d' % g])
                tt('dve', h2, h2, hp, ALU.mult, ['h2b%d' % g, 'hp%d' % g], ['h2c%d' % g])
                act(h2, h2, AF.Sigmoid, ['h2c%d' % g], ['h2d%d' % g], scale=1.5957691216057308)
                tt('dve', hidv[:, kv * 2 + g, 0:127], h2, hp, ALU.mult, ['h2d%d' % g, 'hp%d' % g], ['hid'])
        for g in range(2):
            mm(PS[6][:, 0:127], w2b[:, 0:128], hidv[:, g, 0:127], True, True, ['w2b', 'hid'], ['ps6'])
            cp('dve', kcT[g * 64:(g + 1) * 64, 0:127], PS[6][g * 64:(g + 1) * 64, 0:127], ['ps6'], ['kcT'])
            mm(PS[7][0:127, 0:64], hidv[:, 2 + g, 0:127], w2b[:, 128:192], True, True, ['w2b', 'hid'], ['ps7'])
            cp('dve', vcv[0:127, g, 0:64], PS[7][0:127, 0:64], ['ps7'], ['vcaug'])
            S.add('pool', lambda e, g=g: e.memset(vcv[0:127, g, 64:65], 1.0), reads=[], writes=['vcaug'])
            cp('pool', vcv[0:127, g, 65:97], cb[0:127, C_OVL:C_OVL + 32], ['cb'], ['vcaug'])
        S.barrier()
        if stop_after == 'C':
            break

        steps = []

        def attn_pair(qg, insts, span, far_mask, post):
            kts = range(max(0, 4 * qg - span), 4 * qg + 4)
            started = set()
            for kt in kts:
                qlo = max(4 * qg, kt)
                qhi = min(4 * qg + 3, kt + span)
                if qhi < qlo:
                    continue
                q0 = qlo * 128
                n = (qhi - qlo + 1) * 128
                info = {}

                def score(kt=kt, qlo=qlo, qhi=qhi, q0=q0, n=n, info=info):
                    sts = []
                    for I in insts:
                        pi = rr['ps'] % 4
                        rr['ps'] += 1
                        sts.append((PS[pi], 'ps%d' % pi))
                    for I, (st, sres) in zip(insts, sts):
                        mm(st[:, 0:n], I['kT'](kt), I['qT'](q0, n), True, False, [I['kres'], I['qres']], [sres])
                    for I, (st, sres) in zip(insts, sts):
                        if kt == qlo:
                            mm(st[:, 0:128], ident, cb[:, C_TLO:C_TLO + 128], False, False, ['cb'], [sres])
                        if far_mask and qhi == kt + span:
                            mm(st[:, n - 128:n], ident, cb[:, C_THI:C_THI + 128], False, False, ['cb'], [sres])
                    for I, (st, sres) in zip(insts, sts):
                        if I['selg'] is not None:
                            r0 = I['r0']
                            mm(st[:, 0:n], cb[r0:r0 + 32, C_E + kt * 128:C_E + (kt + 1) * 128],
                               selbTv[r0:r0 + 32, I['selg'], q0 - qg * 512:q0 - qg * 512 + n], False, False, ['cb', 'selbT'], [sres])
                    for I, (st, sres) in zip(insts, sts):
                        r0 = I['r0']
                        si = I['slope']
                        mm(st[:, 0:n], cb[r0:r0 + 9, C_ALA + si * 128:C_ALA + (si + 1) * 128], cb[r0:r0 + 9, C_ALB:C_ALB + n],
                           False, True, ['cb'], [sres])
                    pjs = []
                    m_off = qlo - kt
                    for I, (st, sres) in zip(insts, sts):
                        pj = rr['pt'] % 6
                        rr['pt'] += 1
                        si = I['slope']
                        act(pt[pj][:, 0:n], st[:, 0:n], AF.Exp, [sres, 'cf'], ['pt%d' % pj],
                            bias=cf[:, C_BT + si * 16 + m_off:C_BT + si * 16 + m_off + 1])
                        pjs.append(pj)
                    info['pjs'] = pjs

                def pv(kt=kt, qlo=qlo, qhi=qhi, info=info):
                    for I, pj in zip(insts, info['pjs']):
                        for qt in range(qlo, qhi + 1):
                            oap, obank = I['o_slot'](qt - 4 * qg)
                            first = obank not in started
                            started.add(obank)
                            mm(oap, pt[pj][:, (qt - qlo) * 128:(qt - qlo + 1) * 128], I['v'](kt), first, kt == qt,
                               ['pt%d' % pj, I['vres']], I['ores'])
                steps.append([score, pv, None])
            steps[-1][2] = post

        def flush_steps():
            prev = None
            for stp in steps:
                stp[0]()
                if prev is not None:
                    prev[1]()
                    if prev[2] is not None:
                        prev[2]()
                prev = stp
            if prev is not None:
                prev[1]()
                if prev[2] is not None:
                    prev[2]()
            del steps[:]

        for h in range(4):
            for qg in range(4):
                tsl = slice(qg * 4, qg * 4 + 4)
                insts = []
                for c in range(2):
                    ob = 4 + 2 * c

                    def o_slot(qi, ob=ob):
                        return PS[ob + qi // 2][:, (qi % 2) * 256:(qi % 2) * 256 + 129], ob + qi // 2
                    insts.append(dict(
                        kT=lambda kt, h=h, c=c: featT[F_DK + h][c * 64:(c + 1) * 64, kt * 128:(kt + 1) * 128],
                        qT=lambda q0, n, h=h, c=c: featT[F_DQ + h][c * 64:(c + 1) * 64, q0:q0 + n],
                        v=lambda kt, h=h: dvaug[:, kt, h, 0:129],
                        slope=h, r0=64 * c, selg=None, o_slot=o_slot,
                        kres='featT%d' % (F_DK + h), qres='featT%d' % (F_DQ + h), vres='dvaug',
                        ores=['o%d' % ob, 'o%d' % (ob + 1)]))

                def post(h=h, qg=qg, tsl=tsl):
                    for c in range(2):
                        ob = 4 + 2 * c
                        ores = ['o%d' % ob, 'o%d' % (ob + 1)]
                        for half in range(2):
                            ov = PS[ob + half][:, :].rearrange("p (a e) -> p a e", a=2)
                            rs = sm[:, 140 + c * 4 + half * 2:140 + c * 4 + half * 2 + 2]
                            S.add('dve', lambda e, rs=rs, ov=ov: e.reciprocal(out=rs.unsqueeze(2), in_=ov[:, :, 128:129]), reads=ores, writes=['rs%d%d' % (c, half)])
                            dst = (tmp0 if c == 0 else tmpa)[:, half * 256:(half + 1) * 256].rearrange("p (a e) -> p a e", a=2)
                            tt('dve', dst, ov[:, :, 0:128], rs.unsqueeze(2).to_broadcast([128, 2, 128]), ALU.mult,
                               ores + ['rs%d%d' % (c, half)], ['tmp%d' % c])
                    stt('dve', tmpb[:, :], tmpa[:, :], neglam, tmp0[:, :], ALU.mult, ALU.add, ['tmp0', 'tmp1', 'neglam'], ['tmpb'])
                    for qi in range(4):
                        act(tmpa[:, qi * 128:(qi + 1) * 128], tmpb[:, qi * 128:(qi + 1) * 128], AF.Square, ['tmpb'], ['tmp1'],
                            accum=sm[:, 150 + qi:151 + qi])
                    ts('dve', sm[:, 156:160], sm[:, 150:154], 1.0 / 128.0, 1e-5, ALU.mult, ALU.add, ['tmp1'], ['rms_a'])
                    act(sm[:, 156:160], sm[:, 156:160], AF.Sqrt, ['rms_a'], ['rms_b'])
                    S.add('dve', lambda e: e.reciprocal(out=sm[:, 160:164], in_=sm[:, 156:160]), reads=['rms_b'], writes=['rms_c'])
                    tb3 = tmpb[:, :].rearrange("p (a e) -> p a e", a=4)
                    tt('dve', tb3, tb3, sm[:, 160:164].unsqueeze(2).to_broadcast([128, 4, 128]), ALU.mult, ['tmpb', 'rms_c'], ['tmpb', 'tmpb2'])
                    tt('pool', cat[:, tsl, h * 128:(h + 1) * 128], tb3, gsub.unsqueeze(1).to_broadcast([128, 4, 128]), ALU.mult,
                       ['tmpb2', 'gsub'], ['cat'])
                attn_pair(qg, insts, MMAX[h], False, post)
        if stop_after == 'D':
            flush_steps()
            break

        for qg in range(4):
            tsl = slice(qg * 4, qg * 4 + 4)
            for h in range(8):
                info = {}

                def cscore(h=h, qg=qg, info=info):
                    base = (h // 4) * 64
                    j = h % 4
                    pi = rr['ps'] % 4
                    rr['ps'] += 1
                    st = PS[pi]
                    sres = 'ps%d' % pi
                    mm(st[0:127, :], kcT[base:base + 64, 0:127], featT[F_NQ + j][base:base + 64, qg * 512:(qg + 1) * 512], True, False,
                       ['kcT', 'featT%d' % (F_NQ + j)], [sres])
                    mm(st[0:127, :], cb[0:127, C_ID:C_ID + 127], cb[0:127, C_MCMP + qg * 512:C_MCMP + (qg + 1) * 512], False, True, ['cb'], [sres])
                    pj = rr['pt'] % 6
                    rr['pt'] += 1
                    act(pt[pj][0:127, :], st[0:127, :], AF.Exp, [sres], ['pt%d' % pj])
                    info['pj'] = pj

                def cpv(h=h, info=info):
                    g = h // 4
                    pj = info['pj']
                    ob = 4 + (h % 2)
                    ov = PS[ob][:, 0:400].rearrange("p (a e) -> p a e", a=4)
                    for qi in range(4):
                        mm(ov[:, qi, 0:97], pt[pj][0:127, qi * 128:(qi + 1) * 128], vcv[0:127, g, 0:97], True, True, ['pt%d' % pj, 'vcaug'], ['o%d' % ob])

                def cpost(h=h, qg=qg, tsl=tsl):
                    g = h // 4
                    j = h % 4
                    ob = 4 + (h % 2)
                    ores = 'o%d' % ob
                    ov = PS[ob][:, 0:400].rearrange("p (a e) -> p a e", a=4)
                    rs = sm[:, 170:174]
                    ts('dve', rs.unsqueeze(2), ov[:, :, 64:65], 1e-30, None, ALU.max, None, [ores], ['rsA'])
                    S.add('dve', lambda e, rs=rs: e.reciprocal(out=rs, in_=rs), reads=['rsA'], writes=['rsB'])
                    sg = sm[:, 174:178]
                    tt('dve', sg.unsqueeze(2), rs.unsqueeze(2), gates[:, tsl, h * 3:h * 3 + 1], ALU.mult, ['rsB', 'gates'], ['sgA'])
                    tt('dve', accv[:, :, h, :], ov[:, :, 0:64], sg.unsqueeze(2).to_broadcast([128, 4, 64]), ALU.mult, [ores, 'sgA'], ['acc%d' % h])
                    if not DBG_BR[0]:
                        S.add('dve', lambda e, h=h: e.memset(accv[:, :, h, :], 0.0), reads=[], writes=['acc%d' % h])
                    if j == 0:
                        tt('dve', impv[:, :, g, :], ov[:, :, 65:97], rs.unsqueeze(2).to_broadcast([128, 4, 32]), ALU.mult, [ores, 'rsB'], ['imp%d' % g])
                    else:
                        tt('dve', tmp0[:, 0:128].rearrange("p (a e) -> p a e", a=4), ov[:, :, 65:97], rs.unsqueeze(2).to_broadcast([128, 4, 32]),
                           ALU.mult, [ores, 'rsB'], ['tmp0'])
                        tt('dve', impv[:, :, g, :], impv[:, :, g, :], tmp0[:, 0:128].rearrange("p (a e) -> p a e", a=4), ALU.add,
                           ['tmp0', 'imp%d' % g], ['imp%d' % g])
                    if j != 3:
                        return
                    pi = rr['ps'] % 4
                    rr['ps'] += 1
                    pbb = PS[pi][:, :].bitcast(BF16)
                    for qi in range(4):
                        qt = qg * 4 + qi
                        val = selw[:, 0:32]
                        tt('dve', val, impv[:, qi, g, :], cf[:, C_FB + qt * 32:C_FB + (qt + 1) * 32], ALU.add, ['imp%d' % g, 'cf'], ['sw_a'])
                        S.add('dve', lambda e: e.max(out=selw[:, 32:40], in_=selw[:, 0:32]), reads=['sw_a'], writes=['sw_b'])
                        S.add('dve', lambda e: e.match_replace(out=selw[:, 64:96], in_to_replace=selw[:, 32:40], in_values=selw[:, 0:32], imm_value=-1e9),
                              reads=['sw_a', 'sw_b'], writes=['sw_c'])
                        S.add('dve', lambda e: e.max(out=selw[:, 40:48], in_=selw[:, 64:96]), reads=['sw_c'], writes=['sw_d'])
                        S.add('dve', lambda e: e.tensor_reduce(out=selw[:, 48:49], in_=selw[:, 40:48], axis=AX.X, op=ALU.min), reads=['sw_d'], writes=['sw_e'])
                        ts('dve', selbv[:, qi, g, :], val, selw[:, 48:49], NEG, ALU.is_lt, ALU.mult, ['sw_a', 'sw_e'], ['selb'])
                        tr(pbb[64 * g:64 * g + 32, qi * 128:(qi + 1) * 128], selbv[:, qi, g, :], ['selb'], ['ps%d' % pi])
                    cp('act', selbTv[64 * g:64 * g + 32, g, :], pbb[64 * g:64 * g + 32, 0:512], ['ps%d' % pi], ['selbT'])
                steps.append([cscore, cpv, cpost])
            for br in (1, 0):
                for j in range(4):
                    insts = []
                    for g in range(2):
                        h = g * 4 + j
                        base = g * 64
                        ob = 4 + 2 * (j % 2) + g
                        ovv = PS[ob][:, 0:264].rearrange("p (a e) -> p a e", a=4)

                        def o_slot(qi, ovv=ovv, ob=ob):
                            return ovv[:, qi, 0:65], ob
                        kf = F_SK if br == 0 else F_WK
                        va = svaug if br == 0 else wvaug
                        insts.append(dict(
                            kT=lambda kt, kf=kf, base=base: featT[kf][base:base + 64, kt * 128:(kt + 1) * 128],
                            qT=lambda q0, n, j=j, base=base: featT[F_NQ + j][base:base + 64, q0:q0 + n],
                            v=lambda kt, va=va, g=g: va[:, kt, g, 0:65],
                            slope=4 + h, r0=base, selg=(g if br == 0 else None), o_slot=o_slot,
                            kres='featT%d' % kf, qres='featT%d' % (F_NQ + j), vres='svaug' if br == 0 else 'wvaug',
                            ores=['o%d' % ob]))

                    def post(j=j, br=br, tsl=tsl):
                        for g in range(2):
                            h = g * 4 + j
                            ob = 4 + 2 * (j % 2) + g
                            ores = ['o%d' % ob]
                            ovv = PS[ob][:, 0:264].rearrange("p (a e) -> p a e", a=4)
                            rs = sm[:, 180 + g * 8:184 + g * 8]
                            sg = sm[:, 184 + g * 8:188 + g * 8]
                            rk = 'rs%d' % g
                            S.add('dve', lambda e, rs=rs, ovv=ovv: e.reciprocal(out=rs.unsqueeze(2), in_=ovv[:, :, 64:65]), reads=ores, writes=[rk + 'A'])
                            tt('dve', sg.unsqueeze(2), rs.unsqueeze(2), gates[:, tsl, h * 3 + 1 + br:h * 3 + 2 + br], ALU.mult, [rk + 'A', 'gates'], [rk + 'B'])
                            tv = (tmpa if g else tmpb)[:, 0:256].rearrange("p (a e) -> p a e", a=4)
                            tk = 'tv%d' % g
                            tt('dve', tv, ovv[:, :, 0:64], sg.unsqueeze(2).to_broadcast([128, 4, 64]), ALU.mult, ores + [rk + 'B'], [tk])
                            if not DBG_BR[1 + br]:
                                S.add('dve', lambda e, tv=tv: e.memset(tv, 0.0), reads=[], writes=[tk])
                            if br == 1:
                                tt('pool', accv[:, :, h, :], accv[:, :, h, :], tv, ALU.add, [tk, 'acc%d' % h], ['acc%d' % h])
                            else:
                                tt('pool', cat[:, tsl, 512 + h * 64:512 + (h + 1) * 64], accv[:, :, h, :], tv, ALU.add, [tk, 'acc%d' % h], ['cat'])
                    attn_pair(qg, insts, (4 if br == 1 else max(MMAX[4 + j], MMAX[8 + j])), br == 1, post)
        flush_steps()
        S.barrier()
        if stop_after == 'N':
            break

        dma('woutb', [(A2[:, 0:4096].bitcast(BF16), wos)], ['pcB'], ['woutb'])
        dma('lnp', [(lnp[:, a, :], lnc[a].partition_broadcast(128)) for a in range(4)], [], ['lnp'])

        def layer_norm_multi(items, gi):
            def L(i, a, b_):
                return lns[:, 16 * i + a:16 * i + b_]
            for (src, srck, dst, dstk, i) in items:
                S.add('dve', lambda e, src=src, i=i: e.bn_stats(out=L(i, 0, 6), in_=src[:, 0:512]), reads=[srck], writes=['bn_a%d' % i])
            for (src, srck, dst, dstk, i) in items:
                S.add('dve', lambda e, src=src, i=i: e.bn_stats(out=L(i, 6, 12), in_=src[:, 512:1024]), reads=[srck], writes=['bn_b%d' % i])
            for (src, srck, dst, dstk, i) in items:
                S.add('dve', lambda e, i=i: e.bn_aggr(out=L(i, 12, 14), in_=L(i, 0, 12)), reads=['bn_a%d' % i, 'bn_b%d' % i], writes=['bn_c%d' % i])
            for (src, srck, dst, dstk, i) in items:
                ts('dve', L(i, 14, 15), L(i, 13, 14), 1e-5, None, ALU.add, None, ['bn_c%d' % i], ['bn_d%d' % i])
            for (src, srck, dst, dstk, i) in items:
                act(L(i, 14, 15), L(i, 14, 15), AF.Sqrt, ['bn_d%d' % i], ['bn_e%d' % i])
            for (src, srck, dst, dstk, i) in items:
                S.add('dve', lambda e, i=i: e.reciprocal(out=L(i, 15, 16), in_=L(i, 14, 15)), reads=['bn_e%d' % i], writes=['bn_f%d' % i])
            for (src, srck, dst, dstk, i) in items:
                stt('dve', src, src, L(i, 12, 13), lnp[:, gi, :], ALU.subtract, ALU.mult, [srck, 'bn_c%d' % i, 'lnp'], [srck])
            for (src, srck, dst, dstk, i) in items:
                stt('dve', dst, src, L(i, 15, 16), lnp[:, gi + 1, :], ALU.mult, ALU.add, [srck, 'bn_f%d' % i, 'lnp'], [dstk])

        for tg in range(4):
            def stage1a(ti, tg=tg):
                t = tg * 4 + ti
                b = t % 2
                dma('xs%d' % b, [(xs[b][:, :], x[s, t * 128:(t + 1) * 128, :])], [], ['xs%d' % b])
                pbb = PS[3][:, :].bitcast(BF16)
                for c in range(8):
                    tr(pbb[:, c * 128:(c + 1) * 128], cat[:, t, c * 128:(c + 1) * 128], ['cat'], ['ps3'])
                cT = catT2[b]
                cp('act', cT[:, :, :], pbb.rearrange("p (c t) -> p c t", c=8), ['ps3'], ['catT%d' % b])
                for hf in range(2):
                    ob = 4 + 2 * b + hf
                    for k in range(8):
                        mm(PS[ob][:, :], cT[:, k, :], woutb[:, k, hf * 512:(hf + 1) * 512], k == 0, k == 7, ['catT%d' % b, 'woutb'], ['o%d' % ob])
                    stt('dve', xs[b][:, hf * 512:(hf + 1) * 512], xs[b][:, hf * 512:(hf + 1) * 512], float(ALPHA), PS[ob][:, :],
                        ALU.mult, ALU.add, ['xs%d' % b, 'o%d' % ob], ['xs%d' % b])

            def stage1b(tis, tg=tg):
                items = []
                for ti in tis:
                    b = (tg * 4 + ti) % 2
                    items.append((xs[b][:, :], 'xs%d' % b, x1[:, ti, :], 'x1_%d' % ti, b))
                layer_norm_multi(items, 0)
                for ti in tis:
                    b = (tg * 4 + ti) % 2
                    cp('act', x1b2[b][:, :], x1[:, ti, :], ['x1_%d' % ti], ['x1b%d' % b])

            def stage2(ti, tg=tg):
                b = (tg * 4 + ti) % 2
                pbb2 = PS[2][:, :].bitcast(BF16)
                for c in range(8):
                    tr(pbb2[:, c * 128:(c + 1) * 128], x1b2[b][:, c * 128:(c + 1) * 128], ['x1b%d' % b], ['ps2'])
                cp('act', x1T[:, :, ti * 128:(ti + 1) * 128], pbb2.rearrange("p (c t) -> p c t", c=8), ['ps2'], ['x1T'])
            stage1a(0)
            stage1a(1)
            stage1b((0, 1))
            stage1a(2)
            stage1a(3)
            stage2(0)
            stage2(1)
            stage1b((2, 3))
            stage2(2)
            stage2(3)
            for j in range(NJ):
                wb1 = rr['w'] % 6
                rr['w'] += 1
                dma('wbf%d' % wb1, [(wbf[wb1][:, :], wgs[j])], ['pcB'], ['wbf%d' % wb1])
                wb2 = rr['w'] % 6
                rr['w'] += 1
                dma('wbf%d' % wb2, [(wbf[wb2][:, :], wus[j])], ['pcB'], ['wbf%d' % wb2])
                wgv = wbf[wb1][:, :].rearrange("p (c n) -> p c n", c=8)
                wuv = wbf[wb2][:, :].rearrange("p (c n) -> p c n", c=8)
                pg = (j % 2) * 2
                for k in range(8):
                    mm(PS[pg][:, :], wgv[:, k, :], x1T[:, k, :], k == 0, k == 7, ['wbf%d' % wb1, 'x1T'], ['ps%d' % pg])
                for k in range(8):
                    mm(PS[pg + 1][:, :], wuv[:, k, :], x1T[:, k, :], k == 0, k == 7, ['wbf%d' % wb2, 'x1T'], ['ps%d' % (pg + 1)])
                sgb = tmp0 if j % 2 == 0 else tmpa
                sgk = 'tmp0' if j % 2 == 0 else 'tmp1'
                act(sgb[:, :], PS[pg][:, :], AF.Silu, ['ps%d' % pg], [sgk])
                tt('dve', aT[:, j, :], sgb[:, :], PS[pg + 1][:, :], ALU.mult, [sgk, 'ps%d' % (pg + 1)], ['aT'])
            for hf in range(2):
                for j in range(NJ):
                    wb = rr['w'] % 6
                    rr['w'] += 1
                    dma('wbf%d' % wb, [(wbf[wb][:, 0:512], wds[j, :, hf * 512:(hf + 1) * 512])], ['pcB'], ['wbf%d' % wb])
                    for ti in range(4):
                        mm(PS[4 + ti][:, :], aT[:, j, ti * 128:(ti + 1) * 128], wbf[wb][:, 0:512], j == 0, j == NJ - 1,
                           ['aT', 'wbf%d' % wb], ['o%d' % (4 + ti)])
                for ti in range(4):
                    stt('dve', x1[:, ti, hf * 512:(hf + 1) * 512], x1[:, ti, hf * 512:(hf + 1) * 512], float(ALPHA), PS[4 + ti][:, :],
                        ALU.mult, ALU.add, ['x1_%d' % ti, 'o%d' % (4 + ti)], ['x1_%d' % ti])
            for pr in ((0, 1), (2, 3)):
                layer_norm_multi([(x1[:, ti, :], 'x1_%d' % ti, ost[ti % 2][:, :], 'ost%d' % (ti % 2), ti % 2) for ti in pr], 2)
                for ti in pr:
                    t = tg * 4 + ti
                    dma('out%d' % (ti % 2), [(out[s, t * 128:(t + 1) * 128, :], ost[ti % 2][:, :])], ['ost%d' % (ti % 2)], ['outd'])
        S.barrier()

    if dbg:
        S.barrier()
        lst = []
        for c0 in range(0, 8192, 2048):
            lst.append((dbg_o[:, c0:c0 + 2048], A1[:, c0:c0 + 2048]))
        for c0 in range(0, 23040, 2048):
            c1 = min(23040, c0 + 2048)
            lst.append((dbg_o[:, 8192 + c0:8192 + c1], A2[:, c0:c1]))
        lst += [(dbg_o[:, 31232:31232 + 2048], acc[:, :]), (dbg_o[:, 33280:33280 + 256], sm[:, :]), (dbg_o[:, 33536:33536 + 256], impacc[:, :])]
        lst += [(dbg_o[:, 33792:33856], kcT[:, :].bitcast(F32)), (dbg_o[:, 33856:33956], vcaug[:, :].bitcast(F32)),
                (dbg_o[:, 33956:34212], hid[:, :].bitcast(F32)), (dbg_o[:, 34212:34340], selb[:, :].bitcast(F32)),
                (dbg_o[:, 34340:34340 + 512], selbT[:, :].bitcast(F32))]
        dma('dbg', lst, [], ['dbgd'])
    S.barrier()
    stuck = S.check_deadlock()
    assert not stuck, stuck
    S.emit()
    es.close()
    return nc


def host_prep(inputs):
    f = lambda a: np.ascontiguousarray(a, dtype=np.float32)
    w_in = f(inputs["w_in"][0])
    winc = np.zeros((NCH, 128, 8, 128), np.float32)
    for ci, (pieces, kind, scale) in enumerate(CH):
        o = 0
        for (c0, ncol) in pieces:
            blk = w_in[:, c0:c0 + ncol].reshape(8, 128, ncol).transpose(1, 0, 2)
            winc[ci, :, :, o:o + ncol] = blk
            o += ncol
    winc = winc.reshape(NCH, 128, 1024)

    def chunk_cols(w):
        return np.ascontiguousarray(w.reshape(8, 128, NJ, 128).transpose(2, 1, 0, 3)).reshape(NJ, 128, 1024)
    wgc = chunk_cols(f(inputs["w_gate"][0]))
    wuc = chunk_cols(f(inputs["w_up"][0]))
    wdc = np.ascontiguousarray(f(inputs["w_down"][0]).reshape(NJ, 128, 1024))
    woc = np.ascontiguousarray(f(inputs["w_out"][0]).reshape(8, 128, 1024).transpose(1, 0, 2)).reshape(128, 8192)
    w1 = np.stack([f(inputs["cmp_w1_k"][0]), f(inputs["cmp_w1_v"][0])], 0)
    w1 = w1.reshape(2, 32, 64, 128).transpose(2, 0, 1, 3).reshape(64, 8192)
    w1c = np.ascontiguousarray(np.concatenate([w1, w1], 0))
    w2k = f(inputs["cmp_w2_k"][0])
    w2v = f(inputs["cmp_w2_v"][0])
    w2c = np.ascontiguousarray(np.concatenate([w2k, w2k, w2v], 1))
    lnc = np.ascontiguousarray(np.stack([f(inputs["ln1_g"][0]), f(inputs["ln1_b"][0]), f(inputs["ln2_g"][0]), f(inputs["ln2_b"][0])], 0))
    pvc = np.zeros((128, NPV), np.float32)
    pvc[:, 0:64] = f(inputs["diff_lq1"][0])[None, :]
    pvc[:, 64:128] = f(inputs["diff_lk1"][0])[None, :]
    pvc[:, 128:192] = f(inputs["diff_lq2"][0])[None, :]
    pvc[:, 192:256] = f(inputs["diff_lk2"][0])[None, :]
    pvc[:, P_SUBG:P_SUBG + 128] = f(inputs["diff_subln_g"][0])[None, :]
    pek = f(inputs["cmp_pe_k"][0]).T
    pev = f(inputs["cmp_pe_v"][0]).T
    pvc[:, P_PEK:P_PEK + 32] = np.concatenate([pek, pek], 0)
    pvc[:, P_PEV:P_PEV + 32] = np.concatenate([pev, pev], 0)
    cb16, cf32 = make_consts()
    return dict(winc=winc, wgc=wgc, wuc=wuc, wdc=wdc, woc=woc, w1c=w1c, w2c=w2c, lnc=lnc, pvc=pvc, cb16=cb16, cf32=cf32)


def kernel(**inputs):
    x = np.ascontiguousarray(inputs["x"], dtype=np.float32)
    shared = host_prep(inputs)
    ncores = 8
    nseq = x.shape[0] // ncores
    nc = build(nseq)
    in_maps = []
    for c in range(ncores):
        m = dict(shared)
        m["x"] = np.ascontiguousarray(x[c * nseq:(c + 1) * nseq])
        in_maps.append(m)
    res = run_bass_kernel_spmd(nc, in_maps, core_ids=list(range(ncores)))
    return np.concatenate([r["out"] for r in res.results], axis=0).astype(np.float32)
```
